# Optimizing a Trainium2 kernel written in Bass

```python
import math
import jax, jax.numpy as jnp
from jax import lax
import numpy as np

D_MODEL = 2048
BATCH = 16
SEQ = 2048
DEPTH = 2

N_MIXERS = 2
N_MAMBA = (DEPTH + 1) // 2
N_RET = DEPTH // 2
CHUNK = 128

SSM_EXPAND = 2
SSM_D_INNER = SSM_EXPAND * D_MODEL
SSM_HEADDIM = 64
SSM_HEADS = SSM_D_INNER // SSM_HEADDIM
SSM_STATE = 128
SSM_GROUPS = 8
SSM_HPG = SSM_HEADS // SSM_GROUPS
SSM_CONV = 4
SSM_CONV_DIM = SSM_D_INNER + 2 * SSM_GROUPS * SSM_STATE
SSM_IN_DIM = SSM_D_INNER + SSM_CONV_DIM + SSM_HEADS

RET_HEADS = D_MODEL // 256
RET_DK = 256
RET_DV = 512
RET_QK_DIM = RET_HEADS * RET_DK
RET_V_DIM = RET_HEADS * RET_DV
RET_IN_DIM = 2 * RET_QK_DIM + 2 * RET_V_DIM
ROPE_BASE = 10000.0

PEER_HEADS = 8
PEER_NKEYS = 128
PEER_N_EXPERTS = PEER_NKEYS * PEER_NKEYS
PEER_TOPK = 16
PEER_DQ = 256
PEER_DHALF = PEER_DQ // 2
PEER_BLOCK = 128

DN_ALPHA = (2 * DEPTH) ** 0.25
DN_BETA = (8 * DEPTH) ** -0.25
LN_EPS = 1e-5

kernel_name = "hybrid_ssd_retention_peer_deepnorm"


def layer_norm(x, w, b):
    xf = x.astype(jnp.float32)
    mu = jnp.mean(xf, axis=-1, keepdims=True)
    var = jnp.mean(jnp.square(xf - mu), axis=-1, keepdims=True)
    y = (xf - mu) * lax.rsqrt(var + LN_EPS)
    return (y * w + b).astype(x.dtype)


def to_chunks(a):
    b, s = a.shape[:2]
    return a.reshape(b, s // CHUNK, CHUNK, *a.shape[2:]).swapaxes(0, 1)


def from_chunks(a):
    nc, b, l = a.shape[:3]
    return a.swapaxes(0, 1).reshape(b, nc * l, *a.shape[3:])


def causal_dwconv(x, w, b):
    y = lax.conv_general_dilated(
        x, w[:, None, :], window_strides=(1,), padding=[(SSM_CONV - 1, 0)],
        dimension_numbers=("NWC", "WIO", "NWC"), feature_group_count=x.shape[-1])
    return y + b


def ssd_chunked(x_dt, dA, Bm, Cm):
    bsz = x_dt.shape[0]
    mask = jnp.tril(jnp.ones((CHUNK, CHUNK), bool))

    def step(state, inp):
        xc, ac, bc, cc = inp
        cum = jnp.cumsum(ac, axis=1)
        seg = cum[:, :, None] - cum[:, None, :]
        decay = jnp.exp(jnp.where(mask[None, :, :, None, None], seg, -jnp.inf))
        cb = jnp.einsum("blgn,bsgn->blsg", cc, bc)
        y_diag = jnp.einsum("blsg,blsgr,bsgrp->blgrp", cb, decay, xc)
        y_off = jnp.einsum("blgn,bgrpn,blgr->blgrp", cc, state, jnp.exp(cum))
        last = cum[:, -1]
        w_in = jnp.exp(last[:, None] - cum)
        new_state = (state * jnp.exp(last)[..., None, None]
                     + jnp.einsum("bsgn,bsgr,bsgrp->bgrpn", bc, w_in, xc))
        return new_state, y_diag + y_off

    init = jnp.zeros((bsz, SSM_GROUPS, SSM_HPG, SSM_HEADDIM, SSM_STATE), jnp.float32)
    _, y = lax.scan(step, init, (to_chunks(x_dt), to_chunks(dA), to_chunks(Bm), to_chunks(Cm)))
    return from_chunks(y)


def mamba2_mixer(x, in_proj, conv_w, conv_b, dt_bias, A_log, D_skip, norm_w, out_proj):
    bsz, s, _ = x.shape
    zxbcdt = x @ in_proj
    z, xbc, dt = jnp.split(zxbcdt, [SSM_D_INNER, SSM_D_INNER + SSM_CONV_DIM], axis=-1)
    xbc = jax.nn.silu(causal_dwconv(xbc, conv_w, conv_b))
    xs, Bm, Cm = jnp.split(xbc, [SSM_D_INNER, SSM_D_INNER + SSM_GROUPS * SSM_STATE], axis=-1)
    xs = xs.reshape(bsz, s, SSM_GROUPS, SSM_HPG, SSM_HEADDIM).astype(jnp.float32)
    Bm = Bm.reshape(bsz, s, SSM_GROUPS, SSM_STATE).astype(jnp.float32)
    Cm = Cm.reshape(bsz, s, SSM_GROUPS, SSM_STATE).astype(jnp.float32)
    dt = jax.nn.softplus(dt.astype(jnp.float32) + dt_bias.astype(jnp.float32))
    dt = dt.reshape(bsz, s, SSM_GROUPS, SSM_HPG)
    A = -jnp.exp(A_log.astype(jnp.float32)).reshape(SSM_GROUPS, SSM_HPG)
    y = ssd_chunked(xs * dt[..., None], dt * A, Bm, Cm)
    y = y + D_skip.astype(jnp.float32).reshape(SSM_GROUPS, SSM_HPG, 1) * xs
    y = y.reshape(bsz, s, SSM_D_INNER) * jax.nn.silu(z.astype(jnp.float32))
    yg = y.reshape(bsz, s, SSM_GROUPS, SSM_D_INNER // SSM_GROUPS)
    yg = yg * lax.rsqrt(jnp.mean(jnp.square(yg), axis=-1, keepdims=True) + LN_EPS)
    y = yg.reshape(bsz, s, SSM_D_INNER) * norm_w
    return y.astype(x.dtype) @ out_proj


def rotary(t, cos, sin):
    half = t.shape[-1] // 2
    t1, t2 = t[..., :half], t[..., half:]
    return jnp.concatenate([t1 * cos - t2 * sin, t2 * cos + t1 * sin], axis=-1)


def retention_mixer(x, positions, in_proj, gn_w, gn_b, out_proj):
    bsz, s, _ = x.shape
    q, k, v, g = jnp.split(x @ in_proj, [RET_QK_DIM, 2 * RET_QK_DIM, 2 * RET_QK_DIM + RET_V_DIM], axis=-1)
    inv_freq = ROPE_BASE ** (-jnp.arange(0, RET_DK, 2, dtype=jnp.float32) / RET_DK)
    ang = positions.astype(jnp.float32)[..., None] * inv_freq
    cos, sin = jnp.cos(ang)[:, :, None], jnp.sin(ang)[:, :, None]
    q = rotary(q.reshape(bsz, s, RET_HEADS, RET_DK).astype(jnp.float32), cos, sin) * (RET_DK ** -0.5)
    k = rotary(k.reshape(bsz, s, RET_HEADS, RET_DK).astype(jnp.float32), cos, sin)
    v = v.reshape(bsz, s, RET_HEADS, RET_DV).astype(jnp.float32)

    log_gamma = jnp.log1p(-jnp.exp2(-5.0 - jnp.arange(RET_HEADS, dtype=jnp.float32)))
    idx = jnp.arange(CHUNK, dtype=jnp.float32)
    mask = jnp.tril(jnp.ones((CHUNK, CHUNK), bool))
    rel = idx[:, None] - idx[None, :]
    dmat = jnp.exp(jnp.where(mask[:, :, None], rel[:, :, None] * log_gamma, -jnp.inf))
    q_decay = jnp.exp((idx[:, None] + 1.0) * log_gamma)
    k_decay = jnp.exp((CHUNK - 1.0 - idx)[:, None] * log_gamma)
    chunk_decay = jnp.exp(CHUNK * log_gamma)

    def step(R, inp):
        qc, kc, vc = inp
        scores = jnp.einsum("blhd,bshd->blsh", qc, kc) * dmat
        intra = jnp.einsum("blsh,bshv->blhv", scores, vc)
        inter = jnp.einsum("blhd,bhdv->blhv", qc, R) * q_decay[:, :, None]
        R_new = R * chunk_decay[:, None, None] + jnp.einsum("bshd,sh,bshv->bhdv", kc, k_decay, vc)
        return R_new, intra + inter

    R0 = jnp.zeros((bsz, RET_HEADS, RET_DK, RET_DV), jnp.float32)
    _, y = lax.scan(step, R0, (to_chunks(q), to_chunks(k), to_chunks(v)))
    y = from_chunks(y)
    mu = jnp.mean(y, axis=-1, keepdims=True)
    var = jnp.mean(jnp.square(y - mu), axis=-1, keepdims=True)
    y = ((y - mu) * lax.rsqrt(var + LN_EPS)).reshape(bsz, s, RET_V_DIM) * gn_w + gn_b
    y = jax.nn.silu(g.astype(jnp.float32)) * y
    return y.astype(x.dtype) @ out_proj


def peer_ffn(x, w_q, sub_keys, u, v):
    bsz, s, d = x.shape
    xt = x.reshape(-1, PEER_BLOCK, d)

    def block(xb):
        q = (xb @ w_q).reshape(PEER_BLOCK, PEER_HEADS, 2, PEER_DHALF)
        sc = jnp.einsum("thcd,hckd->thck", q, sub_keys)
        s_top, i_top = lax.top_k(sc, PEER_TOPK)
        cand = (s_top[:, :, 0, :, None] + s_top[:, :, 1, None, :]).reshape(PEER_BLOCK, PEER_HEADS, -1)
        cand_idx = (i_top[:, :, 0, :, None] * PEER_NKEYS + i_top[:, :, 1, None, :]).reshape(PEER_BLOCK, PEER_HEADS, -1)
        best, pos = lax.top_k(cand, PEER_TOPK)
        eidx = jnp.take_along_axis(cand_idx, pos, axis=-1)
        gate = jax.nn.softmax(best.astype(jnp.float32), axis=-1)
        ue = jnp.take(u, eidx, axis=0)
        ve = jnp.take(v, eidx, axis=0)
        act = jax.nn.gelu(jnp.einsum("td,thkd->thk", xb, ue).astype(jnp.float32), approximate=False)
        return jnp.einsum("thk,thkd->td", (gate * act).astype(x.dtype), ve)

    return lax.map(block, xt).reshape(bsz, s, d)


def _normal(k, shape, scale):
    return jax.random.normal(k, shape, jnp.float32) * scale


def setup_inputs(seed: int = 0) -> dict:
    key = jax.random.key(seed)
    ks = jax.random.split(key, 24)
    x = _normal(ks[0], (BATCH, SEQ, D_MODEL), 1.0)
    positions = (jax.random.randint(ks[1], (BATCH, 1), 0, 1024, dtype=jnp.int32)
                 + jnp.arange(SEQ, dtype=jnp.int32)[None, :])
    ssm_in_proj = _normal(ks[2], (N_MAMBA, D_MODEL, SSM_IN_DIM), D_MODEL ** -0.5)
    ssm_conv_w = _normal(ks[3], (N_MAMBA, SSM_CONV, SSM_CONV_DIM), SSM_CONV ** -0.5)
    ssm_conv_b = _normal(ks[4], (N_MAMBA, SSM_CONV_DIM), 0.01)
    dt0 = jnp.exp(jax.random.uniform(ks[5], (N_MAMBA, SSM_HEADS), jnp.float32)
                  * (math.log(0.1) - math.log(1e-3)) + math.log(1e-3))
    ssm_dt_bias = dt0 + jnp.log(-jnp.expm1(-dt0))
    ssm_A_log = jnp.log(jax.random.uniform(ks[6], (N_MAMBA, SSM_HEADS), jnp.float32, 1.0, 16.0))
    ssm_D = 1.0 + _normal(ks[7], (N_MAMBA, SSM_HEADS), 0.01)
    ssm_norm_w = 1.0 + _normal(ks[8], (N_MAMBA, SSM_D_INNER), 0.01)
    ssm_out_proj = _normal(ks[9], (N_MAMBA, SSM_D_INNER, D_MODEL), DN_BETA * SSM_D_INNER ** -0.5)
    ret_in_proj = _normal(ks[10], (N_RET, D_MODEL, RET_IN_DIM), D_MODEL ** -0.5)
    ret_gn_w = 1.0 + _normal(ks[11], (N_RET, RET_V_DIM), 0.01)
    ret_gn_b = _normal(ks[12], (N_RET, RET_V_DIM), 0.01)
    ret_out_proj = _normal(ks[13], (N_RET, RET_V_DIM, D_MODEL), DN_BETA * RET_V_DIM ** -0.5)
    mix_ln_w = 1.0 + _normal(ks[14], (DEPTH, D_MODEL), 0.01)
    mix_ln_b = _normal(ks[15], (DEPTH, D_MODEL), 0.01)
    peer_w_q = _normal(ks[16], (DEPTH, D_MODEL, PEER_HEADS * PEER_DQ), D_MODEL ** -0.5)
    peer_sub_keys = _normal(ks[17], (DEPTH, PEER_HEADS, 2, PEER_NKEYS, PEER_DHALF), PEER_DHALF ** -0.5)
    peer_u = _normal(ks[18], (DEPTH, PEER_N_EXPERTS, D_MODEL), D_MODEL ** -0.5)
    peer_v = _normal(ks[19], (DEPTH, PEER_N_EXPERTS, D_MODEL), DN_BETA * PEER_HEADS ** -0.5)
    ffn_ln_w = 1.0 + _normal(ks[20], (DEPTH, D_MODEL), 0.01)
    ffn_ln_b = _normal(ks[21], (DEPTH, D_MODEL), 0.01)
    return {"x": x, "positions": positions,
            "ssm_in_proj": ssm_in_proj, "ssm_conv_w": ssm_conv_w, "ssm_conv_b": ssm_conv_b,
            "ssm_dt_bias": ssm_dt_bias, "ssm_A_log": ssm_A_log, "ssm_D": ssm_D,
            "ssm_norm_w": ssm_norm_w, "ssm_out_proj": ssm_out_proj,
            "ret_in_proj": ret_in_proj, "ret_gn_w": ret_gn_w, "ret_gn_b": ret_gn_b,
            "ret_out_proj": ret_out_proj,
            "mix_ln_w": mix_ln_w, "mix_ln_b": mix_ln_b,
            "peer_w_q": peer_w_q, "peer_sub_keys": peer_sub_keys, "peer_u": peer_u, "peer_v": peer_v,
            "ffn_ln_w": ffn_ln_w, "ffn_ln_b": ffn_ln_b}


def reference(x, positions, ssm_in_proj, ssm_conv_w, ssm_conv_b, ssm_dt_bias, ssm_A_log, ssm_D,
              ssm_norm_w, ssm_out_proj, ret_in_proj, ret_gn_w, ret_gn_b, ret_out_proj,
              mix_ln_w, mix_ln_b, peer_w_q, peer_sub_keys, peer_u, peer_v, ffn_ln_w, ffn_ln_b):
    for i in range(DEPTH):
        j = i // N_MIXERS
        if i % N_MIXERS == 0:
            h = mamba2_mixer(x, ssm_in_proj[j], ssm_conv_w[j], ssm_conv_b[j], ssm_dt_bias[j],
                             ssm_A_log[j], ssm_D[j], ssm_norm_w[j], ssm_out_proj[j])
        else:
            h = retention_mixer(x, positions, ret_in_proj[j], ret_gn_w[j], ret_gn_b[j], ret_out_proj[j])
        x = layer_norm(DN_ALPHA * x + h, mix_ln_w[i], mix_ln_b[i])
        f = peer_ffn(x, peer_w_q[i], peer_sub_keys[i], peer_u[i], peer_v[i])
        x = layer_norm(DN_ALPHA * x + f, ffn_ln_w[i], ffn_ln_b[i])
    return x
```

```python
import contextlib
import numpy as np
import concourse.bass as bass
import concourse.mybir as mybir
from concourse.bass_utils import run_bass_kernel_spmd

F32 = mybir.dt.float32
BF16 = mybir.dt.bfloat16
I32 = mybir.dt.int32
U32 = mybir.dt.uint32
AF = mybir.ActivationFunctionType
ALU = mybir.AluOpType
AX = mybir.AxisListType

D = 2048
DN_ALPHA = 4 ** 0.25
LN_EPS = 1e-5
NEG = -30000.0


class Buf:
    __slots__ = ("t", "lw", "rd", "dsem", "dval", "name")

    def __init__(self, t, name=""):
        self.t = t
        self.lw = None
        self.rd = []
        self.dsem = None
        self.dval = 0
        self.name = name

    def __getitem__(self, k):
        return self.t[k]


class Sched:
    ENGS = ("pe", "act", "dve", "pool", "sp")

    def __init__(self, nc):
        self.nc = nc
        self.es = contextlib.ExitStack()
        self.eng = {"pe": nc.tensor, "act": nc.scalar, "dve": nc.vector, "pool": nc.gpsimd, "sp": nc.sync}
        self.sem = {}
        self.cnt = {}
        for e in ("pe", "act", "dve", "pool"):
            self.sem[e] = self.es.enter_context(nc.semaphore("s_" + e))
            self.cnt[e] = 0
        self.waited = {e: {} for e in self.ENGS}
        self.dma_bufs = []
        self.free_dsems = []
        self.nbuf = 0
        self.banks = None
        self.bank_i = 0
        self.bank_pool = (0, 8)

    def sbuf(self, stack, shape, dt, name=None):
        self.nbuf += 1
        name = name or "b"
        t = stack.enter_context(self.nc.sbuf_tensor(f"{name}_{self.nbuf}", list(shape), dt))
        return Buf(t, name)

    def hbm(self, name, shape, dt):
        self.nbuf += 1
        return self.nc.dram_tensor(f"{name}_{self.nbuf}", list(shape), dt).ap()

    def init_banks(self, stack):
        self.banks = []
        for i in range(8):
            t = stack.enter_context(self.nc.psum_tensor(f"bank{i}", [128, 512], F32))
            self.banks.append(Buf(t, f"bank{i}"))

    def bank(self):
        lo, n = self.bank_pool
        b = self.banks[lo + self.bank_i % n]
        self.bank_i += 1
        return b

    def bank_private(self, lo, n):
        self.bank_j = getattr(self, "bank_j", 0) + 1
        return self.banks[lo + self.bank_j % n]

    def _dsem(self, b):
        if b.dsem is None:
            if self.free_dsems:
                b.dsem, b.dval = self.free_dsems.pop()
            else:
                self.nsem = getattr(self, "nsem", 0) + 1
                b.dsem = self.es.enter_context(self.nc.semaphore(f"d{self.nsem}"))
            self.dma_bufs.append(b)
        return b.dsem

    def release_dma_sems(self):
        for b in self.dma_bufs:
            self.free_dsems.append((b.dsem, b.dval))
            b.dsem = None
            b.dval = 0
            b.lw = None if (b.lw is not None and b.lw[2] == "dma") else b.lw
            b.rd = [r for r in b.rd if r[2] != "dma"]
        self.dma_bufs = []

    def _wait(self, e, tok):
        if tok is None:
            return
        sem, val = tok[0], tok[1]
        w = self.waited[e]
        k = id(sem)
        if w.get(k, 0) >= val:
            return
        self.eng[e].wait_ge(sem, val)
        w[k] = val

    def _deps(self, e, reads, writes):
        for b in reads:
            self._wait(e, b.lw)
        for b in writes:
            if b.lw is not None and b.lw[2] != e:
                self._wait(e, b.lw)
            for r in b.rd:
                if r[2] != e:
                    self._wait(e, r)

    def _commit(self, tok, reads, writes):
        for b in reads:
            b.rd.append(tok)
            if len(b.rd) > 16:
                best = {}
                for r in b.rd:
                    k = id(r[0])
                    if k not in best or best[k][1] < r[1]:
                        best[k] = r
                b.rd = list(best.values())
        for b in writes:
            b.lw = tok
            b.rd = []

    def op(self, e, fn, reads=(), writes=(), sig=True):
        self._deps(e, reads, writes)
        ins = fn(self.eng[e])
        if sig:
            self.cnt[e] += 1
            ins.then_inc(self.sem[e], 1)
            tok = (self.sem[e], self.cnt[e], e)
        else:
            tok = (self.sem[e], self.cnt[e] + 1, e)
        self._commit(tok, reads, writes)
        return tok

    def dma(self, q, out, in_, sb, reads=(), writes=(), **kw):
        sem = self._dsem(sb)
        self._deps(q, reads, writes)
        if sb.dval:
            self._wait(q, (sem, sb.dval, "dma"))
        ins = self.eng[q].dma_start(out=out, in_=in_, **kw)
        sb.dval += 16
        ins.then_inc(sem, 16)
        tok = (sem, sb.dval, "dma")
        self._commit(tok, reads, writes)
        return tok

    def barrier(self):
        toks = []
        for e in ("pe", "act", "dve", "pool"):
            if self.cnt[e]:
                toks.append((self.sem[e], self.cnt[e], e))
        for b in self.dma_bufs:
            if b.dval:
                toks.append((b.dsem, b.dval, "dma"))
        for e in self.ENGS:
            for t in toks:
                if t[2] != e:
                    self._wait(e, t)
        self.release_dma_sems()

    def finish(self):
        self.barrier()
        self.es.close()


def cp(S, e, out, in_, reads, writes):
    if e == "act":
        return S.op("act", lambda en: en.copy(out=out, in_=in_), reads, writes)
    return S.op(e, lambda en: en.tensor_copy(out=out, in_=in_), reads, writes)


def tt(S, e, out, a, b, op, reads, writes):
    return S.op(e, lambda en: en.tensor_tensor(out=out, in0=a, in1=b, op=op), reads, writes)


def ts(S, e, out, a, s1, s2, op0, op1, reads, writes):
    if op1 is None:
        return S.op(e, lambda en: en.tensor_scalar(out=out, in0=a, scalar1=s1, scalar2=None, op0=op0), reads, writes)
    return S.op(e, lambda en: en.tensor_scalar(out=out, in0=a, scalar1=s1, scalar2=s2, op0=op0, op1=op1), reads, writes)


def act(S, out, in_, func, reads, writes, **kw):
    return S.op("act", lambda en: en.activation(out=out, in_=in_, func=func, **kw), reads, writes)


def bcast_rows(ap1d, n, parts=128):
    return ap1d.rearrange("(o n) -> o n", o=1).to_broadcast([parts, n])


class Ctx:
    pass


def rsqrt_eps(S, out, in_, in_bufs, out_buf, mul, eps):
    ts(S, "dve", out, in_, float(mul), float(eps), ALU.mult, ALU.add, in_bufs, [out_buf])
    act(S, out, out, AF.Sqrt, [out_buf], [out_buf])
    S.op("dve", lambda en: en.reciprocal(out=out, in_=out), [out_buf], [out_buf])


def transposes_to(S, C, src, src_bufs, ncol_chunks, dst_fn, dst_bufs, dt=BF16, evac="dve"):
    per = 8 if dt == BF16 else 4
    ident = C.identb if dt == BF16 else C.identf
    j = 0
    while j < ncol_chunks:
        n = min(per, ncol_chunks - j)
        bk = S.bank()
        bv = bk.t[:].bitcast(BF16) if dt == BF16 else bk.t[:]
        for i in range(n):
            S.op("pe", lambda en: en.transpose(bv[:, i * 128:(i + 1) * 128], src[:, (j + i) * 128:(j + i + 1) * 128], ident[:]),
                 reads=list(src_bufs) + [C.identb if dt == BF16 else C.identf], writes=[bk], sig=(i == n - 1))
        cp(S, evac, dst_fn(j, n), bv[:, 0:n * 128].rearrange("p (a b) -> p a b", a=n), [bk], dst_bufs)
        j += n


def load_wslab(S, slab, W, KC, f0, fw):
    wv = W.rearrange("(kc p) f -> p kc f", p=128)
    S.dma("pool", slab[:, 0:KC, 0:fw], wv[:, :, f0:f0 + fw], slab, writes=[slab])


def gemm_tok(S, C, xT, KC, TG, W, f0, nf, slabs, epi):
    blocks = []
    f = f0
    while f < f0 + nf:
        fw = min(512, f0 + nf - f)
        blocks.append((f, fw))
        f += fw
    load_wslab(S, slabs[0], W, KC, blocks[0][0], blocks[0][1])
    for bi, (f, fw) in enumerate(blocks):
        if bi + 1 < len(blocks):
            load_wslab(S, slabs[(bi + 1) % 2], W, KC, blocks[bi + 1][0], blocks[bi + 1][1])
        sl = slabs[bi % 2]
        for t_ in range(TG // 128):
            bk = S.bank()
            for kc in range(KC):
                S.op("pe", lambda en: en.matmul(bk[:, 0:fw], lhsT=xT[:, kc, t_ * 128:(t_ + 1) * 128], rhs=sl[:, kc, 0:fw],
                                                start=(kc == 0), stop=(kc == KC - 1)),
                     reads=[xT, sl], writes=[bk], sig=(kc == KC - 1))
            epi(bi, t_, bk, fw)


def gemm_feat(S, C, xT, KC, TG, W, f0, nf, slabs, epi, tb_w=512, fbw=512):
    blocks = []
    f = f0
    while f < f0 + nf:
        fw = min(fbw, f0 + nf - f)
        blocks.append((f, fw))
        f += fw
    load_wslab(S, slabs[0], W, KC, blocks[0][0], blocks[0][1])
    ci = 0
    for bi, (f, fw) in enumerate(blocks):
        if bi + 1 < len(blocks):
            load_wslab(S, slabs[(bi + 1) % 2], W, KC, blocks[bi + 1][0], blocks[bi + 1][1])
        sl = slabs[bi % 2]
        c0 = 0
        while c0 < fw:
            rows = min(128, fw - c0)
            for tb in range((TG + tb_w - 1) // tb_w):
                tw = min(tb_w, TG - tb * tb_w)
                bk = S.bank()
                for kc in range(KC):
                    S.op("pe", lambda en: en.matmul(bk[0:rows, 0:tw], lhsT=sl[:, kc, c0:c0 + rows],
                                                    rhs=xT[:, kc, tb * tb_w:tb * tb_w + tw],
                                                    start=(kc == 0), stop=(kc == KC - 1)),
                         reads=[xT, sl], writes=[bk], sig=(kc == KC - 1))
                epi(ci, tb, bk, rows, tw)
            ci += 1
            c0 += rows


class LNBufs:
    def __init__(self, S, st, lnw, lnb, nb=2):
        self.nb = nb
        self.xo = [S.sbuf(st, [128, D], F32, "ln_xo") for _ in range(nb)]
        self.r = S.sbuf(st, [128, D], F32, "ln_r")
        self.o = [S.sbuf(st, [128, D], F32, "ln_o") for _ in range(nb)]
        self.ob = S.sbuf(st, [128, D], BF16, "ln_ob")
        self.oT = [S.sbuf(st, [128, 16, 128], BF16, "ln_oT") for _ in range(nb)]
        self.stats = S.sbuf(st, [128, 4, 6], F32, "ln_stats")
        self.mv = S.sbuf(st, [128, 2], F32, "ln_mv")
        self.rstd = S.sbuf(st, [128, 1], F32, "ln_rstd")
        self.w = S.sbuf(st, [128, D], F32, "ln_w")
        self.b = S.sbuf(st, [128, D], F32, "ln_b")
        S.dma("sp", self.w[:], bcast_rows(lnw, D), self.w, writes=[self.w])
        S.dma("sp", self.b[:], bcast_rows(lnb, D), self.b, writes=[self.b])
        self.i = 0


def ln_prefetch(S, L, xold, g0):
    xo = L.xo[L.i % L.nb]
    S.dma("sp", xo[:], xold[g0:g0 + 128, :], xo, writes=[xo])


def ln_epilogue(S, C, L, h_ap, h_bufs, g0, xnew, xnewT):
    i = L.i
    L.i += 1
    xo = L.xo[i % L.nb]
    o = L.o[i % L.nb]
    oT = L.oT[i % L.nb]
    S.op("dve", lambda en: en.scalar_tensor_tensor(out=L.r[:], in0=xo[:], scalar=float(DN_ALPHA), in1=h_ap,
                                                   op0=ALU.mult, op1=ALU.add), reads=[xo] + list(h_bufs), writes=[L.r])
    for q in range(4):
        S.op("dve", lambda en: en.bn_stats(out=L.stats[:, q, :], in_=L.r[:, q * 512:(q + 1) * 512]), reads=[L.r], writes=[L.stats])
    S.op("dve", lambda en: en.bn_aggr(out=L.mv[:], in_=L.stats[:].rearrange("p a b -> p (a b)")), reads=[L.stats], writes=[L.mv])
    rsqrt_eps(S, L.rstd[:], L.mv[:, 1:2], [L.mv], L.rstd, 1.0, LN_EPS)
    ts(S, "dve", L.r[:], L.r[:], L.mv[:, 0:1], L.rstd[:, 0:1], ALU.subtract, ALU.mult, [L.r, L.mv, L.rstd], [L.r])
    tt(S, "pool", L.r[:], L.r[:], L.w[:], ALU.mult, [L.r, L.w], [L.r])
    tt(S, "pool", o[:], L.r[:], L.b[:], ALU.add, [L.r, L.b], [o])
    S.dma("sp", xnew[g0:g0 + 128, :], o[:], o, reads=[o])
    if xnewT is not None:
        cp(S, "act", L.ob[:], o[:], [o], [L.ob])
        transposes_to(S, C, L.ob, [L.ob], 16, lambda j, n: oT[:, j:j + n, :], [oT], BF16, evac="act")
        S.dma("sp", xnewT.rearrange("(kc p) t -> p kc t", p=128)[:, :, g0:g0 + 128], oT[:], oT, reads=[oT])


def phase_proj_ln(S, C, srcT, K, W, xold, xnew, xnewT, lnw, lnb, NT):
    KC = K // 128
    TG = min(1024, NT)
    nt = TG // 128
    FB = 256
    with contextlib.ExitStack() as st:
        xTt = [S.sbuf(st, [128, KC, 128], BF16, "pl_xT") for _ in range(nt)]
        slabs = [S.sbuf(st, [128, KC, FB], BF16, "pl_slab") for _ in range(3)]
        hst = [S.sbuf(st, [128, D], BF16, "pl_h") for _ in range(nt)]
        L = LNBufs(S, st, lnw, lnb, nb=1)
        srcv = srcT.rearrange("(kc p) t -> p kc t", p=128)
        ng = NT // TG
        nblk = D // FB
        total_blk = ng * nblk
        load_wslab(S, slabs[0], W, KC, 0, FB)
        if total_blk > 1:
            load_wslab(S, slabs[1], W, KC, (1 % nblk) * FB, FB)
        nslab = 0
        pending = []
        for g in range(ng):
            for t_ in range(nt):
                S.dma("sp", xTt[t_][:], srcv[:, :, g * TG + t_ * 128:g * TG + (t_ + 1) * 128], xTt[t_], writes=[xTt[t_]])
            for bi in range(nblk):
                if nslab + 2 < total_blk:
                    load_wslab(S, slabs[(nslab + 2) % 3], W, KC, ((bi + 2) % nblk) * FB, FB)
                sl = slabs[nslab % 3]
                nslab += 1
                for t_ in range(nt):
                    bk = S.bank()
                    for kc in range(KC):
                        S.op("pe", lambda en: en.matmul(bk[:, 0:FB], lhsT=xTt[t_][:, kc, :], rhs=sl[:, kc, :],
                                                        start=(kc == 0), stop=(kc == KC - 1)),
                             reads=[xTt[t_], sl], writes=[bk], sig=(kc == KC - 1))
                    if pending:
                        pending.pop(0)()
                    cp(S, "act", hst[t_][:, bi * FB:(bi + 1) * FB], bk[:, 0:FB], [bk], [hst[t_]])
                    if bi == nblk - 1:
                        def ep(g=g, t_=t_):
                            ln_prefetch(S, L, xold, g * TG + t_ * 128)
                            ln_epilogue(S, C, L, hst[t_][:], [hst[t_]], g * TG + t_ * 128, xnew, xnewT)
                        pending.append(ep)
        for ep in pending:
            ep()
        S.barrier()


def phase_make_xT(S, C, x, xT_d, NT):
    with contextlib.ExitStack() as st:
        xin = [S.sbuf(st, [128, D], F32, "mx_in") for _ in range(2)]
        xb = [S.sbuf(st, [128, D], BF16, "mx_b") for _ in range(2)]
        oT = [S.sbuf(st, [128, 16, 128], BF16, "mx_oT") for _ in range(2)]
        dv = xT_d.rearrange("(kc p) t -> p kc t", p=128)
        n = NT // 128
        S.dma("sp", xin[0][:], x[0:128, :], xin[0], writes=[xin[0]])
        for i in range(n):
            if i + 1 < n:
                S.dma("sp", xin[(i + 1) % 2][:], x[(i + 1) * 128:(i + 2) * 128, :], xin[(i + 1) % 2], writes=[xin[(i + 1) % 2]])
            cp(S, "act", xb[i % 2][:], xin[i % 2][:], [xin[i % 2]], [xb[i % 2]])
            o = oT[i % 2]
            transposes_to(S, C, xb[i % 2], [xb[i % 2]], 16, lambda j, n_: o[:, j:j + n_, :], [o], BF16)
            S.dma("sp", dv[:, :, i * 128:(i + 1) * 128], o[:], o, reads=[o])
        S.barrier()


def phase_peer_route(S, C, xT_d, wq, keysT_d, Gh, NT):
    TG = min(512, NT)
    ntile = NT // 128
    with contextlib.ExitStack() as st:
        xT = S.sbuf(st, [128, 16, TG], BF16, "rt_xT")
        qT = S.sbuf(st, [128, 16, TG], BF16, "rt_qT")
        slabs = [S.sbuf(st, [128, 16, 256], BF16, "rt_slab") for _ in range(2)]
        keys = S.sbuf(st, [128, 16, 128], BF16, "rt_keys")
        S.dma("pool", keys[:], keysT_d.rearrange("hc d k -> d hc k"), keys, writes=[keys])

        class TS:
            pass
        tsb = []
        for p in range(2):
            T = TS()
            S.nbuf += 1
            scr = st.enter_context(S.nc.sbuf_tensor(f"rt_scr{p}_{S.nbuf}", [128, 3, 2048], F32))
            T.sc = Buf(scr[:, 0, :].rearrange("p (a b) -> p a b", a=16), "sc")
            T.cand2 = Buf(scr[:, 0, :].rearrange("p (a b) -> p a b", a=8), "cand2")
            T.sc2 = Buf(scr[:, 1, :].rearrange("p (a b) -> p a b", a=16), "sc2")
            T.eq = Buf(scr[:, 1, :].rearrange("p (h k j) -> p h k j", h=8, k=16), "eq")
            T.cand = Buf(scr[:, 2, :].rearrange("p (a b) -> p a b", a=8), "cand")
            T.vals = S.sbuf(st, [128, 16, 16], F32, "rt_vals")
            T.idx = S.sbuf(st, [128, 16, 16], U32, "rt_idx")
            T.idxf = S.sbuf(st, [128, 16, 16], F32, "rt_idxf")
            T.best = S.sbuf(st, [128, 8, 16], F32, "rt_best")
            T.pos = S.sbuf(st, [128, 8, 16], U32, "rt_pos")
            T.pos2 = S.sbuf(st, [128, 8, 16], U32, "rt_pos2")
            T.k1 = S.sbuf(st, [128, 8, 16], F32, "rt_k1")
            T.k2 = S.sbuf(st, [128, 8, 16], F32, "rt_k2")
            T.sel = S.sbuf(st, [128, 3, 128], F32, "rt_sel")
            T.selT = S.sbuf(st, [128, 3, 128], F32, "rt_selT")
            T.nidx7 = S.sbuf(st, [128, 128], F32, "rt_nidx7")
            T.nb = S.sbuf(st, [128, 8], F32, "rt_nb")
            T.Z = S.sbuf(st, [128, 8], F32, "rt_Z")
            T.ex = S.sbuf(st, [128, 8, 16], F32, "rt_ex")
            tsb.append(T)
        NOH = 8
        oh1 = [S.sbuf(st, [128, 128], BF16, "rt_oh1") for _ in range(NOH)]
        ohraw = [S.sbuf(st, [128, 16, 128], BF16, "rt_ohraw") for _ in range(2)]
        oh2b = [S.sbuf(st, [128, 16, 128], BF16, "rt_oh2b") for _ in range(3)]
        Gsb = [S.sbuf(st, [128, 128, 128], BF16, "rt_G") for _ in range(2)]
        xv = xT_d.rearrange("(kc p) t -> p kc t", p=128)
        Ghv = Gh.rearrange("a p t -> p a t")

        def gemm_group(g):
            S.dma("sp", xT[:], xv[:, :, g * TG:(g + 1) * TG], xT, writes=[xT])

            def epi(ci, tb, bk, rows, tw):
                cp(S, "act", qT[:, ci, tb * 512:tb * 512 + tw], bk[:, 0:tw], [bk], [qT])
            gemm_feat(S, C, xT, 16, TG, wq, 0, D, slabs, epi, fbw=256)

        def topk_thunks(gi):
            T = tsb[gi % 2]
            g, t_ = divmod(gi, TG // 128)
            tsl = slice(t_ * 128, (t_ + 1) * 128)
            th = []
            if t_ == 0:
                th.append(lambda: gemm_group(g))

            def scores(q4):
                bk = S.bank()
                for j in range(4):
                    hc = q4 * 4 + j
                    S.op("pe", lambda en: en.matmul(bk[:, j * 128:(j + 1) * 128], lhsT=qT[:, hc, tsl], rhs=keys[:, hc, :],
                                                    start=True, stop=True), reads=[qT, keys], writes=[bk], sig=(j == 3))
                cp(S, "act", T.sc[:, q4 * 4:(q4 + 1) * 4, :], bk[:].rearrange("p (a b) -> p a b", a=4), [bk], [T.sc, T.cand2])
            for q4 in range(4):
                th.append(lambda q4=q4: scores(q4))
            for hc in range(16):
                th.append(lambda hc=hc: S.op("dve", lambda en: en.max(out=T.vals[:, hc, 0:8], in_=T.sc[:, hc, :]), [T.sc], [T.vals]))
            for hc in range(16):
                th.append(lambda hc=hc: S.op("dve", lambda en: en.max_index(out=T.idx[:, hc, 0:8], in_max=T.vals[:, hc, 0:8],
                                                                           in_values=T.sc[:, hc, :]), [T.sc, T.vals], [T.idx]))
            for hc in range(16):
                th.append(lambda hc=hc: S.op("dve", lambda en: en.match_replace(out=T.sc2[:, hc, :], in_to_replace=T.vals[:, hc, 0:8],
                                                                               in_values=T.sc[:, hc, :], imm_value=-1e30),
                                             [T.sc, T.vals], [T.sc2]))
            for hc in range(16):
                th.append(lambda hc=hc: S.op("dve", lambda en: en.max(out=T.vals[:, hc, 8:16], in_=T.sc2[:, hc, :]), [T.sc2], [T.vals]))
            for hc in range(16):
                th.append(lambda hc=hc: S.op("dve", lambda en: en.max_index(out=T.idx[:, hc, 8:16], in_max=T.vals[:, hc, 8:16],
                                                                           in_values=T.sc2[:, hc, :]), [T.sc2, T.vals], [T.idx]))
            v4 = T.vals[:].rearrange("p (h c) k -> p h c k", c=2)
            i4 = T.idxf[:].rearrange("p (h c) k -> p h c k", c=2)
            th.append(lambda: cp(S, "dve", T.idxf[:], T.idx[:], [T.idx], [T.idxf]))
            th.append(lambda: tt(S, "pool", T.cand[:].rearrange("p h (a b) -> p h a b", a=16),
                                 v4[:, :, 0, :].unsqueeze(3).to_broadcast([128, 8, 16, 16]),
                                 v4[:, :, 1, :].unsqueeze(2).to_broadcast([128, 8, 16, 16]), ALU.add, [T.vals], [T.cand]))
            for h in range(8):
                th.append(lambda h=h: S.op("dve", lambda en: en.max(out=T.best[:, h, 0:8], in_=T.cand[:, h, :]), [T.cand], [T.best]))
            for h in range(8):
                th.append(lambda h=h: S.op("dve", lambda en: en.max_index(out=T.pos[:, h, 0:8], in_max=T.best[:, h, 0:8],
                                                                         in_values=T.cand[:, h, :]), [T.cand, T.best], [T.pos]))
            for h in range(8):
                th.append(lambda h=h: S.op("dve", lambda en: en.match_replace(out=T.cand2[:, h, :], in_to_replace=T.best[:, h, 0:8],
                                                                             in_values=T.cand[:, h, :], imm_value=-1e30),
                                           [T.cand, T.best], [T.cand2, T.sc]))
            for h in range(8):
                th.append(lambda h=h: S.op("dve", lambda en: en.max(out=T.best[:, h, 8:16], in_=T.cand2[:, h, :]), [T.cand2], [T.best]))
            for h in range(8):
                th.append(lambda h=h: S.op("dve", lambda en: en.max_index(out=T.pos[:, h, 8:16], in_max=T.best[:, h, 8:16],
                                                                         in_values=T.cand2[:, h, :]), [T.cand2, T.best], [T.pos]))
            th.append(lambda: ts(S, "dve", T.pos2[:], T.pos[:], 15, None, ALU.bitwise_and, None, [T.pos], [T.pos2]))
            th.append(lambda: cp(S, "dve", T.k2[:], T.pos2[:], [T.pos2], [T.k2]))
            th.append(lambda: ts(S, "dve", T.pos2[:], T.pos[:], 4, None, ALU.logical_shift_right, None, [T.pos], [T.pos2]))
            th.append(lambda: cp(S, "dve", T.k1[:], T.pos2[:], [T.pos2], [T.k1]))
            for which, kk in ((0, T.k1), (1, T.k2)):
                th.append(lambda kk=kk: tt(S, "dve", T.eq[:], kk[:].unsqueeze(3).to_broadcast([128, 8, 16, 16]),
                                           C.iota16[:].unsqueeze(1).unsqueeze(1).to_broadcast([128, 8, 16, 16]), ALU.is_equal,
                                           [kk, C.iota16], [T.eq, T.sc2]))
                th.append(lambda which=which: tt(S, "pool", T.eq[:], T.eq[:], i4[:, :, which, :].unsqueeze(2).to_broadcast([128, 8, 16, 16]),
                                                 ALU.mult, [T.eq, T.idxf], [T.eq]))
                th.append(lambda which=which: S.op("dve", lambda en: en.tensor_reduce(out=T.sel[:, which, :],
                                                                                     in_=T.eq[:].rearrange("p h k j -> p (h k) j"),
                                                                                     axis=AX.X, op=ALU.add), [T.eq], [T.sel]))
            th.append(lambda: ts(S, "dve", T.nb[:], T.best[:, :, 0], -1.0, None, ALU.mult, None, [T.best], [T.nb]))
            def exps():
                for h in range(8):
                    act(S, T.ex[:, h, :], T.best[:, h, :], AF.Exp, [T.best, T.nb], [T.ex, T.Z], bias=T.nb[:, h:h + 1], accum_out=T.Z[:, h:h + 1])
            th.append(exps)
            th.append(lambda: S.op("dve", lambda en: en.reciprocal(out=T.Z[:], in_=T.Z[:]), [T.Z], [T.Z]))
            th.append(lambda: ts(S, "dve", T.Z[:], T.Z[:], 0.886226925452758, None, ALU.mult, None, [T.Z], [T.Z]))
            th.append(lambda: tt(S, "dve", T.sel[:, 2, :].rearrange("p (h k) -> p h k", h=8), T.ex[:],
                                 T.Z[:].unsqueeze(2).to_broadcast([128, 8, 16]), ALU.mult, [T.ex, T.Z], [T.sel]))

            def tr():
                bk = S.bank()
                for j in range(3):
                    S.op("pe", lambda en: en.transpose(bk[:, j * 128:(j + 1) * 128], T.sel[:, j, :], C.identf[:]),
                         reads=[T.sel, C.identf], writes=[bk], sig=(j == 2))
                cp(S, "act", T.selT[:], bk[:, 0:384].rearrange("p (a b) -> p a b", a=3), [bk], [T.selT])
                ts(S, "dve", T.nidx7[:], T.selT[:, 0, :], -7.0, None, ALU.mult, None, [T.selT], [T.nidx7])
            th.append(tr)
            return th

        def pertoken_thunks(gi):
            T = tsb[gi % 2]
            Gs = Gsb[gi % 2]
            pend = []
            th = []

            def evac(pb, p4):
                cp(S, "act" if p4 % 2 == 0 else "dve", Gs[:, :, p4 * 4:(p4 + 1) * 4], pb[:].rearrange("p (i t) -> p i t", t=4), [pb], [Gs])

            def batch(t16):
                raw = ohraw[t16 % 2]
                ob = oh2b[t16 % 3]
                tt(S, "dve", raw[:], C.iota[:].unsqueeze(1).to_broadcast([128, 16, 128]),
                   T.selT[:, 1, t16 * 16:(t16 + 1) * 16].unsqueeze(2).to_broadcast([128, 16, 128]), ALU.is_equal, [C.iota, T.selT], [raw])
                tt(S, "pool", ob[:], raw[:], T.selT[:, 2, t16 * 16:(t16 + 1) * 16].unsqueeze(2).to_broadcast([128, 16, 128]), ALU.mult,
                   [raw, T.selT], [ob])

            def grp(t4):
                if t4 == 0:
                    batch(0)
                    batch(1)
                if t4 % 4 == 0 and t4 // 4 + 2 < 8:
                    batch(t4 // 4 + 2)
                ob = oh2b[(t4 // 4) % 3]
                bk = S.bank_private(0, 4)
                for j in range(4):
                    tk = t4 * 4 + j
                    o1 = oh1[tk % NOH]
                    act(S, o1[:], C.iota[:], AF.Derivative_Erf, [C.iota, T.nidx7], [o1], scale=7.0, bias=T.nidx7[:, tk:tk + 1])
                    S.op("pe", lambda en: en.matmul(bk[:].rearrange("p (i j) -> p j i", j=4)[:, j, :], lhsT=ob[:, tk % 16, :], rhs=o1[:],
                                                    start=True, stop=True), reads=[o1, ob], writes=[bk], sig=(j == 3))
                pend.append((bk, t4))
                if len(pend) > 2:
                    evac(*pend.pop(0))
            for t4 in range(32):
                th.append(lambda t4=t4: grp(t4))

            def fin():
                for pb, p4 in pend:
                    evac(pb, p4)
                g0 = gi * 128
                for q in range(4):
                    S.dma("sp", Ghv[:, q * 32:(q + 1) * 32, g0:g0 + 128], Gs[:, q * 32:(q + 1) * 32, :], Gs, reads=[Gs])
            th.append(fin)
            return th

        S.bank_pool = (4, 4)
        for f in topk_thunks(0):
            f()
        for gi in range(ntile):
            A = pertoken_thunks(gi)
            B = topk_thunks(gi + 1) if gi + 1 < ntile else []
            ia = ib = 0
            ratio = (len(B) + len(A) - 1) // len(A) if B else 0
            while ia < len(A) or ib < len(B):
                if ia < len(A):
                    A[ia]()
                    ia += 1
                for _ in range(ratio):
                    if ib < len(B):
                        B[ib]()
                        ib += 1
                if ia >= len(A):
                    while ib < len(B):
                        B[ib]()
                        ib += 1
        S.bank_pool = (0, 8)
        S.barrier()


def phase_peer_dense(S, C, xT_d, uT_d, v_d, Gh, xold, xnew, xnewT, lnw, lnb, NT):
    TG = min(1024, NT)
    EG = 4
    NEG_ = 128 // EG
    ntile = TG // 128
    with contextlib.ExitStack() as st:
        xT = S.sbuf(st, [128, 16, TG], BF16, "pd_xT")
        acc = S.sbuf(st, [128, ntile, D], F32, "pd_acc")
        gl = [S.sbuf(st, [128, 512], BF16, "pd_gl") for _ in range(2)]
        xv = xT_d.rearrange("(kc p) t -> p kc t", p=128)
        uv = uT_d.rearrange("(kc p) e -> p kc e", p=128)
        vv = v_d.rearrange("(a p) d -> p a d", p=128)
        Ghv = Gh.rearrange("a p t -> p a t")
        ngl = 0
        for g in range(NT // TG):
            S.dma("sp", xT[:], xv[:, :, g * TG:(g + 1) * TG], xT, writes=[xT])
            with contextlib.ExitStack() as st2:
                us = [S.sbuf(st2, [128, 16, EG * 128], BF16, "pd_u") for _ in range(2)]
                vs = [S.sbuf(st2, [128, EG, D], BF16, "pd_v") for _ in range(2)]
                Gs = [S.sbuf(st2, [128, EG, TG], BF16, "pd_G") for _ in range(2)]
                GH = [S.sbuf(st2, [128, EG, TG], BF16, "pd_GH") for _ in range(2)]

                def load(eg):
                    b = eg % 2
                    S.dma("pool", us[b][:], uv[:, :, eg * EG * 128:(eg + 1) * EG * 128], us[b], writes=[us[b]])
                    S.dma("pool", vs[b][:], vv[:, eg * EG:(eg + 1) * EG, :], vs[b], writes=[vs[b]])
                    S.dma("sp", Gs[b][:], Ghv[:, eg * EG:(eg + 1) * EG, g * TG:(g + 1) * TG], Gs[b], writes=[Gs[b]])

                load(0)
                for eg in range(NEG_):
                    if eg + 1 < NEG_:
                        load(eg + 1)
                    b = eg % 2
                    for j in range(EG):
                        for tb in range((TG + 511) // 512):
                            tw = min(512, TG - tb * 512)
                            bk = S.bank()
                            for kc in range(16):
                                S.op("pe", lambda en: en.matmul(bk[:, 0:tw], lhsT=us[b][:, kc, j * 128:(j + 1) * 128],
                                                                rhs=xT[:, kc, tb * 512:tb * 512 + tw], start=(kc == 0), stop=(kc == 15)),
                                     reads=[us[b], xT], writes=[bk], sig=(kc == 15))
                            glb = gl[ngl % 2]
                            ngl += 1
                            act(S, glb[:, 0:tw], bk[:, 0:tw], AF.Gelu, [bk], [glb])
                            tt(S, "pool", GH[b][:, j, tb * 512:tb * 512 + tw], glb[:, 0:tw], Gs[b][:, j, tb * 512:tb * 512 + tw], ALU.mult,
                               [glb, Gs[b]], [GH[b]])
                    for t_ in range(ntile):
                        for db in range(4):
                            bk = S.bank()
                            for j in range(EG):
                                S.op("pe", lambda en: en.matmul(bk[:], lhsT=GH[b][:, j, t_ * 128:(t_ + 1) * 128],
                                                                rhs=vs[b][:, j, db * 512:(db + 1) * 512], start=(j == 0), stop=(j == EG - 1)),
                                     reads=[GH[b], vs[b]], writes=[bk], sig=(j == EG - 1))
                            a = acc[:, t_, db * 512:(db + 1) * 512]
                            if eg == 0:
                                cp(S, "dve", a, bk[:], [bk], [acc])
                            else:
                                tt(S, "dve", a, a, bk[:], ALU.add, [acc, bk], [acc])
                S.barrier()
            with contextlib.ExitStack() as st3:
                L = LNBufs(S, st3, lnw, lnb, nb=2)
                ln_prefetch(S, L, xold, g * TG)
                for t_ in range(ntile):
                    if t_ + 1 < ntile:
                        L.i += 1
                        ln_prefetch(S, L, xold, g * TG + (t_ + 1) * 128)
                        L.i -= 1
                    ln_epilogue(S, C, L, acc[:, t_, :], [acc], g * TG + t_ * 128, xnew, xnewT)
                S.barrier()


def setup_consts(S, C, st, identf_d, iota_d):
    C.identf = S.sbuf(st, [128, 128], F32, "identf")
    C.identb = S.sbuf(st, [128, 128], BF16, "identb")
    C.iota = S.sbuf(st, [128, 128], F32, "iota")
    C.iota16 = S.sbuf(st, [128, 16], F32, "iota16")
    S.dma("sp", C.identf[:], identf_d, C.identf, writes=[C.identf])
    S.dma("sp", C.iota[:], iota_d, C.iota, writes=[C.iota])
    cp(S, "dve", C.identb[:], C.identf[:], [C.identf], [C.identb])
    cp(S, "dve", C.iota16[:], C.iota[:, 0:16], [C.iota], [C.iota16])
    C.onecol = S.sbuf(st, [128, 1], F32, "onecol")
    S.op("dve", lambda en: en.memset(C.onecol[:], 1.0), [], [C.onecol])


def phase_ssm_inproj(S, C, xT_d, W, convwT, convb2, dtbias, Alog, xbcT_d, z_d, dtT_d, dAT_d, NT, LSEQ):
    with contextlib.ExitStack() as st:
        xT = S.sbuf(st, [128, 16, LSEQ], BF16, "si_xT")
        slabs = [S.sbuf(st, [128, 16, 512], BF16, "si_slab") for _ in range(2)]
        pre = [S.sbuf(st, [128, 3 + LSEQ], F32, "si_pre") for _ in range(2)]
        accb = S.sbuf(st, [128, LSEQ], F32, "si_acc")
        outb = [S.sbuf(st, [128, LSEQ], BF16, "si_out") for _ in range(2)]
        zst = [S.sbuf(st, [128, 512], BF16, "si_z") for _ in range(4)]
        cw = S.sbuf(st, [128, 48, 4], F32, "si_cw")
        cb = S.sbuf(st, [128, 48], F32, "si_cb")
        dtb = S.sbuf(st, [64, 1], F32, "si_dtb")
        Aneg = S.sbuf(st, [64, 1], F32, "si_A")
        dtr = S.sbuf(st, [64, LSEQ], F32, "si_dtr")
        dta = S.sbuf(st, [64, LSEQ], F32, "si_dta")
        dtl = S.sbuf(st, [64, LSEQ], F32, "si_dtl")
        S.dma("sp", cw[:], convwT.rearrange("(ci p) j -> p ci j", p=128), cw, writes=[cw])
        S.dma("sp", cb[:], convb2, cb, writes=[cb])
        S.dma("sp", dtb[:], dtbias.rearrange("(p o) -> p o", o=1), dtb, writes=[dtb])
        S.dma("sp", Aneg[:], Alog.rearrange("(p o) -> p o", o=1), Aneg, writes=[Aneg])
        act(S, Aneg[:], Aneg[:], AF.Exp, [Aneg], [Aneg])
        ts(S, "dve", Aneg[:], Aneg[:], -1.0, None, ALU.mult, None, [Aneg], [Aneg])
        for p_ in pre:
            S.op("dve", lambda en: en.memset(p_[:, 0:3], 0.0), [], [p_])
        xv = xT_d.rearrange("(kc p) t -> p kc t", p=128)
        nz = [0]
        for sq in range(NT // LSEQ):
            s0 = sq * LSEQ
            S.dma("sp", xT[:], xv[:, :, s0:s0 + LSEQ], xT, writes=[xT])
            ntb = (LSEQ + 511) // 512

            def epi_f(ci, tb, bk, rows, tw, s0=s0):
                if ci < 48:
                    pr = pre[ci % 2]
                    cp(S, "act", pr[:, 3 + tb * 512:3 + tb * 512 + tw], bk[:, 0:tw], [bk], [pr])
                    if tb == ntb - 1:
                        ob = outb[ci % 2]
                        ts(S, "dve", accb[:], pr[:, 3:3 + LSEQ], cw[:, ci, 3:4], None, ALU.mult, None, [pr, cw], [accb])
                        for j in (2, 1, 0):
                            S.op("dve", lambda en: en.scalar_tensor_tensor(out=accb[:], in0=pr[:, j:j + LSEQ], scalar=cw[:, ci, j:j + 1],
                                                                           in1=accb[:], op0=ALU.mult, op1=ALU.add), [pr, cw, accb], [accb])
                        act(S, ob[:], accb[:], AF.Silu, [accb, cb], [ob], bias=cb[:, ci:ci + 1])
                        S.dma("sp", xbcT_d[ci * 128:(ci + 1) * 128, s0:s0 + LSEQ], ob[:], ob, reads=[ob])
                else:
                    cp(S, "act", dtr[:, tb * 512:tb * 512 + tw], bk[0:64, 0:tw], [bk], [dtr])
                    if tb == ntb - 1:
                        ts(S, "dve", dtr[:], dtr[:], dtb[:, 0:1], None, ALU.add, None, [dtr, dtb], [dtr])
                        act(S, dta[:], dtr[:], AF.Abs, [dtr], [dta])
                        act(S, dta[:], dta[:], AF.Exp, [dta], [dta], scale=-1.0)
                        act(S, dtl[:], dta[:], AF.Ln, [dta, C.onecol], [dtl], bias=C.onecol[0:64, 0:1])
                        S.op("dve", lambda en: en.scalar_tensor_tensor(out=dta[:], in0=dtr[:], scalar=0.0, in1=dtl[:],
                                                                       op0=ALU.max, op1=ALU.add), [dtr, dtl], [dta])
                        ts(S, "dve", dtl[:], dta[:], Aneg[:, 0:1], None, ALU.mult, None, [dta, Aneg], [dtl])
                        S.dma("sp", dtT_d[:, s0:s0 + LSEQ], dta[:], dta, reads=[dta])
                        S.dma("sp", dAT_d[:, s0:s0 + LSEQ], dtl[:], dtl, reads=[dtl])
            gemm_feat(S, C, xT, 16, LSEQ, W, 4096, 6144 + 64, slabs, epi_f)

            def epi_z(fb, t_, bk, fw, s0=s0):
                zb = zst[nz[0] % 4]
                nz[0] += 1
                cp(S, "act", zb[:, 0:fw], bk[:, 0:fw], [bk], [zb])
                S.dma("sp", z_d[s0 + t_ * 128:s0 + (t_ + 1) * 128, fb * 512:fb * 512 + fw], zb[:, 0:fw], zb, reads=[zb])
            gemm_tok(S, C, xT, 16, LSEQ, W, 0, 4096, slabs, epi_z)
        S.barrier()


def phase_ssd(S, C, xbcT_d, z_d, dtT_d, dAT_d, ssmD, normw, negmask_d, ynT_d, NT, LSEQ):
    with contextlib.ExitStack() as st:
        xsT = [S.sbuf(st, [128, 32, 128], BF16, "sd_xsT") for _ in range(2)]
        BCT = [S.sbuf(st, [128, 16, 128], BF16, "sd_BCT") for _ in range(2)]
        zt = S.sbuf(st, [128, 4096], BF16, "sd_z")
        dsm = [S.sbuf(st, [64, 2, 128], F32, "sd_dsm") for _ in range(3)]
        cumT = S.sbuf(st, [64, 128], F32, "sd_cumT")
        cumhl = S.sbuf(st, [64, 2, 128], BF16, "sd_cumhl")
        winT = S.sbuf(st, [64, 128], F32, "sd_winT")
        elT = S.sbuf(st, [64, 1], F32, "sd_elT")
        diagE = S.sbuf(st, [64, 64], F32, "sd_diagE")
        tm = S.sbuf(st, [128, 3, 64], F32, "sd_tm")
        ncum = S.sbuf(st, [128, 64], F32, "sd_ncum")
        ecum = S.sbuf(st, [128, 64], F32, "sd_ecum")
        elbc = S.sbuf(st, [128, 64], F32, "sd_elbc")
        sel = S.sbuf(st, [64, 64, 128], BF16, "sd_sel")
        LT = S.sbuf(st, [128, 64, 128], BF16, "sd_LT")
        xs = S.sbuf(st, [128, 64, 64], BF16, "sd_xs")
        Btm = S.sbuf(st, [128, 8, 128], BF16, "sd_Btm")
        xdt = S.sbuf(st, [128, 64, 64], BF16, "sd_xdt")
        xw = S.sbuf(st, [128, 64, 64], BF16, "sd_xw")
        MT = [S.sbuf(st, [128, 8, 128], BF16, "sd_MT") for _ in range(2)]
        Y = S.sbuf(st, [128, 64, 64], F32, "sd_Y")
        t1 = S.sbuf(st, [128, 8, 64], F32, "sd_t1")
        state = S.sbuf(st, [128, 8, 512], F32, "sd_state")
        stbf = S.sbuf(st, [128, 8, 512], BF16, "sd_stbf")
        sz = S.sbuf(st, [128, 4096], BF16, "sd_sz")
        nw = S.sbuf(st, [128, 4096], BF16, "sd_nw")
        Dbc = S.sbuf(st, [128, 64], F32, "sd_Dbc")
        nm = S.sbuf(st, [128, 512], BF16, "sd_negmask")
        nmf = S.sbuf(st, [128, 512], F32, "sd_negmaskf")
        ones64 = S.sbuf(st, [64, 128], F32, "sd_ones64")
        ss = S.sbuf(st, [128, 8], F32, "sd_ss")
        junk = S.sbuf(st, [128, 512], BF16, "sd_junk")
        ynb = S.sbuf(st, [128, 4096], BF16, "sd_ynb")
        ynT = S.sbuf(st, [128, 32, 128], BF16, "sd_ynT")
        S.dma("pool", nw[:], bcast_rows(normw, 4096), nw, writes=[nw])
        S.dma("sp", Dbc[:], bcast_rows(ssmD, 64), Dbc, writes=[Dbc])
        S.dma("sp", nmf[:], negmask_d, nmf, writes=[nmf])
        cp(S, "dve", nm[:], nmf[:], [nmf], [nm])
        S.op("dve", lambda en: en.memset(ones64[:], 1.0), [], [ones64])
        cp(S, "dve", sel[:], C.identf[0:64, 0:64].unsqueeze(2).to_broadcast([64, 64, 128]), [C.identf], [sel])
        xv = xbcT_d.rearrange("(c p) t -> p c t", p=128)
        ynv = ynT_d.rearrange("(c p) t -> p c t", p=128)
        nch = LSEQ // 128
        ntot = (NT // LSEQ) * nch

        def pos_of(i):
            sq, c = divmod(i, nch)
            return sq, c, sq * LSEQ + c * 128

        def load(i):
            _, _, g0 = pos_of(i)
            b = i % 2
            S.dma("sp", xsT[b][:], xv[:, 0:32, g0:g0 + 128], xsT[b], writes=[xsT[b]])
            S.dma("sp", BCT[b][:], xv[:, 32:48, g0:g0 + 128], BCT[b], writes=[BCT[b]])

        def load_small(i):
            _, _, g0 = pos_of(i)
            d = dsm[i % 3]
            S.dma("sp", d[:, 0, :], dtT_d[:, g0:g0 + 128], d, writes=[d])
            S.dma("sp", d[:, 1, :], dAT_d[:, g0:g0 + 128], d, writes=[d])

        def prep(i):
            d = dsm[i % 3]
            S.op("dve", lambda en: en.tensor_tensor_scan(out=cumT[:], data0=ones64[:], data1=d[:, 1, :], initial=0.0,
                                                         op0=ALU.mult, op1=ALU.add), [ones64, d], [cumT])
            cp(S, "act", cumhl[:, 0, :], cumT[:], [cumT], [cumhl])
            tt(S, "dve", cumhl[:, 1, :], cumT[:], cumhl[:, 0, :], ALU.subtract, [cumT, cumhl], [cumhl])
            act(S, winT[:], cumT[:], AF.Exp, [cumT], [winT], scale=-1.0, bias=cumT[:, 127:128])
            act(S, elT[:], cumT[:, 127:128], AF.Exp, [cumT], [elT])
            ts(S, "dve", diagE[:], C.identf[0:64, 0:64], elT[:, 0:1], None, ALU.mult, None, [C.identf, elT], [diagE])
            bk = S.bank()
            for j, (src, sb_) in enumerate(((d[:, 0, :], d), (cumT[:], cumT), (winT[:], winT))):
                S.op("pe", lambda en: en.transpose(bk[:, j * 64:(j + 1) * 64], src, C.identf[0:64, 0:64]),
                     reads=[sb_, C.identf], writes=[bk], sig=(j == 2))
            cp(S, "act", tm[:], bk[:, 0:192].rearrange("p (a b) -> p a b", a=3), [bk], [tm])
            ts(S, "dve", ncum[:], tm[:, 1, :], -1.0, None, ALU.mult, None, [tm], [ncum])
            act(S, ecum[:], tm[:, 1, :], AF.Exp, [tm], [ecum])
            bk = S.bank()
            S.op("pe", lambda en: en.matmul(bk[:, 0:64], lhsT=ones64[:], rhs=diagE[:], start=True, stop=True),
                 reads=[ones64, diagE], writes=[bk])
            cp(S, "act", elbc[:], bk[:, 0:64], [bk], [elbc])
            for q in range(16):
                bk = S.bank()
                first = True
                for j in range(4):
                    h = q * 4 + j
                    for hl in range(2):
                        S.op("pe", lambda en: en.matmul(bk[:, j * 128:(j + 1) * 128], lhsT=sel[:, h, :], rhs=cumhl[:, hl, :],
                                                        start=first, stop=False), reads=[sel, cumhl], writes=[bk], sig=False)
                        first = False
                S.op("pe", lambda en: en.matmul(bk[:], lhsT=C.identb[:], rhs=nm[:], start=False, stop=True),
                     reads=[C.identb, nm], writes=[bk])
                for j in range(4):
                    h = q * 4 + j
                    act(S, LT[:, h, :], bk[:, j * 128:(j + 1) * 128], AF.Exp, [bk, ncum], [LT], bias=ncum[:, h:h + 1])

        def head(i):
            b = i % 2
            transposes_to(S, C, xsT[b][:].rearrange("p a b -> p (a b)"), [xsT[b]], 32,
                          lambda j, n: xs[:].rearrange("p h d -> p (h d)")[:, j * 128:(j + n) * 128].rearrange("p (a b) -> p a b", a=n), [xs], BF16)
            transposes_to(S, C, BCT[b][:].rearrange("p a b -> p (a b)"), [BCT[b]], 8, lambda j, n: Btm[:, j:j + n, :], [Btm], BF16)
            tt(S, "dve", xdt[:], xs[:], tm[:, 0, :].unsqueeze(2).to_broadcast([128, 64, 64]), ALU.mult, [xs, tm], [xdt])
            tt(S, "pool", xw[:], xdt[:], tm[:, 2, :].unsqueeze(2).to_broadcast([128, 64, 64]), ALU.mult, [xdt, tm], [xw])

        def groups(i):
            b = i % 2
            for g in range(8):
                hs = slice(8 * g, 8 * g + 8)
                bcb = S.bank()
                S.op("pe", lambda en: en.matmul(bcb[:, 0:128], lhsT=BCT[b][:, g, :], rhs=BCT[b][:, 8 + g, :], start=True, stop=True),
                     reads=[BCT[b]], writes=[bcb])
                M = MT[g % 2]
                tt(S, "dve", M[:], LT[:, hs, :], bcb[:, 0:128].unsqueeze(1).to_broadcast([128, 8, 128]), ALU.mult, [LT, bcb], [M])
                byd = S.bank()
                for r in range(8):
                    S.op("pe", lambda en: en.matmul(byd[:, r * 64:(r + 1) * 64], lhsT=M[:, r, :], rhs=xdt[:, 8 * g + r, :], start=True, stop=True),
                         reads=[M, xdt], writes=[byd], sig=(r == 7))
                byo = S.bank()
                S.op("pe", lambda en: en.matmul(byo[:], lhsT=BCT[b][:, 8 + g, :], rhs=stbf[:, g, :], start=True, stop=True),
                     reads=[BCT[b], stbf], writes=[byo])
                tt(S, "dve", t1[:], byo[:].rearrange("p (r d) -> p r d", r=8), ecum[:, hs].unsqueeze(2).to_broadcast([128, 8, 64]), ALU.mult,
                   [byo, ecum], [t1])
                tt(S, "dve", Y[:, hs, :], byd[:].rearrange("p (r d) -> p r d", r=8), t1[:], ALU.add, [byd, t1], [Y])
                bns = S.bank()
                S.op("pe", lambda en: en.matmul(bns[:], lhsT=Btm[:, g, :], rhs=xw[:, hs, :], start=True, stop=True),
                     reads=[Btm, xw], writes=[bns])
                sg = state[:, g, :].rearrange("p (r d) -> p r d", r=8)
                tt(S, "pool", sg, sg, elbc[:, hs].unsqueeze(2).to_broadcast([128, 8, 64]), ALU.mult, [state, elbc], [state])
                tt(S, "dve", state[:, g, :], state[:, g, :], bns[:], ALU.add, [state, bns], [state])
                cp(S, "act", stbf[:, g, :], state[:, g, :], [state], [stbf])

        def tail(i):
            tt(S, "pool", xdt[:], xs[:], Dbc[:].unsqueeze(2).to_broadcast([128, 64, 64]), ALU.mult, [xs, Dbc], [xdt])
            Yf = Y[:].rearrange("p h d -> p (h d)")
            tt(S, "dve", Yf, Yf, xdt[:].rearrange("p h d -> p (h d)"), ALU.add, [Y, xdt], [Y])
            act(S, sz[:], zt[:], AF.Silu, [zt], [sz])
            tt(S, "dve", Yf, Yf, sz[:], ALU.mult, [Y, sz], [Y])
            for g in range(8):
                act(S, junk[:], Yf[:, g * 512:(g + 1) * 512], AF.Square, [Y], [junk, ss], accum_out=ss[:, g:g + 1])
            rsqrt_eps(S, ss[:], ss[:], [ss], ss, 1.0 / 512.0, LN_EPS)
            Y8 = Y[:].rearrange("p (g r) d -> p g (r d)", g=8)
            tt(S, "dve", Y8, Y8, ss[:].unsqueeze(2).to_broadcast([128, 8, 512]), ALU.mult, [Y, ss], [Y])
            tt(S, "pool", ynb[:], Yf, nw[:], ALU.mult, [Y, nw], [ynb])

        def out(i):
            _, _, g0 = pos_of(i)
            transposes_to(S, C, ynb, [ynb], 32, lambda j, n: ynT[:, j:j + n, :], [ynT], BF16, evac="act")
            S.dma("sp", ynv[:, :, g0:g0 + 128], ynT[:], ynT, reads=[ynT])

        load(0)
        load_small(0)
        if ntot > 1:
            load_small(1)
        prep(0)
        for i in range(ntot):
            sq, c, g0 = pos_of(i)
            if i + 1 < ntot:
                load(i + 1)
            if i + 2 < ntot:
                load_small(i + 2)
            S.dma("sp", zt[:], z_d[g0:g0 + 128, :], zt, writes=[zt])
            if c == 0:
                S.op("pool", lambda en: en.memset(state[:], 0.0), [], [state])
                S.op("pool", lambda en: en.memset(stbf[:], 0.0), [], [stbf])
            head(i)
            groups(i)
            if i > 0:
                out(i - 1)
            if i + 1 < ntot:
                prep(i + 1)
            tail(i)
        out(ntot - 1)
        S.barrier()


TWO_PI = 6.283185307179586
PI = 3.141592653589793


def _sin_table(S, out_buf, ang_buf, shift, tmp, tmpi, mulc):
    ts(S, "dve", tmp[:], ang_buf[:], float(shift), 1.0 / TWO_PI, ALU.add, ALU.mult, [ang_buf], [tmp])
    cp(S, "dve", tmpi[:], tmp[:], [tmp], [tmpi])
    cp(S, "dve", tmp[:], tmpi[:], [tmpi], [tmp])
    ts(S, "dve", tmp[:], tmp[:], -TWO_PI, float(shift), ALU.mult, ALU.add, [tmp], [tmp])
    tt(S, "dve", out_buf[:], tmp[:], ang_buf[:], ALU.add, [tmp, ang_buf], [out_buf])
    ts(S, "dve", tmp[:], out_buf[:], PI, -TWO_PI, ALU.is_gt, ALU.mult, [out_buf], [tmp])
    tt(S, "dve", out_buf[:], out_buf[:], tmp[:], ALU.add, [out_buf, tmp], [out_buf])
    ts(S, "dve", tmp[:], out_buf[:], -PI, TWO_PI, ALU.is_lt, ALU.mult, [out_buf], [tmp])
    tt(S, "dve", out_buf[:], out_buf[:], tmp[:], ALU.add, [out_buf, tmp], [out_buf])
    act(S, out_buf[:], out_buf[:], AF.Sin, [out_buf], [out_buf])
    if mulc != 1.0:
        ts(S, "dve", out_buf[:], out_buf[:], float(mulc), None, ALU.mult, None, [out_buf], [out_buf])


def phase_ret_inproj(S, C, xT_d, W, pos_d, invfreq_d, qkT_d, vg_d, NT, LSEQ):
    with contextlib.ExitStack() as st:
        xT = S.sbuf(st, [128, 16, LSEQ], BF16, "ri_xT")
        slabs = [S.sbuf(st, [128, 16, 512], BF16, "ri_slab") for _ in range(2)]
        posi = S.sbuf(st, [128, LSEQ], I32, "ri_posi")
        ang = S.sbuf(st, [128, LSEQ], F32, "ri_ang")
        tmp = S.sbuf(st, [128, LSEQ], F32, "ri_tmp")
        tmpi = S.sbuf(st, [128, LSEQ], I32, "ri_tmpi")
        cosk = S.sbuf(st, [128, LSEQ], F32, "ri_cosk")
        sink = S.sbuf(st, [128, LSEQ], F32, "ri_sink")
        cosq = S.sbuf(st, [128, LSEQ], F32, "ri_cosq")
        sinq = S.sbuf(st, [128, LSEQ], F32, "ri_sinq")
        invf = S.sbuf(st, [128, 1], F32, "ri_invf")
        t1s = S.sbuf(st, [128, LSEQ], F32, "ri_t1s")
        ra = S.sbuf(st, [128, 512], F32, "ri_ra")
        rb = S.sbuf(st, [128, 512], F32, "ri_rb")
        rc = S.sbuf(st, [128, 512], F32, "ri_rc")
        rd = S.sbuf(st, [128, 512], F32, "ri_rd")
        o1 = [S.sbuf(st, [128, 512], BF16, "ri_o1") for _ in range(2)]
        o2 = [S.sbuf(st, [128, 512], BF16, "ri_o2") for _ in range(2)]
        vst = [S.sbuf(st, [128, 512], BF16, "ri_v") for _ in range(4)]
        S.dma("sp", invf[:], invfreq_d, invf, writes=[invf])
        xv = xT_d.rearrange("(kc p) t -> p kc t", p=128)
        cnt = [0, 0]
        for sq in range(NT // LSEQ):
            s0 = sq * LSEQ
            S.dma("sp", xT[:], xv[:, :, s0:s0 + LSEQ], xT, writes=[xT])
            S.dma("sp", posi[:], pos_d[sq:sq + 1, :].to_broadcast([128, LSEQ]), posi, writes=[posi])
            cp(S, "dve", ang[:], posi[:], [posi], [ang])
            ts(S, "dve", ang[:], ang[:], invf[:, 0:1], None, ALU.mult, None, [ang, invf], [ang])
            _sin_table(S, sink, ang, 0.0, tmp, tmpi, 1.0)
            _sin_table(S, cosk, ang, PI / 2, tmp, tmpi, 1.0)
            ts(S, "dve", sinq[:], sink[:], 1.0 / 16.0, None, ALU.mult, None, [sink], [sinq])
            ts(S, "dve", cosq[:], cosk[:], 1.0 / 16.0, None, ALU.mult, None, [cosk], [cosq])

            def epi_f(ci, tb, bk, rows, tw, s0=s0):
                cs, sn = (cosq, sinq) if ci < 16 else (cosk, sink)
                sl = slice(tb * 512, tb * 512 + tw)
                if ci % 2 == 0:
                    cp(S, "act", t1s[:, sl], bk[:, 0:tw], [bk], [t1s])
                else:
                    k = cnt[0] % 2
                    cnt[0] += 1
                    tt(S, "pool", ra[:, 0:tw], t1s[:, sl], cs[:, sl], ALU.mult, [t1s, cs], [ra])
                    tt(S, "dve", rb[:, 0:tw], bk[:, 0:tw], sn[:, sl], ALU.mult, [bk, sn], [rb])
                    tt(S, "dve", o1[k][:, 0:tw], ra[:, 0:tw], rb[:, 0:tw], ALU.subtract, [ra, rb], [o1[k]])
                    tt(S, "dve", rc[:, 0:tw], bk[:, 0:tw], cs[:, sl], ALU.mult, [bk, cs], [rc])
                    tt(S, "pool", rd[:, 0:tw], t1s[:, sl], sn[:, sl], ALU.mult, [t1s, sn], [rd])
                    tt(S, "dve", o2[k][:, 0:tw], rc[:, 0:tw], rd[:, 0:tw], ALU.add, [rc, rd], [o2[k]])
                    S.dma("sp", qkT_d[(ci - 1) * 128:ci * 128, s0 + tb * 512:s0 + tb * 512 + tw], o1[k][:, 0:tw], o1[k], reads=[o1[k]])
                    S.dma("sp", qkT_d[ci * 128:(ci + 1) * 128, s0 + tb * 512:s0 + tb * 512 + tw], o2[k][:, 0:tw], o2[k], reads=[o2[k]])
            gemm_feat(S, C, xT, 16, LSEQ, W, 0, 4096, slabs, epi_f)

            def epi_v(fb, t_, bk, fw, s0=s0):
                zb = vst[cnt[1] % 4]
                cnt[1] += 1
                cp(S, "act", zb[:, 0:fw], bk[:, 0:fw], [bk], [zb])
                S.dma("sp", vg_d[s0 + t_ * 128:s0 + (t_ + 1) * 128, fb * 512:fb * 512 + fw], zb[:, 0:fw], zb, reads=[zb])
            gemm_tok(S, C, xT, 16, LSEQ, W, 4096, 8192, slabs, epi_v)
        S.barrier()


def phase_ret(S, C, qkT_d, vg_d, dmatT_d, qdec_d, kdec_d, cdec, gnw, gnb, ynT_d, NT, LSEQ):
    with contextlib.ExitStack() as st:
        qk = [S.sbuf(st, [128, 32, 128], BF16, "rt_qk") for _ in range(2)]
        vg = [S.sbuf(st, [128, 8192], BF16, "rt_vg") for _ in range(2)]
        dmT = S.sbuf(st, [128, 8, 128], F32, "rt_dmT")
        qdec = S.sbuf(st, [128, 8, 128], F32, "rt_qdec")
        kdec = S.sbuf(st, [128, 8], F32, "rt_kdec")
        ST = [S.sbuf(st, [128, 128], BF16, "rt_ST") for _ in range(2)]
        qd = [S.sbuf(st, [128, 2, 128], BF16, "rt_qd") for _ in range(2)]
        kd = [S.sbuf(st, [128, 2, 128], BF16, "rt_kd") for _ in range(2)]
        R = S.sbuf(st, [128, 8, 2, 512], F32, "rt_R")
        Rb = S.sbuf(st, [128, 8, 2, 512], BF16, "rt_Rb")
        yhs = [S.sbuf(st, [128, 4096], F32, "rt_yh") for _ in range(2)]
        stats = S.sbuf(st, [128, 6], F32, "rt_stats")
        mv = S.sbuf(st, [128, 2], F32, "rt_mv")
        rstd = S.sbuf(st, [128, 1], F32, "rt_rstd")
        gw = S.sbuf(st, [128, 4096], BF16, "rt_gw")
        gb = S.sbuf(st, [128, 4096], BF16, "rt_gb")
        sg = S.sbuf(st, [128, 4096], BF16, "rt_sg")
        ynb = S.sbuf(st, [128, 4096], BF16, "rt_ynb")
        ynT = S.sbuf(st, [128, 32, 128], BF16, "rt_ynT")
        S.dma("sp", dmT[:], dmatT_d, dmT, writes=[dmT])
        S.dma("sp", qdec[:], qdec_d, qdec, writes=[qdec])
        S.dma("sp", kdec[:], kdec_d, kdec, writes=[kdec])
        S.dma("pool", gw[:], bcast_rows(gnw, 4096), gw, writes=[gw])
        S.dma("pool", gb[:], bcast_rows(gnb, 4096), gb, writes=[gb])
        qv = qkT_d.rearrange("(c p) t -> p c t", p=128)
        ynv = ynT_d.rearrange("(c p) t -> p c t", p=128)
        nch = LSEQ // 128
        ntot = (NT // LSEQ) * nch

        def load(i):
            sq, c = divmod(i, nch)
            g0 = sq * LSEQ + c * 128
            S.dma("sp", qk[i % 2][:], qv[:, :, g0:g0 + 128], qk[i % 2], writes=[qk[i % 2]])
            S.dma("sp", vg[i % 2][:], vg_d[g0:g0 + 128, :], vg[i % 2], writes=[vg[i % 2]])

        def heads(i):
            sq, c = divmod(i, nch)
            Q = qk[i % 2]
            V = vg[i % 2]
            yh = yhs[i % 2]
            if c == 0:
                S.op("pool", lambda en: en.memset(R[:], 0.0), [], [R])
                S.op("pool", lambda en: en.memset(Rb[:], 0.0), [], [Rb])
            for h in range(8):
                k = h % 2
                bs = S.bank()
                for half in range(2):
                    S.op("pe", lambda en: en.matmul(bs[:, 0:128], lhsT=Q[:, 16 + 2 * h + half, :], rhs=Q[:, 2 * h + half, :],
                                                    start=(half == 0), stop=(half == 1)), reads=[Q], writes=[bs], sig=(half == 1))
                tt(S, "dve", ST[k][:], bs[:, 0:128], dmT[:, h, :], ALU.mult, [bs, dmT], [ST[k]])
                tt(S, "pool", qd[k][:], Q[:, 2 * h:2 * h + 2, :], qdec[:, h, :].unsqueeze(1).to_broadcast([128, 2, 128]), ALU.mult,
                   [Q, qdec], [qd[k]])
                by = S.bank()
                S.op("pe", lambda en: en.matmul(by[:], lhsT=ST[k][:], rhs=V[:, h * 512:(h + 1) * 512], start=True, stop=False),
                     reads=[ST[k], V], writes=[by], sig=False)
                for half in range(2):
                    S.op("pe", lambda en: en.matmul(by[:], lhsT=qd[k][:, half, :], rhs=Rb[:, h, half, :], start=False, stop=(half == 1)),
                         reads=[qd[k], Rb], writes=[by], sig=(half == 1))
                S.op("dve", lambda en: en.bn_stats(out=stats[:], in_=by[:]), [by], [stats])
                S.op("dve", lambda en: en.bn_aggr(out=mv[:], in_=stats[:]), [stats], [mv])
                rsqrt_eps(S, rstd[:], mv[:, 1:2], [mv], rstd, 1.0, LN_EPS)
                ts(S, "dve", yh[:, h * 512:(h + 1) * 512], by[:], mv[:, 0:1], rstd[:, 0:1], ALU.subtract, ALU.mult, [by, mv, rstd], [yh])
                bt = S.bank()
                btv = bt.t[:].bitcast(BF16)
                for half in range(2):
                    S.op("pe", lambda en: en.transpose(btv[:, half * 128:(half + 1) * 128], Q[:, 16 + 2 * h + half, :], C.identb[:]),
                         reads=[Q, C.identb], writes=[bt], sig=(half == 1))
                ts(S, "dve", kd[k][:], btv[:, 0:256].rearrange("p (a b) -> p a b", a=2), kdec[:, h:h + 1], None, ALU.mult, None,
                   [bt, kdec], [kd[k]])
                for half in range(2):
                    br = S.bank()
                    S.op("pe", lambda en: en.matmul(br[:], lhsT=kd[k][:, half, :], rhs=V[:, h * 512:(h + 1) * 512], start=True, stop=True),
                         reads=[kd[k], V], writes=[br])
                    S.op("dve", lambda en: en.scalar_tensor_tensor(out=R[:, h, half, :], in0=R[:, h, half, :], scalar=float(cdec[h]),
                                                                   in1=br[:], op0=ALU.mult, op1=ALU.add), [R, br], [R])
                    cp(S, "act", Rb[:, h, half, :], R[:, h, half, :], [R], [Rb])

        def tail(i):
            V = vg[i % 2]
            yh = yhs[i % 2]
            tt(S, "pool", yh[:], yh[:], gw[:], ALU.mult, [yh, gw], [yh])
            tt(S, "dve", yh[:], yh[:], gb[:], ALU.add, [yh, gb], [yh])
            act(S, sg[:], V[:, 4096:8192], AF.Silu, [V], [sg])
            tt(S, "dve", ynb[:], yh[:], sg[:], ALU.mult, [yh, sg], [ynb])

        def out(i):
            sq, c = divmod(i, nch)
            g0 = sq * LSEQ + c * 128
            transposes_to(S, C, ynb, [ynb], 32, lambda j, n: ynT[:, j:j + n, :], [ynT], BF16, evac="act")
            S.dma("sp", ynv[:, :, g0:g0 + 128], ynT[:], ynT, reads=[ynT])

        load(0)
        for i in range(ntot):
            if i + 1 < ntot:
                load(i + 1)
            heads(i)
            if i > 0:
                out(i - 1)
            tail(i)
        out(ntot - 1)
        S.barrier()


def _ret_gamma():
    h = np.arange(8, dtype=np.float64)
    return np.log1p(-np.exp2(-5.0 - h))


RET_CDEC = [float(np.exp(128.0 * lg)) for lg in _ret_gamma()]


def ret_consts():
    lg = _ret_gamma()
    idx = np.arange(128, dtype=np.float64)
    rel = idx[None, :] - idx[:, None]
    dm = np.where(rel[:, None, :] >= 0, np.exp(rel[:, None, :] * lg[None, :, None]), 0.0)
    qdec = np.exp((idx[None, :] + 1.0) * lg[:, None])
    kdec = np.exp((127.0 - idx)[:, None] * lg[None, :])
    invf = 10000.0 ** (-np.arange(0, 256, 2, dtype=np.float32) / np.float32(256))
    return {"dmatT": dm.astype(np.float32), "qdec": np.tile(qdec[None], (128, 1, 1)).astype(np.float32),
            "kdec": kdec.astype(np.float32), "invfreq": invf.astype(np.float32).reshape(128, 1)}


WEIGHT_SPECS = [
    ("ssm_in_proj", [D, 10304]), ("ssm_convwT", [6144, 4]), ("ssm_convb2", [128, 48]), ("ssm_dt_bias", [64]),
    ("ssm_A_log", [64]), ("ssm_D", [64]), ("ssm_norm_w", [4096]), ("ssm_out_proj", [4096, D]),
    ("ret_in_proj", [D, 12288]), ("ret_gn_w", [4096]), ("ret_gn_b", [4096]), ("ret_out_proj", [4096, D]),
    ("mix_ln_w", [2, D]), ("mix_ln_b", [2, D]), ("peer_w_q", [2, D, D]), ("peer_keysT", [2, 16, 128, 128]),
    ("peer_uT", [2, D, 16384]), ("peer_v", [2, 16384, D]), ("ffn_ln_w", [2, D]), ("ffn_ln_b", [2, D]),
    ("identf", [128, 128]), ("iota", [128, 128]), ("negmask", [128, 512]),
    ("dmatT", [128, 8, 128]), ("qdec", [128, 8, 128]), ("kdec", [128, 8]), ("invfreq", [128, 1]),
]


def build_program(NT, LSEQ):
    nc = bass.Bass("TRN2", target_bir_lowering=False)
    NSEQ = NT // LSEQ
    x = nc.dram_tensor("x", [NT, D], F32, kind="ExternalInput").ap()
    pos = nc.dram_tensor("positions", [NSEQ, LSEQ], I32, kind="ExternalInput").ap()
    w = {}
    for name, shp in WEIGHT_SPECS:
        w[name] = nc.dram_tensor(name, list(shp), F32, kind="ExternalInput").ap()
    out = nc.dram_tensor("out", [NT, D], F32, kind="ExternalOutput").ap()
    S = Sched(nc)
    C = Ctx()
    with contextlib.ExitStack() as st:
        S.init_banks(st)
        setup_consts(S, C, st, w["identf"], w["iota"])
        xT0 = S.hbm("xT0", [D, NT], BF16)
        xbcT = S.hbm("xbcT", [6144, NT], BF16)
        z_d = S.hbm("z", [NT, 4096], BF16)
        dtT = S.hbm("dtT", [64, NT], F32)
        dAT = S.hbm("dAT", [64, NT], F32)
        ynT = S.hbm("ynT", [4096, NT], BF16)
        x1 = S.hbm("x1", [NT, D], F32)
        x1T = S.hbm("x1T", [D, NT], BF16)
        Gh = S.hbm("Gh", [128, 128, NT], BF16)
        x2 = S.hbm("x2", [NT, D], F32)
        x2T = S.hbm("x2T", [D, NT], BF16)
        qkT = S.hbm("qkT", [4096, NT], BF16)
        vg = S.hbm("vg", [NT, 8192], BF16)
        x3 = S.hbm("x3", [NT, D], F32)
        x3T = S.hbm("x3T", [D, NT], BF16)
        phase_make_xT(S, C, x, xT0, NT)
        phase_ssm_inproj(S, C, xT0, w["ssm_in_proj"], w["ssm_convwT"], w["ssm_convb2"], w["ssm_dt_bias"], w["ssm_A_log"],
                         xbcT, z_d, dtT, dAT, NT, LSEQ)
        phase_ssd(S, C, xbcT, z_d, dtT, dAT, w["ssm_D"], w["ssm_norm_w"], w["negmask"], ynT, NT, LSEQ)
        phase_proj_ln(S, C, ynT, 4096, w["ssm_out_proj"], x, x1, x1T, w["mix_ln_w"][0], w["mix_ln_b"][0], NT)
        phase_peer_route(S, C, x1T, w["peer_w_q"][0], w["peer_keysT"][0], Gh, NT)
        phase_peer_dense(S, C, x1T, w["peer_uT"][0], w["peer_v"][0], Gh, x1, x2, x2T, w["ffn_ln_w"][0], w["ffn_ln_b"][0], NT)
        phase_ret_inproj(S, C, x2T, w["ret_in_proj"], pos, w["invfreq"], qkT, vg, NT, LSEQ)
        phase_ret(S, C, qkT, vg, w["dmatT"], w["qdec"], w["kdec"], RET_CDEC, w["ret_gn_w"], w["ret_gn_b"], ynT, NT, LSEQ)
        phase_proj_ln(S, C, ynT, 4096, w["ret_out_proj"], x2, x3, x3T, w["mix_ln_w"][1], w["mix_ln_b"][1], NT)
        phase_peer_route(S, C, x3T, w["peer_w_q"][1], w["peer_keysT"][1], Gh, NT)
        phase_peer_dense(S, C, x3T, w["peer_uT"][1], w["peer_v"][1], Gh, x3, out, None, w["ffn_ln_w"][1], w["ffn_ln_b"][1], NT)
        S.finish()
    return nc


def host_weights(inp):
    f = lambda a: np.ascontiguousarray(np.asarray(a), dtype=np.float32)
    m = {
        "ssm_in_proj": f(inp["ssm_in_proj"][0]), "ssm_convwT": f(np.asarray(inp["ssm_conv_w"][0]).T),
        "ssm_convb2": f(np.asarray(inp["ssm_conv_b"][0]).reshape(48, 128).T), "ssm_dt_bias": f(inp["ssm_dt_bias"][0]),
        "ssm_A_log": f(inp["ssm_A_log"][0]), "ssm_D": f(inp["ssm_D"][0]), "ssm_norm_w": f(inp["ssm_norm_w"][0]),
        "ssm_out_proj": f(inp["ssm_out_proj"][0]), "ret_in_proj": f(inp["ret_in_proj"][0]), "ret_gn_w": f(inp["ret_gn_w"][0]),
        "ret_gn_b": f(inp["ret_gn_b"][0]), "ret_out_proj": f(inp["ret_out_proj"][0]), "mix_ln_w": f(inp["mix_ln_w"]),
        "mix_ln_b": f(inp["mix_ln_b"]), "peer_w_q": f(inp["peer_w_q"]),
        "peer_keysT": f(np.asarray(inp["peer_sub_keys"]).reshape(2, 16, 128, 128).transpose(0, 1, 3, 2)),
        "peer_uT": f(np.asarray(inp["peer_u"]).transpose(0, 2, 1)), "peer_v": f(inp["peer_v"]),
        "ffn_ln_w": f(inp["ffn_ln_w"]), "ffn_ln_b": f(inp["ffn_ln_b"]),
        "identf": np.eye(128, dtype=np.float32), "iota": np.tile(np.arange(128, dtype=np.float32), (128, 1)),
        "negmask": np.tile(np.where(np.arange(128)[:, None] > np.arange(128)[None, :], NEG, 0.0).astype(np.float32), (1, 4)),
    }
    m.update(ret_consts())
    return m


def kernel(**inputs):
    x = np.asarray(inputs["x"], dtype=np.float32)
    positions = np.asarray(inputs["positions"], dtype=np.int32)
    B, L, _ = x.shape
    ncores = 8 if B % 8 == 0 else 1
    per = B // ncores
    NT = per * L
    nc = build_program(NT, L)
    wts = host_weights(inputs)
    in_maps = []
    for c in range(ncores):
        m = dict(wts)
        m["x"] = np.ascontiguousarray(x[c * per:(c + 1) * per].reshape(NT, D))
        m["positions"] = np.ascontiguousarray(positions[c * per:(c + 1) * per])
        in_maps.append(m)
    res = run_bass_kernel_spmd(nc, in_maps, core_ids=list(range(ncores)))
    outs = [np.asarray(r["out"]).reshape(per, L, D) for r in res.results]
    return np.concatenate(outs, axis=0).astype(np.float32)
```

```python
import contextlib
import numpy as np
import concourse.bass as bass
import concourse.mybir as mybir
from concourse.bass_utils import run_bass_kernel_spmd

F32 = mybir.dt.float32
BF16 = mybir.dt.bfloat16
I32 = mybir.dt.int32
U32 = mybir.dt.uint32
AF = mybir.ActivationFunctionType
ALU = mybir.AluOpType
AX = mybir.AxisListType

D = 2048
DN_ALPHA = 4 ** 0.25
LN_EPS = 1e-5
NEG = -30000.0


class Buf:
    __slots__ = ("t", "lw", "rd", "dsem", "dval", "name")

    def __init__(self, t, name=""):
        self.t = t
        self.lw = None
        self.rd = []
        self.dsem = None
        self.dval = 0
        self.name = name

    def __getitem__(self, k):
        return self.t[k]


class Sched:
    ENGS = ("pe", "act", "dve", "pool", "sp")

    def __init__(self, nc):
        self.nc = nc
        self.es = contextlib.ExitStack()
        self.eng = {"pe": nc.tensor, "act": nc.scalar, "dve": nc.vector, "pool": nc.gpsimd, "sp": nc.sync}
        self.sem = {}
        self.cnt = {}
        for e in ("pe", "act", "dve", "pool"):
            self.sem[e] = self.es.enter_context(nc.semaphore("s_" + e))
            self.cnt[e] = 0
        self.waited = {e: {} for e in self.ENGS}
        self.dma_bufs = []
        self.free_dsems = []
        self.nbuf = 0
        self.banks = None
        self.bank_i = 0
        self.bank_pool = (0, 8)

    def sbuf(self, stack, shape, dt, name=None):
        self.nbuf += 1
        name = name or "b"
        t = stack.enter_context(self.nc.sbuf_tensor(f"{name}_{self.nbuf}", list(shape), dt))
        return Buf(t, name)

    def hbm(self, name, shape, dt):
        self.nbuf += 1
        return self.nc.dram_tensor(f"{name}_{self.nbuf}", list(shape), dt).ap()

    def init_banks(self, stack):
        self.banks = []
        for i in range(8):
            t = stack.enter_context(self.nc.psum_tensor(f"bank{i}", [128, 512], F32))
            self.banks.append(Buf(t, f"bank{i}"))

    def bank(self):
        lo, n = self.bank_pool
        b = self.banks[lo + self.bank_i % n]
        self.bank_i += 1
        return b

    def bank_private(self, lo, n):
        self.bank_j = getattr(self, "bank_j", 0) + 1
        return self.banks[lo + self.bank_j % n]

    def _dsem(self, b):
        if b.dsem is None:
            if self.free_dsems:
                b.dsem, b.dval = self.free_dsems.pop()
            else:
                self.nsem = getattr(self, "nsem", 0) + 1
                b.dsem = self.es.enter_context(self.nc.semaphore(f"d{self.nsem}"))
            self.dma_bufs.append(b)
        return b.dsem

    def release_dma_sems(self):
        for b in self.dma_bufs:
            self.free_dsems.append((b.dsem, b.dval))
            b.dsem = None
            b.dval = 0
            b.lw = None if (b.lw is not None and b.lw[2] == "dma") else b.lw
            b.rd = [r for r in b.rd if r[2] != "dma"]
        self.dma_bufs = []

    def _wait(self, e, tok):
        if tok is None:
            return
        sem, val = tok[0], tok[1]
        w = self.waited[e]
        k = id(sem)
        if w.get(k, 0) >= val:
            return
        self.eng[e].wait_ge(sem, val)
        w[k] = val

    def _deps(self, e, reads, writes):
        for b in reads:
            self._wait(e, b.lw)
        for b in writes:
            if b.lw is not None and b.lw[2] != e:
                self._wait(e, b.lw)
            for r in b.rd:
                if r[2] != e:
                    self._wait(e, r)

    def _commit(self, tok, reads, writes):
        for b in reads:
            b.rd.append(tok)
            if len(b.rd) > 16:
                best = {}
                for r in b.rd:
                    k = id(r[0])
                    if k not in best or best[k][1] < r[1]:
                        best[k] = r
                b.rd = list(best.values())
        for b in writes:
            b.lw = tok
            b.rd = []

    def op(self, e, fn, reads=(), writes=(), sig=True):
        self._deps(e, reads, writes)
        ins = fn(self.eng[e])
        if sig:
            self.cnt[e] += 1
            ins.then_inc(self.sem[e], 1)
            tok = (self.sem[e], self.cnt[e], e)
        else:
            tok = (self.sem[e], self.cnt[e] + 1, e)
        self._commit(tok, reads, writes)
        return tok

    def dma(self, q, out, in_, sb, reads=(), writes=(), **kw):
        sem = self._dsem(sb)
        self._deps(q, reads, writes)
        if sb.dval:
            self._wait(q, (sem, sb.dval, "dma"))
        ins = self.eng[q].dma_start(out=out, in_=in_, **kw)
        sb.dval += 16
        ins.then_inc(sem, 16)
        tok = (sem, sb.dval, "dma")
        self._commit(tok, reads, writes)
        return tok

    def barrier(self):
        toks = []
        for e in ("pe", "act", "dve", "pool"):
            if self.cnt[e]:
                toks.append((self.sem[e], self.cnt[e], e))
        for b in self.dma_bufs:
            if b.dval:
                toks.append((b.dsem, b.dval, "dma"))
        for e in self.ENGS:
            for t in toks:
                if t[2] != e:
                    self._wait(e, t)
        self.release_dma_sems()

    def finish(self):
        self.barrier()
        self.es.close()


def cp(S, e, out, in_, reads, writes):
    if e == "act":
        return S.op("act", lambda en: en.copy(out=out, in_=in_), reads, writes)
    return S.op(e, lambda en: en.tensor_copy(out=out, in_=in_), reads, writes)


def tt(S, e, out, a, b, op, reads, writes):
    return S.op(e, lambda en: en.tensor_tensor(out=out, in0=a, in1=b, op=op), reads, writes)


def ts(S, e, out, a, s1, s2, op0, op1, reads, writes):
    if op1 is None:
        return S.op(e, lambda en: en.tensor_scalar(out=out, in0=a, scalar1=s1, scalar2=None, op0=op0), reads, writes)
    return S.op(e, lambda en: en.tensor_scalar(out=out, in0=a, scalar1=s1, scalar2=s2, op0=op0, op1=op1), reads, writes)


def act(S, out, in_, func, reads, writes, **kw):
    return S.op("act", lambda en: en.activation(out=out, in_=in_, func=func, **kw), reads, writes)


def bcast_rows(ap1d, n, parts=128):
    return ap1d.rearrange("(o n) -> o n", o=1).to_broadcast([parts, n])


class Ctx:
    pass


def rsqrt_eps(S, out, in_, in_bufs, out_buf, mul, eps):
    ts(S, "dve", out, in_, float(mul), float(eps), ALU.mult, ALU.add, in_bufs, [out_buf])
    act(S, out, out, AF.Sqrt, [out_buf], [out_buf])
    S.op("dve", lambda en: en.reciprocal(out=out, in_=out), [out_buf], [out_buf])


def transposes_to(S, C, src, src_bufs, ncol_chunks, dst_fn, dst_bufs, dt=BF16, evac="dve"):
    per = 8 if dt == BF16 else 4
    ident = C.identb if dt == BF16 else C.identf
    j = 0
    while j < ncol_chunks:
        n = min(per, ncol_chunks - j)
        bk = S.bank()
        bv = bk.t[:].bitcast(BF16) if dt == BF16 else bk.t[:]
        for i in range(n):
            S.op("pe", lambda en: en.transpose(bv[:, i * 128:(i + 1) * 128], src[:, (j + i) * 128:(j + i + 1) * 128], ident[:]),
                 reads=list(src_bufs) + [C.identb if dt == BF16 else C.identf], writes=[bk], sig=(i == n - 1))
        cp(S, evac, dst_fn(j, n), bv[:, 0:n * 128].rearrange("p (a b) -> p a b", a=n), [bk], dst_bufs)
        j += n


def load_wslab(S, slab, W, KC, f0, fw):
    wv = W.rearrange("(kc p) f -> p kc f", p=128)
    S.dma("pool", slab[:, 0:KC, 0:fw], wv[:, :, f0:f0 + fw], slab, writes=[slab])


def gemm_tok(S, C, xT, KC, TG, W, f0, nf, slabs, epi):
    blocks = []
    f = f0
    while f < f0 + nf:
        fw = min(512, f0 + nf - f)
        blocks.append((f, fw))
        f += fw
    load_wslab(S, slabs[0], W, KC, blocks[0][0], blocks[0][1])
    for bi, (f, fw) in enumerate(blocks):
        if bi + 1 < len(blocks):
            load_wslab(S, slabs[(bi + 1) % 2], W, KC, blocks[bi + 1][0], blocks[bi + 1][1])
        sl = slabs[bi % 2]
        for t_ in range(TG // 128):
            bk = S.bank()
            for kc in range(KC):
                S.op("pe", lambda en: en.matmul(bk[:, 0:fw], lhsT=xT[:, kc, t_ * 128:(t_ + 1) * 128], rhs=sl[:, kc, 0:fw],
                                                start=(kc == 0), stop=(kc == KC - 1)),
                     reads=[xT, sl], writes=[bk], sig=(kc == KC - 1))
            epi(bi, t_, bk, fw)


def gemm_feat(S, C, xT, KC, TG, W, f0, nf, slabs, epi, tb_w=512, fbw=512):
    blocks = []
    f = f0
    while f < f0 + nf:
        fw = min(fbw, f0 + nf - f)
        blocks.append((f, fw))
        f += fw
    load_wslab(S, slabs[0], W, KC, blocks[0][0], blocks[0][1])
    ci = 0
    for bi, (f, fw) in enumerate(blocks):
        if bi + 1 < len(blocks):
            load_wslab(S, slabs[(bi + 1) % 2], W, KC, blocks[bi + 1][0], blocks[bi + 1][1])
        sl = slabs[bi % 2]
        c0 = 0
        while c0 < fw:
            rows = min(128, fw - c0)
            for tb in range((TG + tb_w - 1) // tb_w):
                tw = min(tb_w, TG - tb * tb_w)
                bk = S.bank()
                for kc in range(KC):
                    S.op("pe", lambda en: en.matmul(bk[0:rows, 0:tw], lhsT=sl[:, kc, c0:c0 + rows],
                                                    rhs=xT[:, kc, tb * tb_w:tb * tb_w + tw],
                                                    start=(kc == 0), stop=(kc == KC - 1)),
                         reads=[xT, sl], writes=[bk], sig=(kc == KC - 1))
                epi(ci, tb, bk, rows, tw)
            ci += 1
            c0 += rows


class LNBufs:
    def __init__(self, S, st, lnw, lnb, nb=2):
        self.nb = nb
        self.xo = [S.sbuf(st, [128, D], F32, "ln_xo") for _ in range(nb)]
        self.r = S.sbuf(st, [128, D], F32, "ln_r")
        self.o = [S.sbuf(st, [128, D], F32, "ln_o") for _ in range(nb)]
        self.ob = S.sbuf(st, [128, D], BF16, "ln_ob")
        self.oT = [S.sbuf(st, [128, 16, 128], BF16, "ln_oT") for _ in range(nb)]
        self.stats = S.sbuf(st, [128, 4, 6], F32, "ln_stats")
        self.mv = S.sbuf(st, [128, 2], F32, "ln_mv")
        self.rstd = S.sbuf(st, [128, 1], F32, "ln_rstd")
        self.w = S.sbuf(st, [128, D], F32, "ln_w")
        self.b = S.sbuf(st, [128, D], F32, "ln_b")
        S.dma("sp", self.w[:], bcast_rows(lnw, D), self.w, writes=[self.w])
        S.dma("sp", self.b[:], bcast_rows(lnb, D), self.b, writes=[self.b])
        self.i = 0


def ln_prefetch(S, L, xold, g0):
    xo = L.xo[L.i % L.nb]
    S.dma("sp", xo[:], xold[g0:g0 + 128, :], xo, writes=[xo])


def ln_epilogue(S, C, L, h_ap, h_bufs, g0, xnew, xnewT):
    i = L.i
    L.i += 1
    xo = L.xo[i % L.nb]
    o = L.o[i % L.nb]
    oT = L.oT[i % L.nb]
    S.op("dve", lambda en: en.scalar_tensor_tensor(out=L.r[:], in0=xo[:], scalar=float(DN_ALPHA), in1=h_ap,
                                                   op0=ALU.mult, op1=ALU.add), reads=[xo] + list(h_bufs), writes=[L.r])
    for q in range(4):
        S.op("dve", lambda en: en.bn_stats(out=L.stats[:, q, :], in_=L.r[:, q * 512:(q + 1) * 512]), reads=[L.r], writes=[L.stats])
    S.op("dve", lambda en: en.bn_aggr(out=L.mv[:], in_=L.stats[:].rearrange("p a b -> p (a b)")), reads=[L.stats], writes=[L.mv])
    rsqrt_eps(S, L.rstd[:], L.mv[:, 1:2], [L.mv], L.rstd, 1.0, LN_EPS)
    ts(S, "dve", L.r[:], L.r[:], L.mv[:, 0:1], L.rstd[:, 0:1], ALU.subtract, ALU.mult, [L.r, L.mv, L.rstd], [L.r])
    tt(S, "pool", L.r[:], L.r[:], L.w[:], ALU.mult, [L.r, L.w], [L.r])
    tt(S, "pool", o[:], L.r[:], L.b[:], ALU.add, [L.r, L.b], [o])
    S.dma("sp", xnew[g0:g0 + 128, :], o[:], o, reads=[o])
    if xnewT is not None:
        cp(S, "act", L.ob[:], o[:], [o], [L.ob])
        transposes_to(S, C, L.ob, [L.ob], 16, lambda j, n: oT[:, j:j + n, :], [oT], BF16, evac="act")
        S.dma("sp", xnewT.rearrange("(kc p) t -> p kc t", p=128)[:, :, g0:g0 + 128], oT[:], oT, reads=[oT])


def phase_proj_ln(S, C, srcT, K, W, xold, xnew, xnewT, lnw, lnb, NT):
    KC = K // 128
    TG = min(1024, NT)
    nt = TG // 128
    FB = 256
    with contextlib.ExitStack() as st:
        xTt = [S.sbuf(st, [128, KC, 128], BF16, "pl_xT") for _ in range(nt)]
        slabs = [S.sbuf(st, [128, KC, FB], BF16, "pl_slab") for _ in range(3)]
        hst = [S.sbuf(st, [128, D], BF16, "pl_h") for _ in range(nt)]
        L = LNBufs(S, st, lnw, lnb, nb=1)
        srcv = srcT.rearrange("(kc p) t -> p kc t", p=128)
        ng = NT // TG
        nblk = D // FB
        total_blk = ng * nblk
        load_wslab(S, slabs[0], W, KC, 0, FB)
        if total_blk > 1:
            load_wslab(S, slabs[1], W, KC, (1 % nblk) * FB, FB)
        nslab = 0
        pending = []
        for g in range(ng):
            for t_ in range(nt):
                S.dma("sp", xTt[t_][:], srcv[:, :, g * TG + t_ * 128:g * TG + (t_ + 1) * 128], xTt[t_], writes=[xTt[t_]])
            for bi in range(nblk):
                if nslab + 2 < total_blk:
                    load_wslab(S, slabs[(nslab + 2) % 3], W, KC, ((bi + 2) % nblk) * FB, FB)
                sl = slabs[nslab % 3]
                nslab += 1
                for t_ in range(nt):
                    bk = S.bank()
                    for kc in range(KC):
                        S.op("pe", lambda en: en.matmul(bk[:, 0:FB], lhsT=xTt[t_][:, kc, :], rhs=sl[:, kc, :],
                                                        start=(kc == 0), stop=(kc == KC - 1)),
                             reads=[xTt[t_], sl], writes=[bk], sig=(kc == KC - 1))
                    if pending:
                        pending.pop(0)()
                    cp(S, "act", hst[t_][:, bi * FB:(bi + 1) * FB], bk[:, 0:FB], [bk], [hst[t_]])
                    if bi == nblk - 1:
                        def ep(g=g, t_=t_):
                            ln_prefetch(S, L, xold, g * TG + t_ * 128)
                            ln_epilogue(S, C, L, hst[t_][:], [hst[t_]], g * TG + t_ * 128, xnew, xnewT)
                        pending.append(ep)
        for ep in pending:
            ep()
        S.barrier()


def phase_make_xT(S, C, x, xT_d, NT):
    with contextlib.ExitStack() as st:
        xin = [S.sbuf(st, [128, D], F32, "mx_in") for _ in range(2)]
        xb = [S.sbuf(st, [128, D], BF16, "mx_b") for _ in range(2)]
        oT = [S.sbuf(st, [128, 16, 128], BF16, "mx_oT") for _ in range(2)]
        dv = xT_d.rearrange("(kc p) t -> p kc t", p=128)
        n = NT // 128
        S.dma("sp", xin[0][:], x[0:128, :], xin[0], writes=[xin[0]])
        for i in range(n):
            if i + 1 < n:
                S.dma("sp", xin[(i + 1) % 2][:], x[(i + 1) * 128:(i + 2) * 128, :], xin[(i + 1) % 2], writes=[xin[(i + 1) % 2]])
            cp(S, "act", xb[i % 2][:], xin[i % 2][:], [xin[i % 2]], [xb[i % 2]])
            o = oT[i % 2]
            transposes_to(S, C, xb[i % 2], [xb[i % 2]], 16, lambda j, n_: o[:, j:j + n_, :], [o], BF16)
            S.dma("sp", dv[:, :, i * 128:(i + 1) * 128], o[:], o, reads=[o])
        S.barrier()


def phase_peer_route(S, C, xT_d, wq, keysT_d, Gh, NT):
    TG = min(512, NT)
    ntile = NT // 128
    with contextlib.ExitStack() as st:
        xT = S.sbuf(st, [128, 16, TG], BF16, "rt_xT")
        qT = S.sbuf(st, [128, 16, TG], BF16, "rt_qT")
        slabs = [S.sbuf(st, [128, 16, 256], BF16, "rt_slab") for _ in range(2)]
        keys = S.sbuf(st, [128, 16, 128], BF16, "rt_keys")
        S.dma("pool", keys[:], keysT_d.rearrange("hc d k -> d hc k"), keys, writes=[keys])

        class TS:
            pass
        tsb = []
        for p in range(2):
            T = TS()
            S.nbuf += 1
            scr = st.enter_context(S.nc.sbuf_tensor(f"rt_scr{p}_{S.nbuf}", [128, 3, 2048], F32))
            T.sc = Buf(scr[:, 0, :].rearrange("p (a b) -> p a b", a=16), "sc")
            T.cand2 = Buf(scr[:, 0, :].rearrange("p (a b) -> p a b", a=8), "cand2")
            T.sc2 = Buf(scr[:, 1, :].rearrange("p (a b) -> p a b", a=16), "sc2")
            T.eq = Buf(scr[:, 1, :].rearrange("p (h k j) -> p h k j", h=8, k=16), "eq")
            T.cand = Buf(scr[:, 2, :].rearrange("p (a b) -> p a b", a=8), "cand")
            T.vals = S.sbuf(st, [128, 16, 16], F32, "rt_vals")
            T.idx = S.sbuf(st, [128, 16, 16], U32, "rt_idx")
            T.idxf = S.sbuf(st, [128, 16, 16], F32, "rt_idxf")
            T.best = S.sbuf(st, [128, 8, 16], F32, "rt_best")
            T.pos = S.sbuf(st, [128, 8, 16], U32, "rt_pos")
            T.pos2 = S.sbuf(st, [128, 8, 16], U32, "rt_pos2")
            T.k1 = S.sbuf(st, [128, 8, 16], F32, "rt_k1")
            T.k2 = S.sbuf(st, [128, 8, 16], F32, "rt_k2")
            T.sel = S.sbuf(st, [128, 3, 128], F32, "rt_sel")
            T.selT = S.sbuf(st, [128, 3, 128], F32, "rt_selT")
            T.nidx7 = S.sbuf(st, [128, 128], F32, "rt_nidx7")
            T.nb = S.sbuf(st, [128, 8], F32, "rt_nb")
            T.Z = S.sbuf(st, [128, 8], F32, "rt_Z")
            T.ex = S.sbuf(st, [128, 8, 16], F32, "rt_ex")
            tsb.append(T)
        NOH = 8
        oh1 = [S.sbuf(st, [128, 128], BF16, "rt_oh1") for _ in range(NOH)]
        ohraw = [S.sbuf(st, [128, 16, 128], BF16, "rt_ohraw") for _ in range(2)]
        oh2b = [S.sbuf(st, [128, 16, 128], BF16, "rt_oh2b") for _ in range(3)]
        Gsb = [S.sbuf(st, [128, 128, 128], BF16, "rt_G") for _ in range(2)]
        xv = xT_d.rearrange("(kc p) t -> p kc t", p=128)

        def gemm_group(g):
            S.dma("sp", xT[:], xv[:, :, g * TG:(g + 1) * TG], xT, writes=[xT])

            def epi(ci, tb, bk, rows, tw):
                cp(S, "act", qT[:, ci, tb * 512:tb * 512 + tw], bk[:, 0:tw], [bk], [qT])
            gemm_feat(S, C, xT, 16, TG, wq, 0, D, slabs, epi, fbw=256)

        def topk_thunks(gi):
            T = tsb[gi % 2]
            g, t_ = divmod(gi, TG // 128)
            tsl = slice(t_ * 128, (t_ + 1) * 128)
            th = []
            if t_ == 0:
                th.append(lambda: gemm_group(g))

            def scores(q4):
                bk = S.bank()
                for j in range(4):
                    hc = q4 * 4 + j
                    S.op("pe", lambda en: en.matmul(bk[:, j * 128:(j + 1) * 128], lhsT=qT[:, hc, tsl], rhs=keys[:, hc, :],
                                                    start=True, stop=True), reads=[qT, keys], writes=[bk], sig=(j == 3))
                cp(S, "act", T.sc[:, q4 * 4:(q4 + 1) * 4, :], bk[:].rearrange("p (a b) -> p a b", a=4), [bk], [T.sc, T.cand2])
            for q4 in range(4):
                th.append(lambda q4=q4: scores(q4))
            for hc in range(16):
                th.append(lambda hc=hc: S.op("dve", lambda en: en.max(out=T.vals[:, hc, 0:8], in_=T.sc[:, hc, :]), [T.sc], [T.vals]))
            for hc in range(16):
                th.append(lambda hc=hc: S.op("dve", lambda en: en.max_index(out=T.idx[:, hc, 0:8], in_max=T.vals[:, hc, 0:8],
                                                                           in_values=T.sc[:, hc, :]), [T.sc, T.vals], [T.idx]))
            for hc in range(16):
                th.append(lambda hc=hc: S.op("dve", lambda en: en.match_replace(out=T.sc2[:, hc, :], in_to_replace=T.vals[:, hc, 0:8],
                                                                               in_values=T.sc[:, hc, :], imm_value=-1e30),
                                             [T.sc, T.vals], [T.sc2]))
            for hc in range(16):
                th.append(lambda hc=hc: S.op("dve", lambda en: en.max(out=T.vals[:, hc, 8:16], in_=T.sc2[:, hc, :]), [T.sc2], [T.vals]))
            for hc in range(16):
                th.append(lambda hc=hc: S.op("dve", lambda en: en.max_index(out=T.idx[:, hc, 8:16], in_max=T.vals[:, hc, 8:16],
                                                                           in_values=T.sc2[:, hc, :]), [T.sc2, T.vals], [T.idx]))
            v4 = T.vals[:].rearrange("p (h c) k -> p h c k", c=2)
            i4 = T.idxf[:].rearrange("p (h c) k -> p h c k", c=2)
            th.append(lambda: cp(S, "dve", T.idxf[:], T.idx[:], [T.idx], [T.idxf]))
            th.append(lambda: tt(S, "pool", T.cand[:].rearrange("p h (a b) -> p h a b", a=16),
                                 v4[:, :, 0, :].unsqueeze(3).to_broadcast([128, 8, 16, 16]),
                                 v4[:, :, 1, :].unsqueeze(2).to_broadcast([128, 8, 16, 16]), ALU.add, [T.vals], [T.cand]))
            for h in range(8):
                th.append(lambda h=h: S.op("dve", lambda en: en.max(out=T.best[:, h, 0:8], in_=T.cand[:, h, :]), [T.cand], [T.best]))
            for h in range(8):
                th.append(lambda h=h: S.op("dve", lambda en: en.max_index(out=T.pos[:, h, 0:8], in_max=T.best[:, h, 0:8],
                                                                         in_values=T.cand[:, h, :]), [T.cand, T.best], [T.pos]))
            for h in range(8):
                th.append(lambda h=h: S.op("dve", lambda en: en.match_replace(out=T.cand2[:, h, :], in_to_replace=T.best[:, h, 0:8],
                                                                             in_values=T.cand[:, h, :], imm_value=-1e30),
                                           [T.cand, T.best], [T.cand2, T.sc]))
            for h in range(8):
                th.append(lambda h=h: S.op("dve", lambda en: en.max(out=T.best[:, h, 8:16], in_=T.cand2[:, h, :]), [T.cand2], [T.best]))
            for h in range(8):
                th.append(lambda h=h: S.op("dve", lambda en: en.max_index(out=T.pos[:, h, 8:16], in_max=T.best[:, h, 8:16],
                                                                         in_values=T.cand2[:, h, :]), [T.cand2, T.best], [T.pos]))
            th.append(lambda: ts(S, "dve", T.pos2[:], T.pos[:], 15, None, ALU.bitwise_and, None, [T.pos], [T.pos2]))
            th.append(lambda: cp(S, "dve", T.k2[:], T.pos2[:], [T.pos2], [T.k2]))
            th.append(lambda: ts(S, "dve", T.pos2[:], T.pos[:], 4, None, ALU.logical_shift_right, None, [T.pos], [T.pos2]))
            th.append(lambda: cp(S, "dve", T.k1[:], T.pos2[:], [T.pos2], [T.k1]))
            for which, kk in ((0, T.k1), (1, T.k2)):
                th.append(lambda kk=kk: tt(S, "dve", T.eq[:], kk[:].unsqueeze(3).to_broadcast([128, 8, 16, 16]),
                                           C.iota16[:].unsqueeze(1).unsqueeze(1).to_broadcast([128, 8, 16, 16]), ALU.is_equal,
                                           [kk, C.iota16], [T.eq, T.sc2]))
                th.append(lambda which=which: tt(S, "pool", T.eq[:], T.eq[:], i4[:, :, which, :].unsqueeze(2).to_broadcast([128, 8, 16, 16]),
                                                 ALU.mult, [T.eq, T.idxf], [T.eq]))
                th.append(lambda which=which: S.op("dve", lambda en: en.tensor_reduce(out=T.sel[:, which, :],
                                                                                     in_=T.eq[:].rearrange("p h k j -> p (h k) j"),
                                                                                     axis=AX.X, op=ALU.add), [T.eq], [T.sel]))
            th.append(lambda: ts(S, "dve", T.nb[:], T.best[:, :, 0], -1.0, None, ALU.mult, None, [T.best], [T.nb]))
            def exps():
                for h in range(8):
                    act(S, T.ex[:, h, :], T.best[:, h, :], AF.Exp, [T.best, T.nb], [T.ex, T.Z], bias=T.nb[:, h:h + 1], accum_out=T.Z[:, h:h + 1])
            th.append(exps)
            th.append(lambda: S.op("dve", lambda en: en.reciprocal(out=T.Z[:], in_=T.Z[:]), [T.Z], [T.Z]))
            th.append(lambda: ts(S, "dve", T.Z[:], T.Z[:], 0.886226925452758, None, ALU.mult, None, [T.Z], [T.Z]))
            th.append(lambda: tt(S, "dve", T.sel[:, 2, :].rearrange("p (h k) -> p h k", h=8), T.ex[:],
                                 T.Z[:].unsqueeze(2).to_broadcast([128, 8, 16]), ALU.mult, [T.ex, T.Z], [T.sel]))

            def tr():
                bk = S.bank()
                for j in range(3):
                    S.op("pe", lambda en: en.transpose(bk[:, j * 128:(j + 1) * 128], T.sel[:, j, :], C.identf[:]),
                         reads=[T.sel, C.identf], writes=[bk], sig=(j == 2))
                cp(S, "act", T.selT[:], bk[:, 0:384].rearrange("p (a b) -> p a b", a=3), [bk], [T.selT])
                ts(S, "dve", T.nidx7[:], T.selT[:, 0, :], -7.0, None, ALU.mult, None, [T.selT], [T.nidx7])
            th.append(tr)
            return th

        def pertoken_thunks(gi):
            T = tsb[gi % 2]
            Gs = Gsb[gi % 2]
            pend = []
            th = []

            def evac(pb, p4):
                cp(S, "act" if p4 % 2 == 0 else "dve", Gs[:, :, p4 * 4:(p4 + 1) * 4], pb[:].rearrange("p (i t) -> p i t", t=4), [pb], [Gs])

            def batch(t16):
                raw = ohraw[t16 % 2]
                ob = oh2b[t16 % 3]
                tt(S, "dve", raw[:], C.iota[:].unsqueeze(1).to_broadcast([128, 16, 128]),
                   T.selT[:, 1, t16 * 16:(t16 + 1) * 16].unsqueeze(2).to_broadcast([128, 16, 128]), ALU.is_equal, [C.iota, T.selT], [raw])
                tt(S, "pool", ob[:], raw[:], T.selT[:, 2, t16 * 16:(t16 + 1) * 16].unsqueeze(2).to_broadcast([128, 16, 128]), ALU.mult,
                   [raw, T.selT], [ob])

            def grp(t4):
                if t4 == 0:
                    batch(0)
                    batch(1)
                if t4 % 4 == 0 and t4 // 4 + 2 < 8:
                    batch(t4 // 4 + 2)
                ob = oh2b[(t4 // 4) % 3]
                bk = S.bank_private(0, 4)
                for j in range(4):
                    tk = t4 * 4 + j
                    o1 = oh1[tk % NOH]
                    act(S, o1[:], C.iota[:], AF.Derivative_Erf, [C.iota, T.nidx7], [o1], scale=7.0, bias=T.nidx7[:, tk:tk + 1])
                    S.op("pe", lambda en: en.matmul(bk[:].rearrange("p (i j) -> p j i", j=4)[:, j, :], lhsT=ob[:, tk % 16, :], rhs=o1[:],
                                                    start=True, stop=True), reads=[o1, ob], writes=[bk], sig=(j == 3))
                pend.append((bk, t4))
                if len(pend) > 2:
                    evac(*pend.pop(0))
            for t4 in range(32):
                th.append(lambda t4=t4: grp(t4))

            def fin():
                for pb, p4 in pend:
                    evac(pb, p4)
                for q in range(2):
                    S.dma("sp", Gh[gi, :, q * 64:(q + 1) * 64, :], Gs[:, q * 64:(q + 1) * 64, :], Gs, reads=[Gs])
            th.append(fin)
            return th

        S.bank_pool = (4, 4)
        for f in topk_thunks(0):
            f()
        for gi in range(ntile):
            A = pertoken_thunks(gi)
            B = topk_thunks(gi + 1) if gi + 1 < ntile else []
            ia = ib = 0
            ratio = (len(B) + len(A) - 1) // len(A) if B else 0
            while ia < len(A) or ib < len(B):
                if ia < len(A):
                    A[ia]()
                    ia += 1
                for _ in range(ratio):
                    if ib < len(B):
                        B[ib]()
                        ib += 1
                if ia >= len(A):
                    while ib < len(B):
                        B[ib]()
                        ib += 1
        S.bank_pool = (0, 8)
        S.barrier()


def phase_peer_dense(S, C, xT_d, uT_d, v_d, Gh, xold, xnew, xnewT, lnw, lnb, NT):
    TG = min(1024, NT)
    EG = 4
    NEG_ = 128 // EG
    ntile = TG // 128
    with contextlib.ExitStack() as st:
        xT = S.sbuf(st, [128, 16, TG], BF16, "pd_xT")
        acc = S.sbuf(st, [128, ntile, D], F32, "pd_acc")
        gl = [S.sbuf(st, [128, 512], BF16, "pd_gl") for _ in range(2)]
        xv = xT_d.rearrange("(kc p) t -> p kc t", p=128)
        uv = uT_d.rearrange("(kc p) e -> p kc e", p=128)
        vv = v_d.rearrange("(a p) d -> p a d", p=128)
        ngl = 0
        for g in range(NT // TG):
            S.dma("sp", xT[:], xv[:, :, g * TG:(g + 1) * TG], xT, writes=[xT])
            with contextlib.ExitStack() as st2:
                us = [S.sbuf(st2, [128, 16, EG * 128], BF16, "pd_u") for _ in range(2)]
                vs = [S.sbuf(st2, [128, EG, D], BF16, "pd_v") for _ in range(2)]
                Gs = [S.sbuf(st2, [128, ntile, EG, 128], BF16, "pd_G") for _ in range(2)]
                GH = [S.sbuf(st2, [128, EG, TG], BF16, "pd_GH") for _ in range(2)]

                def load(eg):
                    b = eg % 2
                    S.dma("pool", us[b][:], uv[:, :, eg * EG * 128:(eg + 1) * EG * 128], us[b], writes=[us[b]])
                    S.dma("pool", vs[b][:], vv[:, eg * EG:(eg + 1) * EG, :], vs[b], writes=[vs[b]])
                    S.dma("sp", Gs[b][:], Gh[g * ntile:(g + 1) * ntile, :, eg * EG:(eg + 1) * EG, :].rearrange("a p j t -> p a j t"),
                          Gs[b], writes=[Gs[b]])

                load(0)
                for eg in range(NEG_):
                    if eg + 1 < NEG_:
                        load(eg + 1)
                    b = eg % 2
                    for j in range(EG):
                        for tb in range((TG + 511) // 512):
                            tw = min(512, TG - tb * 512)
                            bk = S.bank()
                            for kc in range(16):
                                S.op("pe", lambda en: en.matmul(bk[:, 0:tw], lhsT=us[b][:, kc, j * 128:(j + 1) * 128],
                                                                rhs=xT[:, kc, tb * 512:tb * 512 + tw], start=(kc == 0), stop=(kc == 15)),
                                     reads=[us[b], xT], writes=[bk], sig=(kc == 15))
                            glb = gl[ngl % 2]
                            ngl += 1
                            act(S, glb[:, 0:tw], bk[:, 0:tw], AF.Gelu, [bk], [glb])
                            na = tw // 128
                            tt(S, "pool", GH[b][:, j, tb * 512:tb * 512 + tw].rearrange("p (a t) -> p a t", a=na),
                               glb[:, 0:tw].rearrange("p (a t) -> p a t", a=na), Gs[b][:, tb * 4:tb * 4 + na, j, :], ALU.mult,
                               [glb, Gs[b]], [GH[b]])
                    for t_ in range(ntile):
                        for db in range(4):
                            bk = S.bank()
                            for j in range(EG):
                                S.op("pe", lambda en: en.matmul(bk[:], lhsT=GH[b][:, j, t_ * 128:(t_ + 1) * 128],
                                                                rhs=vs[b][:, j, db * 512:(db + 1) * 512], start=(j == 0), stop=(j == EG - 1)),
                                     reads=[GH[b], vs[b]], writes=[bk], sig=(j == EG - 1))
                            a = acc[:, t_, db * 512:(db + 1) * 512]
                            if eg == 0:
                                cp(S, "dve", a, bk[:], [bk], [acc])
                            else:
                                tt(S, "dve", a, a, bk[:], ALU.add, [acc, bk], [acc])
                S.barrier()
            with contextlib.ExitStack() as st3:
                L = LNBufs(S, st3, lnw, lnb, nb=2)
                ln_prefetch(S, L, xold, g * TG)
                for t_ in range(ntile):
                    if t_ + 1 < ntile:
                        L.i += 1
                        ln_prefetch(S, L, xold, g * TG + (t_ + 1) * 128)
                        L.i -= 1
                    ln_epilogue(S, C, L, acc[:, t_, :], [acc], g * TG + t_ * 128, xnew, xnewT)
                S.barrier()


def setup_consts(S, C, st, identf_d, iota_d):
    C.identf = S.sbuf(st, [128, 128], F32, "identf")
    C.identb = S.sbuf(st, [128, 128], BF16, "identb")
    C.iota = S.sbuf(st, [128, 128], F32, "iota")
    C.iota16 = S.sbuf(st, [128, 16], F32, "iota16")
    S.dma("sp", C.identf[:], identf_d, C.identf, writes=[C.identf])
    S.dma("sp", C.iota[:], iota_d, C.iota, writes=[C.iota])
    cp(S, "dve", C.identb[:], C.identf[:], [C.identf], [C.identb])
    cp(S, "dve", C.iota16[:], C.iota[:, 0:16], [C.iota], [C.iota16])
    C.onecol = S.sbuf(st, [128, 1], F32, "onecol")
    S.op("dve", lambda en: en.memset(C.onecol[:], 1.0), [], [C.onecol])


def phase_ssm_inproj(S, C, xT_d, W, convwT, convb2, dtbias, Alog, xbcT_d, z_d, dtT_d, dAT_d, NT, LSEQ):
    with contextlib.ExitStack() as st:
        xT = S.sbuf(st, [128, 16, LSEQ], BF16, "si_xT")
        slabs = [S.sbuf(st, [128, 16, 512], BF16, "si_slab") for _ in range(2)]
        pre = [S.sbuf(st, [128, 3 + LSEQ], F32, "si_pre") for _ in range(2)]
        accb = S.sbuf(st, [128, LSEQ], F32, "si_acc")
        outb = [S.sbuf(st, [128, LSEQ], BF16, "si_out") for _ in range(2)]
        zst = [S.sbuf(st, [128, 512], BF16, "si_z") for _ in range(4)]
        cw = S.sbuf(st, [128, 48, 4], F32, "si_cw")
        cb = S.sbuf(st, [128, 48], F32, "si_cb")
        dtb = S.sbuf(st, [64, 1], F32, "si_dtb")
        Aneg = S.sbuf(st, [64, 1], F32, "si_A")
        dtr = S.sbuf(st, [64, LSEQ], F32, "si_dtr")
        dta = S.sbuf(st, [64, LSEQ], F32, "si_dta")
        dtl = S.sbuf(st, [64, LSEQ], F32, "si_dtl")
        S.dma("sp", cw[:], convwT.rearrange("(ci p) j -> p ci j", p=128), cw, writes=[cw])
        S.dma("sp", cb[:], convb2, cb, writes=[cb])
        S.dma("sp", dtb[:], dtbias.rearrange("(p o) -> p o", o=1), dtb, writes=[dtb])
        S.dma("sp", Aneg[:], Alog.rearrange("(p o) -> p o", o=1), Aneg, writes=[Aneg])
        act(S, Aneg[:], Aneg[:], AF.Exp, [Aneg], [Aneg])
        ts(S, "dve", Aneg[:], Aneg[:], -1.0, None, ALU.mult, None, [Aneg], [Aneg])
        for p_ in pre:
            S.op("dve", lambda en: en.memset(p_[:, 0:3], 0.0), [], [p_])
        xv = xT_d.rearrange("(kc p) t -> p kc t", p=128)
        nz = [0]
        for sq in range(NT // LSEQ):
            s0 = sq * LSEQ
            S.dma("sp", xT[:], xv[:, :, s0:s0 + LSEQ], xT, writes=[xT])
            ntb = (LSEQ + 511) // 512

            def epi_f(ci, tb, bk, rows, tw, s0=s0):
                if ci < 48:
                    pr = pre[ci % 2]
                    cp(S, "act", pr[:, 3 + tb * 512:3 + tb * 512 + tw], bk[:, 0:tw], [bk], [pr])
                    if tb == ntb - 1:
                        ob = outb[ci % 2]
                        ts(S, "dve", accb[:], pr[:, 3:3 + LSEQ], cw[:, ci, 3:4], None, ALU.mult, None, [pr, cw], [accb])
                        for j in (2, 1, 0):
                            S.op("dve", lambda en: en.scalar_tensor_tensor(out=accb[:], in0=pr[:, j:j + LSEQ], scalar=cw[:, ci, j:j + 1],
                                                                           in1=accb[:], op0=ALU.mult, op1=ALU.add), [pr, cw, accb], [accb])
                        act(S, ob[:], accb[:], AF.Silu, [accb, cb], [ob], bias=cb[:, ci:ci + 1])
                        S.dma("sp", xbcT_d[ci * 128:(ci + 1) * 128, s0:s0 + LSEQ], ob[:], ob, reads=[ob])
                else:
                    cp(S, "act", dtr[:, tb * 512:tb * 512 + tw], bk[0:64, 0:tw], [bk], [dtr])
                    if tb == ntb - 1:
                        ts(S, "dve", dtr[:], dtr[:], dtb[:, 0:1], None, ALU.add, None, [dtr, dtb], [dtr])
                        act(S, dta[:], dtr[:], AF.Abs, [dtr], [dta])
                        act(S, dta[:], dta[:], AF.Exp, [dta], [dta], scale=-1.0)
                        act(S, dtl[:], dta[:], AF.Ln, [dta, C.onecol], [dtl], bias=C.onecol[0:64, 0:1])
                        S.op("dve", lambda en: en.scalar_tensor_tensor(out=dta[:], in0=dtr[:], scalar=0.0, in1=dtl[:],
                                                                       op0=ALU.max, op1=ALU.add), [dtr, dtl], [dta])
                        ts(S, "dve", dtl[:], dta[:], Aneg[:, 0:1], None, ALU.mult, None, [dta, Aneg], [dtl])
                        S.dma("sp", dtT_d[:, s0:s0 + LSEQ], dta[:], dta, reads=[dta])
                        S.dma("sp", dAT_d[:, s0:s0 + LSEQ], dtl[:], dtl, reads=[dtl])
            gemm_feat(S, C, xT, 16, LSEQ, W, 4096, 6144 + 64, slabs, epi_f)

            def epi_z(fb, t_, bk, fw, s0=s0):
                zb = zst[nz[0] % 4]
                nz[0] += 1
                cp(S, "act", zb[:, 0:fw], bk[:, 0:fw], [bk], [zb])
                S.dma("sp", z_d[s0 + t_ * 128:s0 + (t_ + 1) * 128, fb * 512:fb * 512 + fw], zb[:, 0:fw], zb, reads=[zb])
            gemm_tok(S, C, xT, 16, LSEQ, W, 0, 4096, slabs, epi_z)
        S.barrier()


def phase_ssd(S, C, xbcT_d, z_d, dtT_d, dAT_d, ssmD, normw, negmask_d, ynT_d, NT, LSEQ):
    with contextlib.ExitStack() as st:
        xsT = [S.sbuf(st, [128, 32, 128], BF16, "sd_xsT") for _ in range(2)]
        BCT = [S.sbuf(st, [128, 16, 128], BF16, "sd_BCT") for _ in range(2)]
        zt = S.sbuf(st, [128, 4096], BF16, "sd_z")
        dsm = [S.sbuf(st, [64, 2, 128], F32, "sd_dsm") for _ in range(3)]
        cumT = S.sbuf(st, [64, 128], F32, "sd_cumT")
        cumhl = S.sbuf(st, [64, 2, 128], BF16, "sd_cumhl")
        winT = S.sbuf(st, [64, 128], F32, "sd_winT")
        elT = S.sbuf(st, [64, 1], F32, "sd_elT")
        diagE = S.sbuf(st, [64, 64], F32, "sd_diagE")
        tm = S.sbuf(st, [128, 3, 64], F32, "sd_tm")
        ncum = S.sbuf(st, [128, 64], F32, "sd_ncum")
        ecum = S.sbuf(st, [128, 64], F32, "sd_ecum")
        elbc = S.sbuf(st, [128, 64], F32, "sd_elbc")
        sel = S.sbuf(st, [64, 64, 128], BF16, "sd_sel")
        LT = S.sbuf(st, [128, 64, 128], BF16, "sd_LT")
        xs = S.sbuf(st, [128, 64, 64], BF16, "sd_xs")
        Btm = S.sbuf(st, [128, 8, 128], BF16, "sd_Btm")
        xdt = S.sbuf(st, [128, 64, 64], BF16, "sd_xdt")
        xw = S.sbuf(st, [128, 64, 64], BF16, "sd_xw")
        MT = [S.sbuf(st, [128, 8, 128], BF16, "sd_MT") for _ in range(2)]
        Y = S.sbuf(st, [128, 64, 64], F32, "sd_Y")
        t1 = S.sbuf(st, [128, 8, 64], F32, "sd_t1")
        state = S.sbuf(st, [128, 8, 512], F32, "sd_state")
        stbf = S.sbuf(st, [128, 8, 512], BF16, "sd_stbf")
        sz = S.sbuf(st, [128, 4096], BF16, "sd_sz")
        nw = S.sbuf(st, [128, 4096], BF16, "sd_nw")
        Dbc = S.sbuf(st, [128, 64], F32, "sd_Dbc")
        nm = S.sbuf(st, [128, 512], BF16, "sd_negmask")
        nmf = S.sbuf(st, [128, 512], F32, "sd_negmaskf")
        ones64 = S.sbuf(st, [64, 128], F32, "sd_ones64")
        ss = S.sbuf(st, [128, 8], F32, "sd_ss")
        junk = S.sbuf(st, [128, 512], BF16, "sd_junk")
        ynb = S.sbuf(st, [128, 4096], BF16, "sd_ynb")
        ynT = S.sbuf(st, [128, 32, 128], BF16, "sd_ynT")
        S.dma("pool", nw[:], bcast_rows(normw, 4096), nw, writes=[nw])
        S.dma("sp", Dbc[:], bcast_rows(ssmD, 64), Dbc, writes=[Dbc])
        S.dma("sp", nmf[:], negmask_d, nmf, writes=[nmf])
        cp(S, "dve", nm[:], nmf[:], [nmf], [nm])
        S.op("dve", lambda en: en.memset(ones64[:], 1.0), [], [ones64])
        cp(S, "dve", sel[:], C.identf[0:64, 0:64].unsqueeze(2).to_broadcast([64, 64, 128]), [C.identf], [sel])
        xv = xbcT_d.rearrange("(c p) t -> p c t", p=128)
        ynv = ynT_d.rearrange("(c p) t -> p c t", p=128)
        nch = LSEQ // 128
        ntot = (NT // LSEQ) * nch

        def pos_of(i):
            sq, c = divmod(i, nch)
            return sq, c, sq * LSEQ + c * 128

        def load(i):
            _, _, g0 = pos_of(i)
            b = i % 2
            S.dma("sp", xsT[b][:], xv[:, 0:32, g0:g0 + 128], xsT[b], writes=[xsT[b]])
            S.dma("sp", BCT[b][:], xv[:, 32:48, g0:g0 + 128], BCT[b], writes=[BCT[b]])

        def load_small(i):
            _, _, g0 = pos_of(i)
            d = dsm[i % 3]
            S.dma("sp", d[:, 0, :], dtT_d[:, g0:g0 + 128], d, writes=[d])
            S.dma("sp", d[:, 1, :], dAT_d[:, g0:g0 + 128], d, writes=[d])

        def prep(i):
            d = dsm[i % 3]
            S.op("dve", lambda en: en.tensor_tensor_scan(out=cumT[:], data0=ones64[:], data1=d[:, 1, :], initial=0.0,
                                                         op0=ALU.mult, op1=ALU.add), [ones64, d], [cumT])
            cp(S, "act", cumhl[:, 0, :], cumT[:], [cumT], [cumhl])
            tt(S, "dve", cumhl[:, 1, :], cumT[:], cumhl[:, 0, :], ALU.subtract, [cumT, cumhl], [cumhl])
            act(S, winT[:], cumT[:], AF.Exp, [cumT], [winT], scale=-1.0, bias=cumT[:, 127:128])
            act(S, elT[:], cumT[:, 127:128], AF.Exp, [cumT], [elT])
            ts(S, "dve", diagE[:], C.identf[0:64, 0:64], elT[:, 0:1], None, ALU.mult, None, [C.identf, elT], [diagE])
            bk = S.bank()
            for j, (src, sb_) in enumerate(((d[:, 0, :], d), (cumT[:], cumT), (winT[:], winT))):
                S.op("pe", lambda en: en.transpose(bk[:, j * 64:(j + 1) * 64], src, C.identf[0:64, 0:64]),
                     reads=[sb_, C.identf], writes=[bk], sig=(j == 2))
            cp(S, "act", tm[:], bk[:, 0:192].rearrange("p (a b) -> p a b", a=3), [bk], [tm])
            ts(S, "dve", ncum[:], tm[:, 1, :], -1.0, None, ALU.mult, None, [tm], [ncum])
            act(S, ecum[:], tm[:, 1, :], AF.Exp, [tm], [ecum])
            bk = S.bank()
            S.op("pe", lambda en: en.matmul(bk[:, 0:64], lhsT=ones64[:], rhs=diagE[:], start=True, stop=True),
                 reads=[ones64, diagE], writes=[bk])
            cp(S, "act", elbc[:], bk[:, 0:64], [bk], [elbc])
            for q in range(16):
                bk = S.bank()
                first = True
                for j in range(4):
                    h = q * 4 + j
                    for hl in range(2):
                        S.op("pe", lambda en: en.matmul(bk[:, j * 128:(j + 1) * 128], lhsT=sel[:, h, :], rhs=cumhl[:, hl, :],
                                                        start=first, stop=False), reads=[sel, cumhl], writes=[bk], sig=False)
                        first = False
                S.op("pe", lambda en: en.matmul(bk[:], lhsT=C.identb[:], rhs=nm[:], start=False, stop=True),
                     reads=[C.identb, nm], writes=[bk])
                for j in range(4):
                    h = q * 4 + j
                    act(S, LT[:, h, :], bk[:, j * 128:(j + 1) * 128], AF.Exp, [bk, ncum], [LT], bias=ncum[:, h:h + 1])

        def head(i):
            b = i % 2
            transposes_to(S, C, xsT[b][:].rearrange("p a b -> p (a b)"), [xsT[b]], 32,
                          lambda j, n: xs[:].rearrange("p h d -> p (h d)")[:, j * 128:(j + n) * 128].rearrange("p (a b) -> p a b", a=n), [xs], BF16)
            transposes_to(S, C, BCT[b][:].rearrange("p a b -> p (a b)"), [BCT[b]], 8, lambda j, n: Btm[:, j:j + n, :], [Btm], BF16)
            tt(S, "dve", xdt[:], xs[:], tm[:, 0, :].unsqueeze(2).to_broadcast([128, 64, 64]), ALU.mult, [xs, tm], [xdt])
            tt(S, "pool", xw[:], xdt[:], tm[:, 2, :].unsqueeze(2).to_broadcast([128, 64, 64]), ALU.mult, [xdt, tm], [xw])

        def groups(i):
            b = i % 2
            for g in range(8):
                hs = slice(8 * g, 8 * g + 8)
                bcb = S.bank()
                S.op("pe", lambda en: en.matmul(bcb[:, 0:128], lhsT=BCT[b][:, g, :], rhs=BCT[b][:, 8 + g, :], start=True, stop=True),
                     reads=[BCT[b]], writes=[bcb])
                M = MT[g % 2]
                tt(S, "dve", M[:], LT[:, hs, :], bcb[:, 0:128].unsqueeze(1).to_broadcast([128, 8, 128]), ALU.mult, [LT, bcb], [M])
                byd = S.bank()
                for r in range(8):
                    S.op("pe", lambda en: en.matmul(byd[:, r * 64:(r + 1) * 64], lhsT=M[:, r, :], rhs=xdt[:, 8 * g + r, :], start=True, stop=True),
                         reads=[M, xdt], writes=[byd], sig=(r == 7))
                byo = S.bank()
                S.op("pe", lambda en: en.matmul(byo[:], lhsT=BCT[b][:, 8 + g, :], rhs=stbf[:, g, :], start=True, stop=True),
                     reads=[BCT[b], stbf], writes=[byo])
                tt(S, "dve", t1[:], byo[:].rearrange("p (r d) -> p r d", r=8), ecum[:, hs].unsqueeze(2).to_broadcast([128, 8, 64]), ALU.mult,
                   [byo, ecum], [t1])
                tt(S, "dve", Y[:, hs, :], byd[:].rearrange("p (r d) -> p r d", r=8), t1[:], ALU.add, [byd, t1], [Y])
                bns = S.bank()
                S.op("pe", lambda en: en.matmul(bns[:], lhsT=Btm[:, g, :], rhs=xw[:, hs, :], start=True, stop=True),
                     reads=[Btm, xw], writes=[bns])
                sg = state[:, g, :].rearrange("p (r d) -> p r d", r=8)
                tt(S, "pool", sg, sg, elbc[:, hs].unsqueeze(2).to_broadcast([128, 8, 64]), ALU.mult, [state, elbc], [state])
                tt(S, "dve", state[:, g, :], state[:, g, :], bns[:], ALU.add, [state, bns], [state])
                cp(S, "act", stbf[:, g, :], state[:, g, :], [state], [stbf])

        def tail(i):
            tt(S, "pool", xdt[:], xs[:], Dbc[:].unsqueeze(2).to_broadcast([128, 64, 64]), ALU.mult, [xs, Dbc], [xdt])
            Yf = Y[:].rearrange("p h d -> p (h d)")
            tt(S, "dve", Yf, Yf, xdt[:].rearrange("p h d -> p (h d)"), ALU.add, [Y, xdt], [Y])
            act(S, sz[:], zt[:], AF.Silu, [zt], [sz])
            tt(S, "dve", Yf, Yf, sz[:], ALU.mult, [Y, sz], [Y])
            for g in range(8):
                act(S, junk[:], Yf[:, g * 512:(g + 1) * 512], AF.Square, [Y], [junk, ss], accum_out=ss[:, g:g + 1])
            rsqrt_eps(S, ss[:], ss[:], [ss], ss, 1.0 / 512.0, LN_EPS)
            Y8 = Y[:].rearrange("p (g r) d -> p g (r d)", g=8)
            tt(S, "dve", Y8, Y8, ss[:].unsqueeze(2).to_broadcast([128, 8, 512]), ALU.mult, [Y, ss], [Y])
            tt(S, "pool", ynb[:], Yf, nw[:], ALU.mult, [Y, nw], [ynb])

        def out(i):
            _, _, g0 = pos_of(i)
            transposes_to(S, C, ynb, [ynb], 32, lambda j, n: ynT[:, j:j + n, :], [ynT], BF16, evac="act")
            S.dma("sp", ynv[:, :, g0:g0 + 128], ynT[:], ynT, reads=[ynT])

        load(0)
        load_small(0)
        if ntot > 1:
            load_small(1)
        prep(0)
        for i in range(ntot):
            sq, c, g0 = pos_of(i)
            if i + 1 < ntot:
                load(i + 1)
            if i + 2 < ntot:
                load_small(i + 2)
            S.dma("sp", zt[:], z_d[g0:g0 + 128, :], zt, writes=[zt])
            if c == 0:
                S.op("pool", lambda en: en.memset(state[:], 0.0), [], [state])
                S.op("pool", lambda en: en.memset(stbf[:], 0.0), [], [stbf])
            head(i)
            groups(i)
            if i > 0:
                out(i - 1)
            if i + 1 < ntot:
                prep(i + 1)
            tail(i)
        out(ntot - 1)
        S.barrier()


TWO_PI = 6.283185307179586
PI = 3.141592653589793


def _sin_table(S, out_buf, ang_buf, shift, tmp, tmpi, mulc):
    ts(S, "dve", tmp[:], ang_buf[:], float(shift), 1.0 / TWO_PI, ALU.add, ALU.mult, [ang_buf], [tmp])
    cp(S, "dve", tmpi[:], tmp[:], [tmp], [tmpi])
    cp(S, "dve", tmp[:], tmpi[:], [tmpi], [tmp])
    ts(S, "dve", tmp[:], tmp[:], -TWO_PI, float(shift), ALU.mult, ALU.add, [tmp], [tmp])
    tt(S, "dve", out_buf[:], tmp[:], ang_buf[:], ALU.add, [tmp, ang_buf], [out_buf])
    ts(S, "dve", tmp[:], out_buf[:], PI, -TWO_PI, ALU.is_gt, ALU.mult, [out_buf], [tmp])
    tt(S, "dve", out_buf[:], out_buf[:], tmp[:], ALU.add, [out_buf, tmp], [out_buf])
    ts(S, "dve", tmp[:], out_buf[:], -PI, TWO_PI, ALU.is_lt, ALU.mult, [out_buf], [tmp])
    tt(S, "dve", out_buf[:], out_buf[:], tmp[:], ALU.add, [out_buf, tmp], [out_buf])
    act(S, out_buf[:], out_buf[:], AF.Sin, [out_buf], [out_buf])
    if mulc != 1.0:
        ts(S, "dve", out_buf[:], out_buf[:], float(mulc), None, ALU.mult, None, [out_buf], [out_buf])


def phase_ret_inproj(S, C, xT_d, W, pos_d, invfreq_d, qkT_d, vg_d, NT, LSEQ):
    with contextlib.ExitStack() as st:
        xT = S.sbuf(st, [128, 16, LSEQ], BF16, "ri_xT")
        slabs = [S.sbuf(st, [128, 16, 512], BF16, "ri_slab") for _ in range(2)]
        posi = S.sbuf(st, [128, LSEQ], I32, "ri_posi")
        ang = S.sbuf(st, [128, LSEQ], F32, "ri_ang")
        tmp = S.sbuf(st, [128, LSEQ], F32, "ri_tmp")
        tmpi = S.sbuf(st, [128, LSEQ], I32, "ri_tmpi")
        cosk = S.sbuf(st, [128, LSEQ], F32, "ri_cosk")
        sink = S.sbuf(st, [128, LSEQ], F32, "ri_sink")
        cosq = S.sbuf(st, [128, LSEQ], F32, "ri_cosq")
        sinq = S.sbuf(st, [128, LSEQ], F32, "ri_sinq")
        invf = S.sbuf(st, [128, 1], F32, "ri_invf")
        t1s = S.sbuf(st, [128, LSEQ], F32, "ri_t1s")
        ra = S.sbuf(st, [128, 512], F32, "ri_ra")
        rb = S.sbuf(st, [128, 512], F32, "ri_rb")
        rc = S.sbuf(st, [128, 512], F32, "ri_rc")
        rd = S.sbuf(st, [128, 512], F32, "ri_rd")
        o1 = [S.sbuf(st, [128, 512], BF16, "ri_o1") for _ in range(2)]
        o2 = [S.sbuf(st, [128, 512], BF16, "ri_o2") for _ in range(2)]
        vst = [S.sbuf(st, [128, 512], BF16, "ri_v") for _ in range(4)]
        S.dma("sp", invf[:], invfreq_d, invf, writes=[invf])
        xv = xT_d.rearrange("(kc p) t -> p kc t", p=128)
        cnt = [0, 0]
        for sq in range(NT // LSEQ):
            s0 = sq * LSEQ
            S.dma("sp", xT[:], xv[:, :, s0:s0 + LSEQ], xT, writes=[xT])
            S.dma("sp", posi[:], pos_d[sq:sq + 1, :].to_broadcast([128, LSEQ]), posi, writes=[posi])
            cp(S, "dve", ang[:], posi[:], [posi], [ang])
            ts(S, "dve", ang[:], ang[:], invf[:, 0:1], None, ALU.mult, None, [ang, invf], [ang])
            _sin_table(S, sink, ang, 0.0, tmp, tmpi, 1.0)
            _sin_table(S, cosk, ang, PI / 2, tmp, tmpi, 1.0)
            ts(S, "dve", sinq[:], sink[:], 1.0 / 16.0, None, ALU.mult, None, [sink], [sinq])
            ts(S, "dve", cosq[:], cosk[:], 1.0 / 16.0, None, ALU.mult, None, [cosk], [cosq])

            def epi_f(ci, tb, bk, rows, tw, s0=s0):
                cs, sn = (cosq, sinq) if ci < 16 else (cosk, sink)
                sl = slice(tb * 512, tb * 512 + tw)
                if ci % 2 == 0:
                    cp(S, "act", t1s[:, sl], bk[:, 0:tw], [bk], [t1s])
                else:
                    k = cnt[0] % 2
                    cnt[0] += 1
                    tt(S, "pool", ra[:, 0:tw], t1s[:, sl], cs[:, sl], ALU.mult, [t1s, cs], [ra])
                    tt(S, "dve", rb[:, 0:tw], bk[:, 0:tw], sn[:, sl], ALU.mult, [bk, sn], [rb])
                    tt(S, "dve", o1[k][:, 0:tw], ra[:, 0:tw], rb[:, 0:tw], ALU.subtract, [ra, rb], [o1[k]])
                    tt(S, "dve", rc[:, 0:tw], bk[:, 0:tw], cs[:, sl], ALU.mult, [bk, cs], [rc])
                    tt(S, "pool", rd[:, 0:tw], t1s[:, sl], sn[:, sl], ALU.mult, [t1s, sn], [rd])
                    tt(S, "dve", o2[k][:, 0:tw], rc[:, 0:tw], rd[:, 0:tw], ALU.add, [rc, rd], [o2[k]])
                    S.dma("sp", qkT_d[(ci - 1) * 128:ci * 128, s0 + tb * 512:s0 + tb * 512 + tw], o1[k][:, 0:tw], o1[k], reads=[o1[k]])
                    S.dma("sp", qkT_d[ci * 128:(ci + 1) * 128, s0 + tb * 512:s0 + tb * 512 + tw], o2[k][:, 0:tw], o2[k], reads=[o2[k]])
            gemm_feat(S, C, xT, 16, LSEQ, W, 0, 4096, slabs, epi_f)

            def epi_v(fb, t_, bk, fw, s0=s0):
                zb = vst[cnt[1] % 4]
                cnt[1] += 1
                cp(S, "act", zb[:, 0:fw], bk[:, 0:fw], [bk], [zb])
                S.dma("sp", vg_d[s0 + t_ * 128:s0 + (t_ + 1) * 128, fb * 512:fb * 512 + fw], zb[:, 0:fw], zb, reads=[zb])
            gemm_tok(S, C, xT, 16, LSEQ, W, 4096, 8192, slabs, epi_v)
        S.barrier()


def phase_ret(S, C, qkT_d, vg_d, dmatT_d, qdec_d, kdec_d, cdec, gnw, gnb, ynT_d, NT, LSEQ):
    with contextlib.ExitStack() as st:
        qk = [S.sbuf(st, [128, 32, 128], BF16, "rt_qk") for _ in range(2)]
        vg = [S.sbuf(st, [128, 8192], BF16, "rt_vg") for _ in range(2)]
        dmT = S.sbuf(st, [128, 8, 128], F32, "rt_dmT")
        qdec = S.sbuf(st, [128, 8, 128], F32, "rt_qdec")
        kdec = S.sbuf(st, [128, 8], F32, "rt_kdec")
        ST = [S.sbuf(st, [128, 128], BF16, "rt_ST") for _ in range(2)]
        qd = [S.sbuf(st, [128, 2, 128], BF16, "rt_qd") for _ in range(2)]
        kd = [S.sbuf(st, [128, 2, 128], BF16, "rt_kd") for _ in range(2)]
        R = S.sbuf(st, [128, 8, 2, 512], F32, "rt_R")
        Rb = S.sbuf(st, [128, 8, 2, 512], BF16, "rt_Rb")
        yhs = [S.sbuf(st, [128, 4096], F32, "rt_yh") for _ in range(2)]
        stats = S.sbuf(st, [128, 6], F32, "rt_stats")
        mv = S.sbuf(st, [128, 2], F32, "rt_mv")
        rstd = S.sbuf(st, [128, 1], F32, "rt_rstd")
        gw = S.sbuf(st, [128, 4096], BF16, "rt_gw")
        gb = S.sbuf(st, [128, 4096], BF16, "rt_gb")
        sg = S.sbuf(st, [128, 4096], BF16, "rt_sg")
        ynb = S.sbuf(st, [128, 4096], BF16, "rt_ynb")
        ynT = S.sbuf(st, [128, 32, 128], BF16, "rt_ynT")
        S.dma("sp", dmT[:], dmatT_d, dmT, writes=[dmT])
        S.dma("sp", qdec[:], qdec_d, qdec, writes=[qdec])
        S.dma("sp", kdec[:], kdec_d, kdec, writes=[kdec])
        S.dma("pool", gw[:], bcast_rows(gnw, 4096), gw, writes=[gw])
        S.dma("pool", gb[:], bcast_rows(gnb, 4096), gb, writes=[gb])
        qv = qkT_d.rearrange("(c p) t -> p c t", p=128)
        ynv = ynT_d.rearrange("(c p) t -> p c t", p=128)
        nch = LSEQ // 128
        ntot = (NT // LSEQ) * nch

        def load(i):
            sq, c = divmod(i, nch)
            g0 = sq * LSEQ + c * 128
            S.dma("sp", qk[i % 2][:], qv[:, :, g0:g0 + 128], qk[i % 2], writes=[qk[i % 2]])
            S.dma("sp", vg[i % 2][:], vg_d[g0:g0 + 128, :], vg[i % 2], writes=[vg[i % 2]])

        def heads(i):
            sq, c = divmod(i, nch)
            Q = qk[i % 2]
            V = vg[i % 2]
            yh = yhs[i % 2]
            if c == 0:
                S.op("pool", lambda en: en.memset(R[:], 0.0), [], [R])
                S.op("pool", lambda en: en.memset(Rb[:], 0.0), [], [Rb])
            for h in range(8):
                k = h % 2
                bs = S.bank()
                for half in range(2):
                    S.op("pe", lambda en: en.matmul(bs[:, 0:128], lhsT=Q[:, 16 + 2 * h + half, :], rhs=Q[:, 2 * h + half, :],
                                                    start=(half == 0), stop=(half == 1)), reads=[Q], writes=[bs], sig=(half == 1))
                tt(S, "dve", ST[k][:], bs[:, 0:128], dmT[:, h, :], ALU.mult, [bs, dmT], [ST[k]])
                tt(S, "pool", qd[k][:], Q[:, 2 * h:2 * h + 2, :], qdec[:, h, :].unsqueeze(1).to_broadcast([128, 2, 128]), ALU.mult,
                   [Q, qdec], [qd[k]])
                by = S.bank()
                S.op("pe", lambda en: en.matmul(by[:], lhsT=ST[k][:], rhs=V[:, h * 512:(h + 1) * 512], start=True, stop=False),
                     reads=[ST[k], V], writes=[by], sig=False)
                for half in range(2):
                    S.op("pe", lambda en: en.matmul(by[:], lhsT=qd[k][:, half, :], rhs=Rb[:, h, half, :], start=False, stop=(half == 1)),
                         reads=[qd[k], Rb], writes=[by], sig=(half == 1))
                S.op("dve", lambda en: en.bn_stats(out=stats[:], in_=by[:]), [by], [stats])
                S.op("dve", lambda en: en.bn_aggr(out=mv[:], in_=stats[:]), [stats], [mv])
                rsqrt_eps(S, rstd[:], mv[:, 1:2], [mv], rstd, 1.0, LN_EPS)
                ts(S, "dve", yh[:, h * 512:(h + 1) * 512], by[:], mv[:, 0:1], rstd[:, 0:1], ALU.subtract, ALU.mult, [by, mv, rstd], [yh])
                bt = S.bank()
                btv = bt.t[:].bitcast(BF16)
                for half in range(2):
                    S.op("pe", lambda en: en.transpose(btv[:, half * 128:(half + 1) * 128], Q[:, 16 + 2 * h + half, :], C.identb[:]),
                         reads=[Q, C.identb], writes=[bt], sig=(half == 1))
                ts(S, "dve", kd[k][:], btv[:, 0:256].rearrange("p (a b) -> p a b", a=2), kdec[:, h:h + 1], None, ALU.mult, None,
                   [bt, kdec], [kd[k]])
                for half in range(2):
                    br = S.bank()
                    S.op("pe", lambda en: en.matmul(br[:], lhsT=kd[k][:, half, :], rhs=V[:, h * 512:(h + 1) * 512], start=True, stop=True),
                         reads=[kd[k], V], writes=[br])
                    S.op("dve", lambda en: en.scalar_tensor_tensor(out=R[:, h, half, :], in0=R[:, h, half, :], scalar=float(cdec[h]),
                                                                   in1=br[:], op0=ALU.mult, op1=ALU.add), [R, br], [R])
                    cp(S, "act", Rb[:, h, half, :], R[:, h, half, :], [R], [Rb])

        def tail(i):
            V = vg[i % 2]
            yh = yhs[i % 2]
            tt(S, "pool", yh[:], yh[:], gw[:], ALU.mult, [yh, gw], [yh])
            tt(S, "dve", yh[:], yh[:], gb[:], ALU.add, [yh, gb], [yh])
            act(S, sg[:], V[:, 4096:8192], AF.Silu, [V], [sg])
            tt(S, "dve", ynb[:], yh[:], sg[:], ALU.mult, [yh, sg], [ynb])

        def out(i):
            sq, c = divmod(i, nch)
            g0 = sq * LSEQ + c * 128
            transposes_to(S, C, ynb, [ynb], 32, lambda j, n: ynT[:, j:j + n, :], [ynT], BF16, evac="act")
            S.dma("sp", ynv[:, :, g0:g0 + 128], ynT[:], ynT, reads=[ynT])

        load(0)
        for i in range(ntot):
            if i + 1 < ntot:
                load(i + 1)
            heads(i)
            if i > 0:
                out(i - 1)
            tail(i)
        out(ntot - 1)
        S.barrier()


def _ret_gamma():
    h = np.arange(8, dtype=np.float64)
    return np.log1p(-np.exp2(-5.0 - h))


RET_CDEC = [float(np.exp(128.0 * lg)) for lg in _ret_gamma()]


def ret_consts():
    lg = _ret_gamma()
    idx = np.arange(128, dtype=np.float64)
    rel = idx[None, :] - idx[:, None]
    dm = np.where(rel[:, None, :] >= 0, np.exp(rel[:, None, :] * lg[None, :, None]), 0.0)
    qdec = np.exp((idx[None, :] + 1.0) * lg[:, None])
    kdec = np.exp((127.0 - idx)[:, None] * lg[None, :])
    invf = 10000.0 ** (-np.arange(0, 256, 2, dtype=np.float32) / np.float32(256))
    return {"dmatT": dm.astype(np.float32), "qdec": np.tile(qdec[None], (128, 1, 1)).astype(np.float32),
            "kdec": kdec.astype(np.float32), "invfreq": invf.astype(np.float32).reshape(128, 1)}


WEIGHT_SPECS = [
    ("ssm_in_proj", [D, 10304]), ("ssm_convwT", [6144, 4]), ("ssm_convb2", [128, 48]), ("ssm_dt_bias", [64]),
    ("ssm_A_log", [64]), ("ssm_D", [64]), ("ssm_norm_w", [4096]), ("ssm_out_proj", [4096, D]),
    ("ret_in_proj", [D, 12288]), ("ret_gn_w", [4096]), ("ret_gn_b", [4096]), ("ret_out_proj", [4096, D]),
    ("mix_ln_w", [2, D]), ("mix_ln_b", [2, D]), ("peer_w_q", [2, D, D]), ("peer_keysT", [2, 16, 128, 128]),
    ("peer_uT", [2, D, 16384]), ("peer_v", [2, 16384, D]), ("ffn_ln_w", [2, D]), ("ffn_ln_b", [2, D]),
    ("identf", [128, 128]), ("iota", [128, 128]), ("negmask", [128, 512]),
    ("dmatT", [128, 8, 128]), ("qdec", [128, 8, 128]), ("kdec", [128, 8]), ("invfreq", [128, 1]),
]


def build_program(NT, LSEQ):
    nc = bass.Bass("TRN2", target_bir_lowering=False)
    NSEQ = NT // LSEQ
    x = nc.dram_tensor("x", [NT, D], F32, kind="ExternalInput").ap()
    pos = nc.dram_tensor("positions", [NSEQ, LSEQ], I32, kind="ExternalInput").ap()
    w = {}
    for name, shp in WEIGHT_SPECS:
        w[name] = nc.dram_tensor(name, list(shp), F32, kind="ExternalInput").ap()
    out = nc.dram_tensor("out", [NT, D], F32, kind="ExternalOutput").ap()
    S = Sched(nc)
    C = Ctx()
    with contextlib.ExitStack() as st:
        S.init_banks(st)
        setup_consts(S, C, st, w["identf"], w["iota"])
        xT0 = S.hbm("xT0", [D, NT], BF16)
        xbcT = S.hbm("xbcT", [6144, NT], BF16)
        z_d = S.hbm("z", [NT, 4096], BF16)
        dtT = S.hbm("dtT", [64, NT], F32)
        dAT = S.hbm("dAT", [64, NT], F32)
        ynT = S.hbm("ynT", [4096, NT], BF16)
        x1 = S.hbm("x1", [NT, D], F32)
        x1T = S.hbm("x1T", [D, NT], BF16)
        Gh = S.hbm("Gh", [NT // 128, 128, 128, 128], BF16)
        x2 = S.hbm("x2", [NT, D], F32)
        x2T = S.hbm("x2T", [D, NT], BF16)
        qkT = S.hbm("qkT", [4096, NT], BF16)
        vg = S.hbm("vg", [NT, 8192], BF16)
        x3 = S.hbm("x3", [NT, D], F32)
        x3T = S.hbm("x3T", [D, NT], BF16)
        phase_make_xT(S, C, x, xT0, NT)
        phase_ssm_inproj(S, C, xT0, w["ssm_in_proj"], w["ssm_convwT"], w["ssm_convb2"], w["ssm_dt_bias"], w["ssm_A_log"],
                         xbcT, z_d, dtT, dAT, NT, LSEQ)
        phase_ssd(S, C, xbcT, z_d, dtT, dAT, w["ssm_D"], w["ssm_norm_w"], w["negmask"], ynT, NT, LSEQ)
        phase_proj_ln(S, C, ynT, 4096, w["ssm_out_proj"], x, x1, x1T, w["mix_ln_w"][0], w["mix_ln_b"][0], NT)
        phase_peer_route(S, C, x1T, w["peer_w_q"][0], w["peer_keysT"][0], Gh, NT)
        phase_peer_dense(S, C, x1T, w["peer_uT"][0], w["peer_v"][0], Gh, x1, x2, x2T, w["ffn_ln_w"][0], w["ffn_ln_b"][0], NT)
        phase_ret_inproj(S, C, x2T, w["ret_in_proj"], pos, w["invfreq"], qkT, vg, NT, LSEQ)
        phase_ret(S, C, qkT, vg, w["dmatT"], w["qdec"], w["kdec"], RET_CDEC, w["ret_gn_w"], w["ret_gn_b"], ynT, NT, LSEQ)
        phase_proj_ln(S, C, ynT, 4096, w["ret_out_proj"], x2, x3, x3T, w["mix_ln_w"][1], w["mix_ln_b"][1], NT)
        phase_peer_route(S, C, x3T, w["peer_w_q"][1], w["peer_keysT"][1], Gh, NT)
        phase_peer_dense(S, C, x3T, w["peer_uT"][1], w["peer_v"][1], Gh, x3, out, None, w["ffn_ln_w"][1], w["ffn_ln_b"][1], NT)
        S.finish()
    return nc


def host_weights(inp):
    f = lambda a: np.ascontiguousarray(np.asarray(a), dtype=np.float32)
    m = {
        "ssm_in_proj": f(inp["ssm_in_proj"][0]), "ssm_convwT": f(np.asarray(inp["ssm_conv_w"][0]).T),
        "ssm_convb2": f(np.asarray(inp["ssm_conv_b"][0]).reshape(48, 128).T), "ssm_dt_bias": f(inp["ssm_dt_bias"][0]),
        "ssm_A_log": f(inp["ssm_A_log"][0]), "ssm_D": f(inp["ssm_D"][0]), "ssm_norm_w": f(inp["ssm_norm_w"][0]),
        "ssm_out_proj": f(inp["ssm_out_proj"][0]), "ret_in_proj": f(inp["ret_in_proj"][0]), "ret_gn_w": f(inp["ret_gn_w"][0]),
        "ret_gn_b": f(inp["ret_gn_b"][0]), "ret_out_proj": f(inp["ret_out_proj"][0]), "mix_ln_w": f(inp["mix_ln_w"]),
        "mix_ln_b": f(inp["mix_ln_b"]), "peer_w_q": f(inp["peer_w_q"]),
        "peer_keysT": f(np.asarray(inp["peer_sub_keys"]).reshape(2, 16, 128, 128).transpose(0, 1, 3, 2)),
        "peer_uT": f(np.asarray(inp["peer_u"]).transpose(0, 2, 1)), "peer_v": f(inp["peer_v"]),
        "ffn_ln_w": f(inp["ffn_ln_w"]), "ffn_ln_b": f(inp["ffn_ln_b"]),
        "identf": np.eye(128, dtype=np.float32), "iota": np.tile(np.arange(128, dtype=np.float32), (128, 1)),
        "negmask": np.tile(np.where(np.arange(128)[:, None] > np.arange(128)[None, :], NEG, 0.0).astype(np.float32), (1, 4)),
    }
    m.update(ret_consts())
    return m


def kernel(**inputs):
    x = np.asarray(inputs["x"], dtype=np.float32)
    positions = np.asarray(inputs["positions"], dtype=np.int32)
    B, L, _ = x.shape
    ncores = 8 if B % 8 == 0 else 1
    per = B // ncores
    NT = per * L
    nc = build_program(NT, L)
    wts = host_weights(inputs)
    in_maps = []
    for c in range(ncores):
        m = dict(wts)
        m["x"] = np.ascontiguousarray(x[c * per:(c + 1) * per].reshape(NT, D))
        m["positions"] = np.ascontiguousarray(positions[c * per:(c + 1) * per])
        in_maps.append(m)
    res = run_bass_kernel_spmd(nc, in_maps, core_ids=list(range(ncores)))
    outs = [np.asarray(r["out"]).reshape(per, L, D) for r in res.results]
    return np.concatenate(outs, axis=0).astype(np.float32)
```

```python
import contextlib
import numpy as np
import concourse.bass as bass
import concourse.mybir as mybir
from concourse.bass_utils import run_bass_kernel_spmd

F32 = mybir.dt.float32
BF16 = mybir.dt.bfloat16
I32 = mybir.dt.int32
U32 = mybir.dt.uint32
AF = mybir.ActivationFunctionType
ALU = mybir.AluOpType
AX = mybir.AxisListType

D = 2048
DN_ALPHA = 4 ** 0.25
LN_EPS = 1e-5
NEG = -30000.0


class Buf:
    __slots__ = ("t", "lw", "rd", "dsem", "dval", "name")

    def __init__(self, t, name=""):
        self.t = t
        self.lw = None
        self.rd = []
        self.dsem = None
        self.dval = 0
        self.name = name

    def __getitem__(self, k):
        return self.t[k]


class Sched:
    ENGS = ("pe", "act", "dve", "pool", "sp")

    def __init__(self, nc):
        self.nc = nc
        self.es = contextlib.ExitStack()
        self.eng = {"pe": nc.tensor, "act": nc.scalar, "dve": nc.vector, "pool": nc.gpsimd, "sp": nc.sync}
        self.sem = {}
        self.cnt = {}
        for e in ("pe", "act", "dve", "pool"):
            self.sem[e] = self.es.enter_context(nc.semaphore("s_" + e))
            self.cnt[e] = 0
        self.waited = {e: {} for e in self.ENGS}
        self.dma_bufs = []
        self.free_dsems = []
        self.nbuf = 0
        self.banks = None
        self.bank_i = 0
        self.bank_pool = (0, 8)

    def sbuf(self, stack, shape, dt, name=None):
        self.nbuf += 1
        name = name or "b"
        t = stack.enter_context(self.nc.sbuf_tensor(f"{name}_{self.nbuf}", list(shape), dt))
        return Buf(t, name)

    def hbm(self, name, shape, dt):
        self.nbuf += 1
        return self.nc.dram_tensor(f"{name}_{self.nbuf}", list(shape), dt).ap()

    def init_banks(self, stack):
        self.banks = []
        for i in range(8):
            t = stack.enter_context(self.nc.psum_tensor(f"bank{i}", [128, 512], F32))
            self.banks.append(Buf(t, f"bank{i}"))

    def bank(self):
        lo, n = self.bank_pool
        b = self.banks[lo + self.bank_i % n]
        self.bank_i += 1
        return b

    def bank_private(self, lo, n):
        self.bank_j = getattr(self, "bank_j", 0) + 1
        return self.banks[lo + self.bank_j % n]

    def _dsem(self, b):
        if b.dsem is None:
            if self.free_dsems:
                b.dsem, b.dval = self.free_dsems.pop()
            else:
                self.nsem = getattr(self, "nsem", 0) + 1
                b.dsem = self.es.enter_context(self.nc.semaphore(f"d{self.nsem}"))
            self.dma_bufs.append(b)
        return b.dsem

    def release_dma_sems(self):
        for b in self.dma_bufs:
            self.free_dsems.append((b.dsem, b.dval))
            b.dsem = None
            b.dval = 0
            b.lw = None if (b.lw is not None and b.lw[2] == "dma") else b.lw
            b.rd = [r for r in b.rd if r[2] != "dma"]
        self.dma_bufs = []

    def _wait(self, e, tok):
        if tok is None:
            return
        sem, val = tok[0], tok[1]
        w = self.waited[e]
        k = id(sem)
        if w.get(k, 0) >= val:
            return
        self.eng[e].wait_ge(sem, val)
        w[k] = val

    def _deps(self, e, reads, writes):
        for b in reads:
            self._wait(e, b.lw)
        for b in writes:
            if b.lw is not None and b.lw[2] != e:
                self._wait(e, b.lw)
            for r in b.rd:
                if r[2] != e:
                    self._wait(e, r)

    def _commit(self, tok, reads, writes):
        for b in reads:
            b.rd.append(tok)
            if len(b.rd) > 16:
                best = {}
                for r in b.rd:
                    k = id(r[0])
                    if k not in best or best[k][1] < r[1]:
                        best[k] = r
                b.rd = list(best.values())
        for b in writes:
            b.lw = tok
            b.rd = []

    def op(self, e, fn, reads=(), writes=(), sig=True):
        self._deps(e, reads, writes)
        ins = fn(self.eng[e])
        if sig:
            self.cnt[e] += 1
            ins.then_inc(self.sem[e], 1)
            tok = (self.sem[e], self.cnt[e], e)
        else:
            tok = (self.sem[e], self.cnt[e] + 1, e)
        self._commit(tok, reads, writes)
        return tok

    def dma(self, q, out, in_, sb, reads=(), writes=(), **kw):
        sem = self._dsem(sb)
        self._deps(q, reads, writes)
        if sb.dval:
            self._wait(q, (sem, sb.dval, "dma"))
        ins = self.eng[q].dma_start(out=out, in_=in_, **kw)
        sb.dval += 16
        ins.then_inc(sem, 16)
        tok = (sem, sb.dval, "dma")
        self._commit(tok, reads, writes)
        return tok

    def barrier(self):
        toks = []
        for e in ("pe", "act", "dve", "pool"):
            if self.cnt[e]:
                toks.append((self.sem[e], self.cnt[e], e))
        for b in self.dma_bufs:
            if b.dval:
                toks.append((b.dsem, b.dval, "dma"))
        for e in self.ENGS:
            for t in toks:
                if t[2] != e:
                    self._wait(e, t)
        self.release_dma_sems()

    def finish(self):
        self.barrier()
        self.es.close()


def cp(S, e, out, in_, reads, writes):
    if e == "act":
        return S.op("act", lambda en: en.copy(out=out, in_=in_), reads, writes)
    return S.op(e, lambda en: en.tensor_copy(out=out, in_=in_), reads, writes)


def tt(S, e, out, a, b, op, reads, writes):
    return S.op(e, lambda en: en.tensor_tensor(out=out, in0=a, in1=b, op=op), reads, writes)


def ts(S, e, out, a, s1, s2, op0, op1, reads, writes):
    if op1 is None:
        return S.op(e, lambda en: en.tensor_scalar(out=out, in0=a, scalar1=s1, scalar2=None, op0=op0), reads, writes)
    return S.op(e, lambda en: en.tensor_scalar(out=out, in0=a, scalar1=s1, scalar2=s2, op0=op0, op1=op1), reads, writes)


def act(S, out, in_, func, reads, writes, **kw):
    return S.op("act", lambda en: en.activation(out=out, in_=in_, func=func, **kw), reads, writes)


def bcast_rows(ap1d, n, parts=128):
    return ap1d.rearrange("(o n) -> o n", o=1).to_broadcast([parts, n])


class Ctx:
    pass


def rsqrt_eps(S, out, in_, in_bufs, out_buf, mul, eps):
    ts(S, "dve", out, in_, float(mul), float(eps), ALU.mult, ALU.add, in_bufs, [out_buf])
    act(S, out, out, AF.Sqrt, [out_buf], [out_buf])
    S.op("dve", lambda en: en.reciprocal(out=out, in_=out), [out_buf], [out_buf])


def transposes_to(S, C, src, src_bufs, ncol_chunks, dst_fn, dst_bufs, dt=BF16, evac="dve"):
    per = 8 if dt == BF16 else 4
    ident = C.identb if dt == BF16 else C.identf
    j = 0
    while j < ncol_chunks:
        n = min(per, ncol_chunks - j)
        bk = S.bank()
        bv = bk.t[:].bitcast(BF16) if dt == BF16 else bk.t[:]
        for i in range(n):
            S.op("pe", lambda en: en.transpose(bv[:, i * 128:(i + 1) * 128], src[:, (j + i) * 128:(j + i + 1) * 128], ident[:]),
                 reads=list(src_bufs) + [C.identb if dt == BF16 else C.identf], writes=[bk], sig=(i == n - 1))
        cp(S, evac, dst_fn(j, n), bv[:, 0:n * 128].rearrange("p (a b) -> p a b", a=n), [bk], dst_bufs)
        j += n


def load_wslab(S, slab, W, KC, f0, fw):
    wv = W.rearrange("(kc p) f -> p kc f", p=128)
    S.dma("pool", slab[:, 0:KC, 0:fw], wv[:, :, f0:f0 + fw], slab, writes=[slab])


def gemm_tok(S, C, xT, KC, TG, W, f0, nf, slabs, epi):
    blocks = []
    f = f0
    while f < f0 + nf:
        fw = min(512, f0 + nf - f)
        blocks.append((f, fw))
        f += fw
    load_wslab(S, slabs[0], W, KC, blocks[0][0], blocks[0][1])
    for bi, (f, fw) in enumerate(blocks):
        if bi + 1 < len(blocks):
            load_wslab(S, slabs[(bi + 1) % 2], W, KC, blocks[bi + 1][0], blocks[bi + 1][1])
        sl = slabs[bi % 2]
        for t_ in range(TG // 128):
            bk = S.bank()
            for kc in range(KC):
                S.op("pe", lambda en: en.matmul(bk[:, 0:fw], lhsT=xT[:, kc, t_ * 128:(t_ + 1) * 128], rhs=sl[:, kc, 0:fw],
                                                start=(kc == 0), stop=(kc == KC - 1)),
                     reads=[xT, sl], writes=[bk], sig=(kc == KC - 1))
            epi(bi, t_, bk, fw)


def gemm_feat(S, C, xT, KC, TG, W, f0, nf, slabs, epi, tb_w=512, fbw=512):
    blocks = []
    f = f0
    while f < f0 + nf:
        fw = min(fbw, f0 + nf - f)
        blocks.append((f, fw))
        f += fw
    load_wslab(S, slabs[0], W, KC, blocks[0][0], blocks[0][1])
    ci = 0
    for bi, (f, fw) in enumerate(blocks):
        if bi + 1 < len(blocks):
            load_wslab(S, slabs[(bi + 1) % 2], W, KC, blocks[bi + 1][0], blocks[bi + 1][1])
        sl = slabs[bi % 2]
        c0 = 0
        while c0 < fw:
            rows = min(128, fw - c0)
            for tb in range((TG + tb_w - 1) // tb_w):
                tw = min(tb_w, TG - tb * tb_w)
                bk = S.bank()
                for kc in range(KC):
                    S.op("pe", lambda en: en.matmul(bk[0:rows, 0:tw], lhsT=sl[:, kc, c0:c0 + rows],
                                                    rhs=xT[:, kc, tb * tb_w:tb * tb_w + tw],
                                                    start=(kc == 0), stop=(kc == KC - 1)),
                         reads=[xT, sl], writes=[bk], sig=(kc == KC - 1))
                epi(ci, tb, bk, rows, tw)
            ci += 1
            c0 += rows


class LNBufs:
    def __init__(self, S, st, lnw, lnb, nb=2, alias_o=False):
        self.nb = nb
        self.xo = [S.sbuf(st, [128, D], F32, "ln_xo") for _ in range(nb)]
        self.r = S.sbuf(st, [128, D], F32, "ln_r")
        self.o = self.xo if alias_o else [S.sbuf(st, [128, D], F32, "ln_o") for _ in range(nb)]
        self.ob = S.sbuf(st, [128, D], BF16, "ln_ob")
        self.oT = [S.sbuf(st, [128, 16, 128], BF16, "ln_oT") for _ in range(nb)]
        self.stats = S.sbuf(st, [128, 4, 6], F32, "ln_stats")
        self.mv = S.sbuf(st, [128, 2], F32, "ln_mv")
        self.rstd = S.sbuf(st, [128, 1], F32, "ln_rstd")
        self.w = S.sbuf(st, [128, D], F32, "ln_w")
        self.b = S.sbuf(st, [128, D], F32, "ln_b")
        S.dma("sp", self.w[:], bcast_rows(lnw, D), self.w, writes=[self.w])
        S.dma("sp", self.b[:], bcast_rows(lnb, D), self.b, writes=[self.b])
        self.i = 0


def ln_prefetch(S, L, xold, g0):
    xo = L.xo[L.i % L.nb]
    S.dma("sp", xo[:], xold[g0:g0 + 128, :], xo, writes=[xo])


def ln_epilogue(S, C, L, h_ap, h_bufs, g0, xnew, xnewT):
    i = L.i
    L.i += 1
    xo = L.xo[i % L.nb]
    o = L.o[i % L.nb]
    oT = L.oT[i % L.nb]
    S.op("dve", lambda en: en.scalar_tensor_tensor(out=L.r[:], in0=xo[:], scalar=float(DN_ALPHA), in1=h_ap,
                                                   op0=ALU.mult, op1=ALU.add), reads=[xo] + list(h_bufs), writes=[L.r])
    for q in range(4):
        S.op("dve", lambda en: en.bn_stats(out=L.stats[:, q, :], in_=L.r[:, q * 512:(q + 1) * 512]), reads=[L.r], writes=[L.stats])
    S.op("dve", lambda en: en.bn_aggr(out=L.mv[:], in_=L.stats[:].rearrange("p a b -> p (a b)")), reads=[L.stats], writes=[L.mv])
    rsqrt_eps(S, L.rstd[:], L.mv[:, 1:2], [L.mv], L.rstd, 1.0, LN_EPS)
    ts(S, "dve", L.r[:], L.r[:], L.mv[:, 0:1], L.rstd[:, 0:1], ALU.subtract, ALU.mult, [L.r, L.mv, L.rstd], [L.r])
    tt(S, "pool", L.r[:], L.r[:], L.w[:], ALU.mult, [L.r, L.w], [L.r])
    tt(S, "pool", o[:], L.r[:], L.b[:], ALU.add, [L.r, L.b], [o])
    S.dma("sp", xnew[g0:g0 + 128, :], o[:], o, reads=[o])
    if xnewT is not None:
        cp(S, "act", L.ob[:], o[:], [o], [L.ob])
        transposes_to(S, C, L.ob, [L.ob], 16, lambda j, n: oT[:, j:j + n, :], [oT], BF16, evac="act")
        S.dma("sp", xnewT.rearrange("(kc p) t -> p kc t", p=128)[:, :, g0:g0 + 128], oT[:], oT, reads=[oT])


def phase_proj_ln(S, C, srcT, K, W, xold, xnew, xnewT, lnw, lnb, NT):
    KC = K // 128
    TG = min(1024, NT)
    nt = TG // 128
    FB = 512
    with contextlib.ExitStack() as st:
        xTt = [S.sbuf(st, [128, KC, 128], BF16, "pl_xT") for _ in range(nt)]
        slabs = [S.sbuf(st, [128, KC, FB], BF16, "pl_slab") for _ in range(2)]
        hst = [S.sbuf(st, [128, D], BF16, "pl_h") for _ in range(nt)]
        L = LNBufs(S, st, lnw, lnb, nb=1, alias_o=True)
        srcv = srcT.rearrange("(kc p) t -> p kc t", p=128)
        ng = NT // TG
        nblk = D // FB
        total_blk = ng * nblk
        load_wslab(S, slabs[0], W, KC, 0, FB)
        nslab = 0
        pending = []
        for g in range(ng):
            for t_ in range(nt):
                S.dma("sp", xTt[t_][:], srcv[:, :, g * TG + t_ * 128:g * TG + (t_ + 1) * 128], xTt[t_], writes=[xTt[t_]])
            for bi in range(nblk):
                if nslab + 1 < total_blk:
                    load_wslab(S, slabs[(nslab + 1) % 2], W, KC, ((bi + 1) % nblk) * FB, FB)
                sl = slabs[nslab % 2]
                nslab += 1
                for t_ in range(nt):
                    bk = S.bank()
                    for kc in range(KC):
                        S.op("pe", lambda en: en.matmul(bk[:, 0:FB], lhsT=xTt[t_][:, kc, :], rhs=sl[:, kc, :],
                                                        start=(kc == 0), stop=(kc == KC - 1)),
                             reads=[xTt[t_], sl], writes=[bk], sig=(kc == KC - 1))
                    if pending:
                        pending.pop(0)()
                    cp(S, "act", hst[t_][:, bi * FB:(bi + 1) * FB], bk[:, 0:FB], [bk], [hst[t_]])
                    if bi == nblk - 1:
                        def ep(g=g, t_=t_):
                            ln_prefetch(S, L, xold, g * TG + t_ * 128)
                            ln_epilogue(S, C, L, hst[t_][:], [hst[t_]], g * TG + t_ * 128, xnew, xnewT)
                        pending.append(ep)
        for ep in pending:
            ep()
        S.barrier()


def phase_make_xT(S, C, x, xT_d, NT):
    with contextlib.ExitStack() as st:
        xin = [S.sbuf(st, [128, D], F32, "mx_in") for _ in range(2)]
        xb = [S.sbuf(st, [128, D], BF16, "mx_b") for _ in range(2)]
        oT = [S.sbuf(st, [128, 16, 128], BF16, "mx_oT") for _ in range(2)]
        dv = xT_d.rearrange("(kc p) t -> p kc t", p=128)
        n = NT // 128
        S.dma("sp", xin[0][:], x[0:128, :], xin[0], writes=[xin[0]])
        for i in range(n):
            if i + 1 < n:
                S.dma("sp", xin[(i + 1) % 2][:], x[(i + 1) * 128:(i + 2) * 128, :], xin[(i + 1) % 2], writes=[xin[(i + 1) % 2]])
            cp(S, "act", xb[i % 2][:], xin[i % 2][:], [xin[i % 2]], [xb[i % 2]])
            o = oT[i % 2]
            transposes_to(S, C, xb[i % 2], [xb[i % 2]], 16, lambda j, n_: o[:, j:j + n_, :], [o], BF16)
            S.dma("sp", dv[:, :, i * 128:(i + 1) * 128], o[:], o, reads=[o])
        S.barrier()


def phase_peer_route(S, C, xT_d, wq, keysT_d, Gh, NT):
    TG = min(512, NT)
    ntile = NT // 128
    with contextlib.ExitStack() as st:
        xT = S.sbuf(st, [128, 16, TG], BF16, "rt_xT")
        qT = S.sbuf(st, [128, 16, TG], BF16, "rt_qT")
        slabs = [S.sbuf(st, [128, 16, 256], BF16, "rt_slab") for _ in range(2)]
        keys = S.sbuf(st, [128, 16, 128], BF16, "rt_keys")
        S.dma("pool", keys[:], keysT_d.rearrange("hc d k -> d hc k"), keys, writes=[keys])

        class TS:
            pass
        tsb = []
        for p in range(2):
            T = TS()
            S.nbuf += 1
            scr = st.enter_context(S.nc.sbuf_tensor(f"rt_scr{p}_{S.nbuf}", [128, 3, 2048], F32))
            T.sc = Buf(scr[:, 0, :].rearrange("p (a b) -> p a b", a=16), "sc")
            T.cand2 = Buf(scr[:, 0, :].rearrange("p (a b) -> p a b", a=8), "cand2")
            T.sc2 = Buf(scr[:, 1, :].rearrange("p (a b) -> p a b", a=16), "sc2")
            T.eq = Buf(scr[:, 1, :].rearrange("p (h k j) -> p h k j", h=8, k=16), "eq")
            T.cand = Buf(scr[:, 2, :].rearrange("p (a b) -> p a b", a=8), "cand")
            T.vals = S.sbuf(st, [128, 16, 16], F32, "rt_vals")
            T.idx = S.sbuf(st, [128, 16, 16], U32, "rt_idx")
            T.idxf = S.sbuf(st, [128, 16, 16], F32, "rt_idxf")
            T.best = S.sbuf(st, [128, 8, 16], F32, "rt_best")
            T.pos = S.sbuf(st, [128, 8, 16], U32, "rt_pos")
            T.pos2 = S.sbuf(st, [128, 8, 16], U32, "rt_pos2")
            T.k1 = S.sbuf(st, [128, 8, 16], F32, "rt_k1")
            T.k2 = S.sbuf(st, [128, 8, 16], F32, "rt_k2")
            T.sel = S.sbuf(st, [128, 3, 128], F32, "rt_sel")
            T.selT = S.sbuf(st, [128, 3, 128], F32, "rt_selT")
            T.nidx7 = S.sbuf(st, [128, 128], F32, "rt_nidx7")
            T.nb = S.sbuf(st, [128, 8], F32, "rt_nb")
            T.Z = S.sbuf(st, [128, 8], F32, "rt_Z")
            T.ex = S.sbuf(st, [128, 8, 16], F32, "rt_ex")
            tsb.append(T)
        NOH = 8
        oh1 = [S.sbuf(st, [128, 128], BF16, "rt_oh1") for _ in range(NOH)]
        ohraw = [S.sbuf(st, [128, 16, 128], BF16, "rt_ohraw") for _ in range(2)]
        oh2b = [S.sbuf(st, [128, 16, 128], BF16, "rt_oh2b") for _ in range(3)]
        Gsb = [S.sbuf(st, [128, 128, 128], BF16, "rt_G") for _ in range(2)]
        xv = xT_d.rearrange("(kc p) t -> p kc t", p=128)

        def gemm_group(g):
            S.dma("sp", xT[:], xv[:, :, g * TG:(g + 1) * TG], xT, writes=[xT])

            def epi(ci, tb, bk, rows, tw):
                cp(S, "act", qT[:, ci, tb * 512:tb * 512 + tw], bk[:, 0:tw], [bk], [qT])
            gemm_feat(S, C, xT, 16, TG, wq, 0, D, slabs, epi, fbw=256)

        def topk_thunks(gi):
            T = tsb[gi % 2]
            g, t_ = divmod(gi, TG // 128)
            tsl = slice(t_ * 128, (t_ + 1) * 128)
            th = []
            if t_ == 0:
                th.append(lambda: gemm_group(g))

            def scores(q4):
                bk = S.bank()
                for j in range(4):
                    hc = q4 * 4 + j
                    S.op("pe", lambda en: en.matmul(bk[:, j * 128:(j + 1) * 128], lhsT=qT[:, hc, tsl], rhs=keys[:, hc, :],
                                                    start=True, stop=True), reads=[qT, keys], writes=[bk], sig=(j == 3))
                cp(S, "act", T.sc[:, q4 * 4:(q4 + 1) * 4, :], bk[:].rearrange("p (a b) -> p a b", a=4), [bk], [T.sc, T.cand2])
            for q4 in range(4):
                th.append(lambda q4=q4: scores(q4))
            for hc in range(16):
                th.append(lambda hc=hc: S.op("dve", lambda en: en.max(out=T.vals[:, hc, 0:8], in_=T.sc[:, hc, :]), [T.sc], [T.vals]))
            for hc in range(16):
                th.append(lambda hc=hc: S.op("dve", lambda en: en.max_index(out=T.idx[:, hc, 0:8], in_max=T.vals[:, hc, 0:8],
                                                                           in_values=T.sc[:, hc, :]), [T.sc, T.vals], [T.idx]))
            for hc in range(16):
                th.append(lambda hc=hc: S.op("dve", lambda en: en.match_replace(out=T.sc2[:, hc, :], in_to_replace=T.vals[:, hc, 0:8],
                                                                               in_values=T.sc[:, hc, :], imm_value=-1e30),
                                             [T.sc, T.vals], [T.sc2]))
            for hc in range(16):
                th.append(lambda hc=hc: S.op("dve", lambda en: en.max(out=T.vals[:, hc, 8:16], in_=T.sc2[:, hc, :]), [T.sc2], [T.vals]))
            for hc in range(16):
                th.append(lambda hc=hc: S.op("dve", lambda en: en.max_index(out=T.idx[:, hc, 8:16], in_max=T.vals[:, hc, 8:16],
                                                                           in_values=T.sc2[:, hc, :]), [T.sc2, T.vals], [T.idx]))
            v4 = T.vals[:].rearrange("p (h c) k -> p h c k", c=2)
            i4 = T.idxf[:].rearrange("p (h c) k -> p h c k", c=2)
            th.append(lambda: cp(S, "dve", T.idxf[:], T.idx[:], [T.idx], [T.idxf]))
            th.append(lambda: tt(S, "pool", T.cand[:].rearrange("p h (a b) -> p h a b", a=16),
                                 v4[:, :, 0, :].unsqueeze(3).to_broadcast([128, 8, 16, 16]),
                                 v4[:, :, 1, :].unsqueeze(2).to_broadcast([128, 8, 16, 16]), ALU.add, [T.vals], [T.cand]))
            for h in range(8):
                th.append(lambda h=h: S.op("dve", lambda en: en.max(out=T.best[:, h, 0:8], in_=T.cand[:, h, :]), [T.cand], [T.best]))
            for h in range(8):
                th.append(lambda h=h: S.op("dve", lambda en: en.max_index(out=T.pos[:, h, 0:8], in_max=T.best[:, h, 0:8],
                                                                         in_values=T.cand[:, h, :]), [T.cand, T.best], [T.pos]))
            for h in range(8):
                th.append(lambda h=h: S.op("dve", lambda en: en.match_replace(out=T.cand2[:, h, :], in_to_replace=T.best[:, h, 0:8],
                                                                             in_values=T.cand[:, h, :], imm_value=-1e30),
                                           [T.cand, T.best], [T.cand2, T.sc]))
            for h in range(8):
                th.append(lambda h=h: S.op("dve", lambda en: en.max(out=T.best[:, h, 8:16], in_=T.cand2[:, h, :]), [T.cand2], [T.best]))
            for h in range(8):
                th.append(lambda h=h: S.op("dve", lambda en: en.max_index(out=T.pos[:, h, 8:16], in_max=T.best[:, h, 8:16],
                                                                         in_values=T.cand2[:, h, :]), [T.cand2, T.best], [T.pos]))
            th.append(lambda: ts(S, "dve", T.pos2[:], T.pos[:], 15, None, ALU.bitwise_and, None, [T.pos], [T.pos2]))
            th.append(lambda: cp(S, "dve", T.k2[:], T.pos2[:], [T.pos2], [T.k2]))
            th.append(lambda: ts(S, "dve", T.pos2[:], T.pos[:], 4, None, ALU.logical_shift_right, None, [T.pos], [T.pos2]))
            th.append(lambda: cp(S, "dve", T.k1[:], T.pos2[:], [T.pos2], [T.k1]))
            for which, kk in ((0, T.k1), (1, T.k2)):
                th.append(lambda kk=kk: tt(S, "dve", T.eq[:], kk[:].unsqueeze(3).to_broadcast([128, 8, 16, 16]),
                                           C.iota16[:].unsqueeze(1).unsqueeze(1).to_broadcast([128, 8, 16, 16]), ALU.is_equal,
                                           [kk, C.iota16], [T.eq, T.sc2]))
                th.append(lambda which=which: tt(S, "pool", T.eq[:], T.eq[:], i4[:, :, which, :].unsqueeze(2).to_broadcast([128, 8, 16, 16]),
                                                 ALU.mult, [T.eq, T.idxf], [T.eq]))
                th.append(lambda which=which: S.op("dve", lambda en: en.tensor_reduce(out=T.sel[:, which, :],
                                                                                     in_=T.eq[:].rearrange("p h k j -> p (h k) j"),
                                                                                     axis=AX.X, op=ALU.add), [T.eq], [T.sel]))
            th.append(lambda: ts(S, "dve", T.nb[:], T.best[:, :, 0], -1.0, None, ALU.mult, None, [T.best], [T.nb]))
            def exps():
                for h in range(8):
                    act(S, T.ex[:, h, :], T.best[:, h, :], AF.Exp, [T.best, T.nb], [T.ex, T.Z], bias=T.nb[:, h:h + 1], accum_out=T.Z[:, h:h + 1])
            th.append(exps)
            th.append(lambda: S.op("dve", lambda en: en.reciprocal(out=T.Z[:], in_=T.Z[:]), [T.Z], [T.Z]))
            th.append(lambda: ts(S, "dve", T.Z[:], T.Z[:], 0.886226925452758, None, ALU.mult, None, [T.Z], [T.Z]))
            th.append(lambda: tt(S, "dve", T.sel[:, 2, :].rearrange("p (h k) -> p h k", h=8), T.ex[:],
                                 T.Z[:].unsqueeze(2).to_broadcast([128, 8, 16]), ALU.mult, [T.ex, T.Z], [T.sel]))

            def tr():
                bk = S.bank()
                for j in range(3):
                    S.op("pe", lambda en: en.transpose(bk[:, j * 128:(j + 1) * 128], T.sel[:, j, :], C.identf[:]),
                         reads=[T.sel, C.identf], writes=[bk], sig=(j == 2))
                cp(S, "act", T.selT[:], bk[:, 0:384].rearrange("p (a b) -> p a b", a=3), [bk], [T.selT])
                ts(S, "dve", T.nidx7[:], T.selT[:, 0, :], -7.0, None, ALU.mult, None, [T.selT], [T.nidx7])
            th.append(tr)
            return th

        def pertoken_thunks(gi):
            T = tsb[gi % 2]
            Gs = Gsb[gi % 2]
            pend = []
            th = []

            def evac(pb, p4):
                cp(S, "act" if p4 % 2 == 0 else "dve", Gs[:, :, p4 * 4:(p4 + 1) * 4], pb[:].rearrange("p (i t) -> p i t", t=4), [pb], [Gs])

            def batch(t16):
                raw = ohraw[t16 % 2]
                ob = oh2b[t16 % 3]
                tt(S, "dve", raw[:], C.iota[:].unsqueeze(1).to_broadcast([128, 16, 128]),
                   T.selT[:, 1, t16 * 16:(t16 + 1) * 16].unsqueeze(2).to_broadcast([128, 16, 128]), ALU.is_equal, [C.iota, T.selT], [raw])
                tt(S, "pool", ob[:], raw[:], T.selT[:, 2, t16 * 16:(t16 + 1) * 16].unsqueeze(2).to_broadcast([128, 16, 128]), ALU.mult,
                   [raw, T.selT], [ob])

            def grp(t4):
                if t4 == 0:
                    batch(0)
                    batch(1)
                if t4 % 4 == 0 and t4 // 4 + 2 < 8:
                    batch(t4 // 4 + 2)
                ob = oh2b[(t4 // 4) % 3]
                bk = S.bank_private(0, 4)
                for j in range(4):
                    tk = t4 * 4 + j
                    o1 = oh1[tk % NOH]
                    act(S, o1[:], C.iota[:], AF.Derivative_Erf, [C.iota, T.nidx7], [o1], scale=7.0, bias=T.nidx7[:, tk:tk + 1])
                    S.op("pe", lambda en: en.matmul(bk[:].rearrange("p (i j) -> p j i", j=4)[:, j, :], lhsT=ob[:, tk % 16, :], rhs=o1[:],
                                                    start=True, stop=True), reads=[o1, ob], writes=[bk], sig=(j == 3))
                pend.append((bk, t4))
                if len(pend) > 2:
                    evac(*pend.pop(0))
            for t4 in range(32):
                th.append(lambda t4=t4: grp(t4))

            def fin():
                for pb, p4 in pend:
                    evac(pb, p4)
                for q in range(2):
                    S.dma("sp", Gh[gi, :, q * 64:(q + 1) * 64, :], Gs[:, q * 64:(q + 1) * 64, :], Gs, reads=[Gs])
            th.append(fin)
            return th

        S.bank_pool = (4, 4)
        for f in topk_thunks(0):
            f()
        for gi in range(ntile):
            A = pertoken_thunks(gi)
            B = topk_thunks(gi + 1) if gi + 1 < ntile else []
            ia = ib = 0
            ratio = (len(B) + len(A) - 1) // len(A) if B else 0
            while ia < len(A) or ib < len(B):
                if ia < len(A):
                    A[ia]()
                    ia += 1
                for _ in range(ratio):
                    if ib < len(B):
                        B[ib]()
                        ib += 1
                if ia >= len(A):
                    while ib < len(B):
                        B[ib]()
                        ib += 1
        S.bank_pool = (0, 8)
        S.barrier()


def phase_peer_dense(S, C, xT_d, uT_d, v_d, Gh, xold, xnew, xnewT, lnw, lnb, NT):
    TG = min(1024, NT)
    EG = 4
    NEG_ = 128 // EG
    ntile = TG // 128
    with contextlib.ExitStack() as st:
        xT = S.sbuf(st, [128, 16, TG], BF16, "pd_xT")
        acc = S.sbuf(st, [128, ntile, D], F32, "pd_acc")
        gl = [S.sbuf(st, [128, 512], BF16, "pd_gl") for _ in range(2)]
        xv = xT_d.rearrange("(kc p) t -> p kc t", p=128)
        uv = uT_d.rearrange("(kc p) e -> p kc e", p=128)
        vv = v_d.rearrange("(a p) d -> p a d", p=128)
        ngl = 0
        for g in range(NT // TG):
            S.dma("sp", xT[:], xv[:, :, g * TG:(g + 1) * TG], xT, writes=[xT])
            with contextlib.ExitStack() as st2:
                us = [S.sbuf(st2, [128, 16, EG * 128], BF16, "pd_u") for _ in range(2)]
                vs = [S.sbuf(st2, [128, EG, D], BF16, "pd_v") for _ in range(2)]
                Gs = [S.sbuf(st2, [128, ntile, EG, 128], BF16, "pd_G") for _ in range(2)]
                GH = [S.sbuf(st2, [128, EG, TG], BF16, "pd_GH") for _ in range(2)]

                def load(eg):
                    b = eg % 2
                    S.dma("pool", us[b][:], uv[:, :, eg * EG * 128:(eg + 1) * EG * 128], us[b], writes=[us[b]])
                    S.dma("pool", vs[b][:], vv[:, eg * EG:(eg + 1) * EG, :], vs[b], writes=[vs[b]])
                    S.dma("sp", Gs[b][:], Gh[g * ntile:(g + 1) * ntile, :, eg * EG:(eg + 1) * EG, :].rearrange("a p j t -> p a j t"),
                          Gs[b], writes=[Gs[b]])

                load(0)
                for eg in range(NEG_):
                    if eg + 1 < NEG_:
                        load(eg + 1)
                    b = eg % 2
                    for j in range(EG):
                        for tb in range((TG + 511) // 512):
                            tw = min(512, TG - tb * 512)
                            bk = S.bank()
                            for kc in range(16):
                                S.op("pe", lambda en: en.matmul(bk[:, 0:tw], lhsT=us[b][:, kc, j * 128:(j + 1) * 128],
                                                                rhs=xT[:, kc, tb * 512:tb * 512 + tw], start=(kc == 0), stop=(kc == 15)),
                                     reads=[us[b], xT], writes=[bk], sig=(kc == 15))
                            glb = gl[ngl % 2]
                            ngl += 1
                            act(S, glb[:, 0:tw], bk[:, 0:tw], AF.Gelu, [bk], [glb])
                            na = tw // 128
                            tt(S, "pool", GH[b][:, j, tb * 512:tb * 512 + tw].rearrange("p (a t) -> p a t", a=na),
                               glb[:, 0:tw].rearrange("p (a t) -> p a t", a=na), Gs[b][:, tb * 4:tb * 4 + na, j, :], ALU.mult,
                               [glb, Gs[b]], [GH[b]])
                    for t_ in range(ntile):
                        for db in range(4):
                            bk = S.bank()
                            for j in range(EG):
                                S.op("pe", lambda en: en.matmul(bk[:], lhsT=GH[b][:, j, t_ * 128:(t_ + 1) * 128],
                                                                rhs=vs[b][:, j, db * 512:(db + 1) * 512], start=(j == 0), stop=(j == EG - 1)),
                                     reads=[GH[b], vs[b]], writes=[bk], sig=(j == EG - 1))
                            a = acc[:, t_, db * 512:(db + 1) * 512]
                            if eg == 0:
                                cp(S, "dve", a, bk[:], [bk], [acc])
                            else:
                                tt(S, "dve", a, a, bk[:], ALU.add, [acc, bk], [acc])
                S.barrier()
            with contextlib.ExitStack() as st3:
                L = LNBufs(S, st3, lnw, lnb, nb=2)
                ln_prefetch(S, L, xold, g * TG)
                for t_ in range(ntile):
                    if t_ + 1 < ntile:
                        L.i += 1
                        ln_prefetch(S, L, xold, g * TG + (t_ + 1) * 128)
                        L.i -= 1
                    ln_epilogue(S, C, L, acc[:, t_, :], [acc], g * TG + t_ * 128, xnew, xnewT)
                S.barrier()


def setup_consts(S, C, st, identf_d, iota_d):
    C.identf = S.sbuf(st, [128, 128], F32, "identf")
    C.identb = S.sbuf(st, [128, 128], BF16, "identb")
    C.iota = S.sbuf(st, [128, 128], F32, "iota")
    C.iota16 = S.sbuf(st, [128, 16], F32, "iota16")
    S.dma("sp", C.identf[:], identf_d, C.identf, writes=[C.identf])
    S.dma("sp", C.iota[:], iota_d, C.iota, writes=[C.iota])
    cp(S, "dve", C.identb[:], C.identf[:], [C.identf], [C.identb])
    cp(S, "dve", C.iota16[:], C.iota[:, 0:16], [C.iota], [C.iota16])
    C.onecol = S.sbuf(st, [128, 1], F32, "onecol")
    S.op("dve", lambda en: en.memset(C.onecol[:], 1.0), [], [C.onecol])


def phase_ssm_inproj(S, C, xT_d, W, convwT, convb2, dtbias, Alog, xbcT_d, z_d, dtT_d, dAT_d, NT, LSEQ):
    with contextlib.ExitStack() as st:
        xT = S.sbuf(st, [128, 16, LSEQ], BF16, "si_xT")
        slabs = [S.sbuf(st, [128, 16, 512], BF16, "si_slab") for _ in range(2)]
        pre = [S.sbuf(st, [128, 3 + LSEQ], F32, "si_pre") for _ in range(2)]
        accb = S.sbuf(st, [128, LSEQ], F32, "si_acc")
        outb = [S.sbuf(st, [128, LSEQ], BF16, "si_out") for _ in range(2)]
        zst = [S.sbuf(st, [128, 512], BF16, "si_z") for _ in range(4)]
        cw = S.sbuf(st, [128, 48, 4], F32, "si_cw")
        cb = S.sbuf(st, [128, 48], F32, "si_cb")
        dtb = S.sbuf(st, [64, 1], F32, "si_dtb")
        Aneg = S.sbuf(st, [64, 1], F32, "si_A")
        dtr = S.sbuf(st, [64, LSEQ], F32, "si_dtr")
        dta = S.sbuf(st, [64, LSEQ], F32, "si_dta")
        dtl = S.sbuf(st, [64, LSEQ], F32, "si_dtl")
        S.dma("sp", cw[:], convwT.rearrange("(ci p) j -> p ci j", p=128), cw, writes=[cw])
        S.dma("sp", cb[:], convb2, cb, writes=[cb])
        S.dma("sp", dtb[:], dtbias.rearrange("(p o) -> p o", o=1), dtb, writes=[dtb])
        S.dma("sp", Aneg[:], Alog.rearrange("(p o) -> p o", o=1), Aneg, writes=[Aneg])
        act(S, Aneg[:], Aneg[:], AF.Exp, [Aneg], [Aneg])
        ts(S, "dve", Aneg[:], Aneg[:], -1.0, None, ALU.mult, None, [Aneg], [Aneg])
        for p_ in pre:
            S.op("dve", lambda en: en.memset(p_[:, 0:3], 0.0), [], [p_])
        xv = xT_d.rearrange("(kc p) t -> p kc t", p=128)
        nz = [0]
        for sq in range(NT // LSEQ):
            s0 = sq * LSEQ
            S.dma("sp", xT[:], xv[:, :, s0:s0 + LSEQ], xT, writes=[xT])
            ntb = (LSEQ + 511) // 512

            def epi_f(ci, tb, bk, rows, tw, s0=s0):
                if ci < 48:
                    pr = pre[ci % 2]
                    cp(S, "act", pr[:, 3 + tb * 512:3 + tb * 512 + tw], bk[:, 0:tw], [bk], [pr])
                    if tb == ntb - 1:
                        ob = outb[ci % 2]
                        ts(S, "dve", accb[:], pr[:, 3:3 + LSEQ], cw[:, ci, 3:4], None, ALU.mult, None, [pr, cw], [accb])
                        for j in (2, 1, 0):
                            S.op("dve", lambda en: en.scalar_tensor_tensor(out=accb[:], in0=pr[:, j:j + LSEQ], scalar=cw[:, ci, j:j + 1],
                                                                           in1=accb[:], op0=ALU.mult, op1=ALU.add), [pr, cw, accb], [accb])
                        act(S, ob[:], accb[:], AF.Silu, [accb, cb], [ob], bias=cb[:, ci:ci + 1])
                        S.dma("sp", xbcT_d[ci * 128:(ci + 1) * 128, s0:s0 + LSEQ], ob[:], ob, reads=[ob])
                else:
                    cp(S, "act", dtr[:, tb * 512:tb * 512 + tw], bk[0:64, 0:tw], [bk], [dtr])
                    if tb == ntb - 1:
                        ts(S, "dve", dtr[:], dtr[:], dtb[:, 0:1], None, ALU.add, None, [dtr, dtb], [dtr])
                        act(S, dta[:], dtr[:], AF.Abs, [dtr], [dta])
                        act(S, dta[:], dta[:], AF.Exp, [dta], [dta], scale=-1.0)
                        act(S, dtl[:], dta[:], AF.Ln, [dta, C.onecol], [dtl], bias=C.onecol[0:64, 0:1])
                        S.op("dve", lambda en: en.scalar_tensor_tensor(out=dta[:], in0=dtr[:], scalar=0.0, in1=dtl[:],
                                                                       op0=ALU.max, op1=ALU.add), [dtr, dtl], [dta])
                        ts(S, "dve", dtl[:], dta[:], Aneg[:, 0:1], None, ALU.mult, None, [dta, Aneg], [dtl])
                        S.dma("sp", dtT_d[:, s0:s0 + LSEQ], dta[:], dta, reads=[dta])
                        S.dma("sp", dAT_d[:, s0:s0 + LSEQ], dtl[:], dtl, reads=[dtl])
            gemm_feat(S, C, xT, 16, LSEQ, W, 4096, 6144 + 64, slabs, epi_f)

            def epi_z(fb, t_, bk, fw, s0=s0):
                zb = zst[nz[0] % 4]
                nz[0] += 1
                cp(S, "act", zb[:, 0:fw], bk[:, 0:fw], [bk], [zb])
                S.dma("sp", z_d[s0 + t_ * 128:s0 + (t_ + 1) * 128, fb * 512:fb * 512 + fw], zb[:, 0:fw], zb, reads=[zb])
            gemm_tok(S, C, xT, 16, LSEQ, W, 0, 4096, slabs, epi_z)
        S.barrier()


def phase_ssd(S, C, xbcT_d, z_d, dtT_d, dAT_d, ssmD, normw, negmask_d, ynT_d, NT, LSEQ):
    with contextlib.ExitStack() as st:
        xsT = [S.sbuf(st, [128, 32, 128], BF16, "sd_xsT") for _ in range(2)]
        BCT = [S.sbuf(st, [128, 16, 128], BF16, "sd_BCT") for _ in range(2)]
        zt = S.sbuf(st, [128, 4096], BF16, "sd_z")
        dsm = [S.sbuf(st, [64, 2, 128], F32, "sd_dsm") for _ in range(3)]
        cumT = S.sbuf(st, [64, 128], F32, "sd_cumT")
        cumhl = S.sbuf(st, [64, 2, 128], BF16, "sd_cumhl")
        winT = S.sbuf(st, [64, 128], F32, "sd_winT")
        elT = S.sbuf(st, [64, 1], F32, "sd_elT")
        diagE = S.sbuf(st, [64, 64], F32, "sd_diagE")
        tm = S.sbuf(st, [128, 3, 64], F32, "sd_tm")
        ncum = S.sbuf(st, [128, 64], F32, "sd_ncum")
        ecum = S.sbuf(st, [128, 64], F32, "sd_ecum")
        elbc = S.sbuf(st, [128, 64], F32, "sd_elbc")
        sel = S.sbuf(st, [64, 64, 128], BF16, "sd_sel")
        LT = S.sbuf(st, [128, 64, 128], BF16, "sd_LT")
        xs = S.sbuf(st, [128, 64, 64], BF16, "sd_xs")
        Btm = S.sbuf(st, [128, 8, 128], BF16, "sd_Btm")
        xdt = S.sbuf(st, [128, 64, 64], BF16, "sd_xdt")
        xw = S.sbuf(st, [128, 64, 64], BF16, "sd_xw")
        MT = [S.sbuf(st, [128, 8, 128], BF16, "sd_MT") for _ in range(2)]
        Y = S.sbuf(st, [128, 64, 64], F32, "sd_Y")
        t1 = S.sbuf(st, [128, 8, 64], F32, "sd_t1")
        state = S.sbuf(st, [128, 8, 512], F32, "sd_state")
        stbf = S.sbuf(st, [128, 8, 512], BF16, "sd_stbf")
        sz = S.sbuf(st, [128, 4096], BF16, "sd_sz")
        nw = S.sbuf(st, [128, 4096], BF16, "sd_nw")
        Dbc = S.sbuf(st, [128, 64], F32, "sd_Dbc")
        nm = S.sbuf(st, [128, 512], BF16, "sd_negmask")
        nmf = S.sbuf(st, [128, 512], F32, "sd_negmaskf")
        ones64 = S.sbuf(st, [64, 128], F32, "sd_ones64")
        ss = S.sbuf(st, [128, 8], F32, "sd_ss")
        junk = S.sbuf(st, [128, 512], BF16, "sd_junk")
        ynb = S.sbuf(st, [128, 4096], BF16, "sd_ynb")
        ynT = S.sbuf(st, [128, 32, 128], BF16, "sd_ynT")
        S.dma("pool", nw[:], bcast_rows(normw, 4096), nw, writes=[nw])
        S.dma("sp", Dbc[:], bcast_rows(ssmD, 64), Dbc, writes=[Dbc])
        S.dma("sp", nmf[:], negmask_d, nmf, writes=[nmf])
        cp(S, "dve", nm[:], nmf[:], [nmf], [nm])
        S.op("dve", lambda en: en.memset(ones64[:], 1.0), [], [ones64])
        cp(S, "dve", sel[:], C.identf[0:64, 0:64].unsqueeze(2).to_broadcast([64, 64, 128]), [C.identf], [sel])
        xv = xbcT_d.rearrange("(c p) t -> p c t", p=128)
        ynv = ynT_d.rearrange("(c p) t -> p c t", p=128)
        nch = LSEQ // 128
        ntot = (NT // LSEQ) * nch

        def pos_of(i):
            sq, c = divmod(i, nch)
            return sq, c, sq * LSEQ + c * 128

        def load(i):
            _, _, g0 = pos_of(i)
            b = i % 2
            S.dma("sp", xsT[b][:], xv[:, 0:32, g0:g0 + 128], xsT[b], writes=[xsT[b]])
            S.dma("sp", BCT[b][:], xv[:, 32:48, g0:g0 + 128], BCT[b], writes=[BCT[b]])

        def load_small(i):
            _, _, g0 = pos_of(i)
            d = dsm[i % 3]
            S.dma("sp", d[:, 0, :], dtT_d[:, g0:g0 + 128], d, writes=[d])
            S.dma("sp", d[:, 1, :], dAT_d[:, g0:g0 + 128], d, writes=[d])

        def prep(i):
            d = dsm[i % 3]
            S.op("dve", lambda en: en.tensor_tensor_scan(out=cumT[:], data0=ones64[:], data1=d[:, 1, :], initial=0.0,
                                                         op0=ALU.mult, op1=ALU.add), [ones64, d], [cumT])
            cp(S, "act", cumhl[:, 0, :], cumT[:], [cumT], [cumhl])
            tt(S, "dve", cumhl[:, 1, :], cumT[:], cumhl[:, 0, :], ALU.subtract, [cumT, cumhl], [cumhl])
            act(S, winT[:], cumT[:], AF.Exp, [cumT], [winT], scale=-1.0, bias=cumT[:, 127:128])
            act(S, elT[:], cumT[:, 127:128], AF.Exp, [cumT], [elT])
            ts(S, "dve", diagE[:], C.identf[0:64, 0:64], elT[:, 0:1], None, ALU.mult, None, [C.identf, elT], [diagE])
            bk = S.bank()
            for j, (src, sb_) in enumerate(((d[:, 0, :], d), (cumT[:], cumT), (winT[:], winT))):
                S.op("pe", lambda en: en.transpose(bk[:, j * 64:(j + 1) * 64], src, C.identf[0:64, 0:64]),
                     reads=[sb_, C.identf], writes=[bk], sig=(j == 2))
            cp(S, "act", tm[:], bk[:, 0:192].rearrange("p (a b) -> p a b", a=3), [bk], [tm])
            ts(S, "dve", ncum[:], tm[:, 1, :], -1.0, None, ALU.mult, None, [tm], [ncum])
            act(S, ecum[:], tm[:, 1, :], AF.Exp, [tm], [ecum])
            bk = S.bank()
            S.op("pe", lambda en: en.matmul(bk[:, 0:64], lhsT=ones64[:], rhs=diagE[:], start=True, stop=True),
                 reads=[ones64, diagE], writes=[bk])
            cp(S, "act", elbc[:], bk[:, 0:64], [bk], [elbc])
            for q in range(16):
                bk = S.bank()
                first = True
                for j in range(4):
                    h = q * 4 + j
                    for hl in range(2):
                        S.op("pe", lambda en: en.matmul(bk[:, j * 128:(j + 1) * 128], lhsT=sel[:, h, :], rhs=cumhl[:, hl, :],
                                                        start=first, stop=False), reads=[sel, cumhl], writes=[bk], sig=False)
                        first = False
                S.op("pe", lambda en: en.matmul(bk[:], lhsT=C.identb[:], rhs=nm[:], start=False, stop=True),
                     reads=[C.identb, nm], writes=[bk])
                for j in range(4):
                    h = q * 4 + j
                    act(S, LT[:, h, :], bk[:, j * 128:(j + 1) * 128], AF.Exp, [bk, ncum], [LT], bias=ncum[:, h:h + 1])

        def head(i):
            b = i % 2
            transposes_to(S, C, xsT[b][:].rearrange("p a b -> p (a b)"), [xsT[b]], 32,
                          lambda j, n: xs[:].rearrange("p h d -> p (h d)")[:, j * 128:(j + n) * 128].rearrange("p (a b) -> p a b", a=n), [xs], BF16)
            transposes_to(S, C, BCT[b][:].rearrange("p a b -> p (a b)"), [BCT[b]], 8, lambda j, n: Btm[:, j:j + n, :], [Btm], BF16)
            tt(S, "dve", xdt[:], xs[:], tm[:, 0, :].unsqueeze(2).to_broadcast([128, 64, 64]), ALU.mult, [xs, tm], [xdt])
            tt(S, "pool", xw[:], xdt[:], tm[:, 2, :].unsqueeze(2).to_broadcast([128, 64, 64]), ALU.mult, [xdt, tm], [xw])

        def groups(i):
            b = i % 2
            for g in range(8):
                hs = slice(8 * g, 8 * g + 8)
                bcb = S.bank()
                S.op("pe", lambda en: en.matmul(bcb[:, 0:128], lhsT=BCT[b][:, g, :], rhs=BCT[b][:, 8 + g, :], start=True, stop=True),
                     reads=[BCT[b]], writes=[bcb])
                M = MT[g % 2]
                tt(S, "dve", M[:], LT[:, hs, :], bcb[:, 0:128].unsqueeze(1).to_broadcast([128, 8, 128]), ALU.mult, [LT, bcb], [M])
                byd = S.bank()
                for r in range(8):
                    S.op("pe", lambda en: en.matmul(byd[:, r * 64:(r + 1) * 64], lhsT=M[:, r, :], rhs=xdt[:, 8 * g + r, :], start=True, stop=True),
                         reads=[M, xdt], writes=[byd], sig=(r == 7))
                byo = S.bank()
                S.op("pe", lambda en: en.matmul(byo[:], lhsT=BCT[b][:, 8 + g, :], rhs=stbf[:, g, :], start=True, stop=True),
                     reads=[BCT[b], stbf], writes=[byo])
                tt(S, "dve", t1[:], byo[:].rearrange("p (r d) -> p r d", r=8), ecum[:, hs].unsqueeze(2).to_broadcast([128, 8, 64]), ALU.mult,
                   [byo, ecum], [t1])
                tt(S, "dve", Y[:, hs, :], byd[:].rearrange("p (r d) -> p r d", r=8), t1[:], ALU.add, [byd, t1], [Y])
                bns = S.bank()
                S.op("pe", lambda en: en.matmul(bns[:], lhsT=Btm[:, g, :], rhs=xw[:, hs, :], start=True, stop=True),
                     reads=[Btm, xw], writes=[bns])
                sg = state[:, g, :].rearrange("p (r d) -> p r d", r=8)
                tt(S, "pool", sg, sg, elbc[:, hs].unsqueeze(2).to_broadcast([128, 8, 64]), ALU.mult, [state, elbc], [state])
                tt(S, "dve", state[:, g, :], state[:, g, :], bns[:], ALU.add, [state, bns], [state])
                cp(S, "act", stbf[:, g, :], state[:, g, :], [state], [stbf])

        def tail(i):
            tt(S, "pool", xdt[:], xs[:], Dbc[:].unsqueeze(2).to_broadcast([128, 64, 64]), ALU.mult, [xs, Dbc], [xdt])
            Yf = Y[:].rearrange("p h d -> p (h d)")
            tt(S, "dve", Yf, Yf, xdt[:].rearrange("p h d -> p (h d)"), ALU.add, [Y, xdt], [Y])
            act(S, sz[:], zt[:], AF.Silu, [zt], [sz])
            tt(S, "dve", Yf, Yf, sz[:], ALU.mult, [Y, sz], [Y])
            for g in range(8):
                act(S, junk[:], Yf[:, g * 512:(g + 1) * 512], AF.Square, [Y], [junk, ss], accum_out=ss[:, g:g + 1])
            rsqrt_eps(S, ss[:], ss[:], [ss], ss, 1.0 / 512.0, LN_EPS)
            Y8 = Y[:].rearrange("p (g r) d -> p g (r d)", g=8)
            tt(S, "dve", Y8, Y8, ss[:].unsqueeze(2).to_broadcast([128, 8, 512]), ALU.mult, [Y, ss], [Y])
            tt(S, "pool", ynb[:], Yf, nw[:], ALU.mult, [Y, nw], [ynb])

        def out(i):
            _, _, g0 = pos_of(i)
            transposes_to(S, C, ynb, [ynb], 32, lambda j, n: ynT[:, j:j + n, :], [ynT], BF16, evac="act")
            S.dma("sp", ynv[:, :, g0:g0 + 128], ynT[:], ynT, reads=[ynT])

        load(0)
        load_small(0)
        if ntot > 1:
            load_small(1)
        prep(0)
        for i in range(ntot):
            sq, c, g0 = pos_of(i)
            if i + 1 < ntot:
                load(i + 1)
            if i + 2 < ntot:
                load_small(i + 2)
            S.dma("sp", zt[:], z_d[g0:g0 + 128, :], zt, writes=[zt])
            if c == 0:
                S.op("pool", lambda en: en.memset(state[:], 0.0), [], [state])
                S.op("pool", lambda en: en.memset(stbf[:], 0.0), [], [stbf])
            head(i)
            groups(i)
            if i > 0:
                out(i - 1)
            if i + 1 < ntot:
                prep(i + 1)
            tail(i)
        out(ntot - 1)
        S.barrier()


TWO_PI = 6.283185307179586
PI = 3.141592653589793


def _sin_table(S, out_buf, ang_buf, shift, tmp, tmpi, mulc):
    ts(S, "dve", tmp[:], ang_buf[:], float(shift), 1.0 / TWO_PI, ALU.add, ALU.mult, [ang_buf], [tmp])
    cp(S, "dve", tmpi[:], tmp[:], [tmp], [tmpi])
    cp(S, "dve", tmp[:], tmpi[:], [tmpi], [tmp])
    ts(S, "dve", tmp[:], tmp[:], -TWO_PI, float(shift), ALU.mult, ALU.add, [tmp], [tmp])
    tt(S, "dve", out_buf[:], tmp[:], ang_buf[:], ALU.add, [tmp, ang_buf], [out_buf])
    ts(S, "dve", tmp[:], out_buf[:], PI, -TWO_PI, ALU.is_gt, ALU.mult, [out_buf], [tmp])
    tt(S, "dve", out_buf[:], out_buf[:], tmp[:], ALU.add, [out_buf, tmp], [out_buf])
    ts(S, "dve", tmp[:], out_buf[:], -PI, TWO_PI, ALU.is_lt, ALU.mult, [out_buf], [tmp])
    tt(S, "dve", out_buf[:], out_buf[:], tmp[:], ALU.add, [out_buf, tmp], [out_buf])
    act(S, out_buf[:], out_buf[:], AF.Sin, [out_buf], [out_buf])
    if mulc != 1.0:
        ts(S, "dve", out_buf[:], out_buf[:], float(mulc), None, ALU.mult, None, [out_buf], [out_buf])


def phase_ret_inproj(S, C, xT_d, W, pos_d, invfreq_d, qkT_d, vg_d, NT, LSEQ):
    with contextlib.ExitStack() as st:
        xT = S.sbuf(st, [128, 16, LSEQ], BF16, "ri_xT")
        slabs = [S.sbuf(st, [128, 16, 512], BF16, "ri_slab") for _ in range(2)]
        posi = S.sbuf(st, [128, LSEQ], I32, "ri_posi")
        ang = S.sbuf(st, [128, LSEQ], F32, "ri_ang")
        tmp = S.sbuf(st, [128, LSEQ], F32, "ri_tmp")
        tmpi = S.sbuf(st, [128, LSEQ], I32, "ri_tmpi")
        cosk = S.sbuf(st, [128, LSEQ], F32, "ri_cosk")
        sink = S.sbuf(st, [128, LSEQ], F32, "ri_sink")
        cosq = S.sbuf(st, [128, LSEQ], F32, "ri_cosq")
        sinq = S.sbuf(st, [128, LSEQ], F32, "ri_sinq")
        invf = S.sbuf(st, [128, 1], F32, "ri_invf")
        t1s = S.sbuf(st, [128, LSEQ], F32, "ri_t1s")
        ra = S.sbuf(st, [128, 512], F32, "ri_ra")
        rb = S.sbuf(st, [128, 512], F32, "ri_rb")
        rc = S.sbuf(st, [128, 512], F32, "ri_rc")
        rd = S.sbuf(st, [128, 512], F32, "ri_rd")
        o1 = [S.sbuf(st, [128, 512], BF16, "ri_o1") for _ in range(2)]
        o2 = [S.sbuf(st, [128, 512], BF16, "ri_o2") for _ in range(2)]
        vst = [S.sbuf(st, [128, 512], BF16, "ri_v") for _ in range(4)]
        S.dma("sp", invf[:], invfreq_d, invf, writes=[invf])
        xv = xT_d.rearrange("(kc p) t -> p kc t", p=128)
        cnt = [0, 0]
        for sq in range(NT // LSEQ):
            s0 = sq * LSEQ
            S.dma("sp", xT[:], xv[:, :, s0:s0 + LSEQ], xT, writes=[xT])
            S.dma("sp", posi[:], pos_d[sq:sq + 1, :].to_broadcast([128, LSEQ]), posi, writes=[posi])
            cp(S, "dve", ang[:], posi[:], [posi], [ang])
            ts(S, "dve", ang[:], ang[:], invf[:, 0:1], None, ALU.mult, None, [ang, invf], [ang])
            _sin_table(S, sink, ang, 0.0, tmp, tmpi, 1.0)
            _sin_table(S, cosk, ang, PI / 2, tmp, tmpi, 1.0)
            ts(S, "dve", sinq[:], sink[:], 1.0 / 16.0, None, ALU.mult, None, [sink], [sinq])
            ts(S, "dve", cosq[:], cosk[:], 1.0 / 16.0, None, ALU.mult, None, [cosk], [cosq])

            def epi_f(ci, tb, bk, rows, tw, s0=s0):
                cs, sn = (cosq, sinq) if ci < 16 else (cosk, sink)
                sl = slice(tb * 512, tb * 512 + tw)
                if ci % 2 == 0:
                    cp(S, "act", t1s[:, sl], bk[:, 0:tw], [bk], [t1s])
                else:
                    k = cnt[0] % 2
                    cnt[0] += 1
                    tt(S, "pool", ra[:, 0:tw], t1s[:, sl], cs[:, sl], ALU.mult, [t1s, cs], [ra])
                    tt(S, "dve", rb[:, 0:tw], bk[:, 0:tw], sn[:, sl], ALU.mult, [bk, sn], [rb])
                    tt(S, "dve", o1[k][:, 0:tw], ra[:, 0:tw], rb[:, 0:tw], ALU.subtract, [ra, rb], [o1[k]])
                    tt(S, "dve", rc[:, 0:tw], bk[:, 0:tw], cs[:, sl], ALU.mult, [bk, cs], [rc])
                    tt(S, "pool", rd[:, 0:tw], t1s[:, sl], sn[:, sl], ALU.mult, [t1s, sn], [rd])
                    tt(S, "dve", o2[k][:, 0:tw], rc[:, 0:tw], rd[:, 0:tw], ALU.add, [rc, rd], [o2[k]])
                    S.dma("sp", qkT_d[(ci - 1) * 128:ci * 128, s0 + tb * 512:s0 + tb * 512 + tw], o1[k][:, 0:tw], o1[k], reads=[o1[k]])
                    S.dma("sp", qkT_d[ci * 128:(ci + 1) * 128, s0 + tb * 512:s0 + tb * 512 + tw], o2[k][:, 0:tw], o2[k], reads=[o2[k]])
            gemm_feat(S, C, xT, 16, LSEQ, W, 0, 4096, slabs, epi_f)

            def epi_v(fb, t_, bk, fw, s0=s0):
                zb = vst[cnt[1] % 4]
                cnt[1] += 1
                cp(S, "act", zb[:, 0:fw], bk[:, 0:fw], [bk], [zb])
                S.dma("sp", vg_d[s0 + t_ * 128:s0 + (t_ + 1) * 128, fb * 512:fb * 512 + fw], zb[:, 0:fw], zb, reads=[zb])
            gemm_tok(S, C, xT, 16, LSEQ, W, 4096, 8192, slabs, epi_v)
        S.barrier()


def phase_ret(S, C, qkT_d, vg_d, dmatT_d, qdec_d, kdec_d, cdec, gnw, gnb, ynT_d, NT, LSEQ):
    with contextlib.ExitStack() as st:
        qk = [S.sbuf(st, [128, 32, 128], BF16, "rt_qk") for _ in range(2)]
        vg = [S.sbuf(st, [128, 8192], BF16, "rt_vg") for _ in range(2)]
        dmT = S.sbuf(st, [128, 8, 128], F32, "rt_dmT")
        qdec = S.sbuf(st, [128, 8, 128], F32, "rt_qdec")
        kdec = S.sbuf(st, [128, 8], F32, "rt_kdec")
        ST = [S.sbuf(st, [128, 128], BF16, "rt_ST") for _ in range(2)]
        qd = [S.sbuf(st, [128, 2, 128], BF16, "rt_qd") for _ in range(2)]
        kd = [S.sbuf(st, [128, 2, 128], BF16, "rt_kd") for _ in range(2)]
        R = S.sbuf(st, [128, 8, 2, 512], F32, "rt_R")
        Rb = S.sbuf(st, [128, 8, 2, 512], BF16, "rt_Rb")
        yhs = [S.sbuf(st, [128, 4096], F32, "rt_yh") for _ in range(2)]
        stats = S.sbuf(st, [128, 6], F32, "rt_stats")
        mv = S.sbuf(st, [128, 2], F32, "rt_mv")
        rstd = S.sbuf(st, [128, 1], F32, "rt_rstd")
        gw = S.sbuf(st, [128, 4096], BF16, "rt_gw")
        gb = S.sbuf(st, [128, 4096], BF16, "rt_gb")
        sg = S.sbuf(st, [128, 4096], BF16, "rt_sg")
        ynb = S.sbuf(st, [128, 4096], BF16, "rt_ynb")
        ynT = S.sbuf(st, [128, 32, 128], BF16, "rt_ynT")
        S.dma("sp", dmT[:], dmatT_d, dmT, writes=[dmT])
        S.dma("sp", qdec[:], qdec_d, qdec, writes=[qdec])
        S.dma("sp", kdec[:], kdec_d, kdec, writes=[kdec])
        S.dma("pool", gw[:], bcast_rows(gnw, 4096), gw, writes=[gw])
        S.dma("pool", gb[:], bcast_rows(gnb, 4096), gb, writes=[gb])
        qv = qkT_d.rearrange("(c p) t -> p c t", p=128)
        ynv = ynT_d.rearrange("(c p) t -> p c t", p=128)
        nch = LSEQ // 128
        ntot = (NT // LSEQ) * nch

        def load(i):
            sq, c = divmod(i, nch)
            g0 = sq * LSEQ + c * 128
            S.dma("sp", qk[i % 2][:], qv[:, :, g0:g0 + 128], qk[i % 2], writes=[qk[i % 2]])
            S.dma("sp", vg[i % 2][:], vg_d[g0:g0 + 128, :], vg[i % 2], writes=[vg[i % 2]])

        def heads(i):
            sq, c = divmod(i, nch)
            Q = qk[i % 2]
            V = vg[i % 2]
            yh = yhs[i % 2]
            if c == 0:
                S.op("pool", lambda en: en.memset(R[:], 0.0), [], [R])
                S.op("pool", lambda en: en.memset(Rb[:], 0.0), [], [Rb])
            for h in range(8):
                k = h % 2
                bs = S.bank()
                for half in range(2):
                    S.op("pe", lambda en: en.matmul(bs[:, 0:128], lhsT=Q[:, 16 + 2 * h + half, :], rhs=Q[:, 2 * h + half, :],
                                                    start=(half == 0), stop=(half == 1)), reads=[Q], writes=[bs], sig=(half == 1))
                tt(S, "dve", ST[k][:], bs[:, 0:128], dmT[:, h, :], ALU.mult, [bs, dmT], [ST[k]])
                tt(S, "pool", qd[k][:], Q[:, 2 * h:2 * h + 2, :], qdec[:, h, :].unsqueeze(1).to_broadcast([128, 2, 128]), ALU.mult,
                   [Q, qdec], [qd[k]])
                by = S.bank()
                S.op("pe", lambda en: en.matmul(by[:], lhsT=ST[k][:], rhs=V[:, h * 512:(h + 1) * 512], start=True, stop=False),
                     reads=[ST[k], V], writes=[by], sig=False)
                for half in range(2):
                    S.op("pe", lambda en: en.matmul(by[:], lhsT=qd[k][:, half, :], rhs=Rb[:, h, half, :], start=False, stop=(half == 1)),
                         reads=[qd[k], Rb], writes=[by], sig=(half == 1))
                S.op("dve", lambda en: en.bn_stats(out=stats[:], in_=by[:]), [by], [stats])
                S.op("dve", lambda en: en.bn_aggr(out=mv[:], in_=stats[:]), [stats], [mv])
                rsqrt_eps(S, rstd[:], mv[:, 1:2], [mv], rstd, 1.0, LN_EPS)
                ts(S, "dve", yh[:, h * 512:(h + 1) * 512], by[:], mv[:, 0:1], rstd[:, 0:1], ALU.subtract, ALU.mult, [by, mv, rstd], [yh])
                bt = S.bank()
                btv = bt.t[:].bitcast(BF16)
                for half in range(2):
                    S.op("pe", lambda en: en.transpose(btv[:, half * 128:(half + 1) * 128], Q[:, 16 + 2 * h + half, :], C.identb[:]),
                         reads=[Q, C.identb], writes=[bt], sig=(half == 1))
                ts(S, "dve", kd[k][:], btv[:, 0:256].rearrange("p (a b) -> p a b", a=2), kdec[:, h:h + 1], None, ALU.mult, None,
                   [bt, kdec], [kd[k]])
                for half in range(2):
                    br = S.bank()
                    S.op("pe", lambda en: en.matmul(br[:], lhsT=kd[k][:, half, :], rhs=V[:, h * 512:(h + 1) * 512], start=True, stop=True),
                         reads=[kd[k], V], writes=[br])
                    S.op("dve", lambda en: en.scalar_tensor_tensor(out=R[:, h, half, :], in0=R[:, h, half, :], scalar=float(cdec[h]),
                                                                   in1=br[:], op0=ALU.mult, op1=ALU.add), [R, br], [R])
                    cp(S, "act", Rb[:, h, half, :], R[:, h, half, :], [R], [Rb])

        def tail(i):
            V = vg[i % 2]
            yh = yhs[i % 2]
            tt(S, "pool", yh[:], yh[:], gw[:], ALU.mult, [yh, gw], [yh])
            tt(S, "dve", yh[:], yh[:], gb[:], ALU.add, [yh, gb], [yh])
            act(S, sg[:], V[:, 4096:8192], AF.Silu, [V], [sg])
            tt(S, "dve", ynb[:], yh[:], sg[:], ALU.mult, [yh, sg], [ynb])

        def out(i):
            sq, c = divmod(i, nch)
            g0 = sq * LSEQ + c * 128
            transposes_to(S, C, ynb, [ynb], 32, lambda j, n: ynT[:, j:j + n, :], [ynT], BF16, evac="act")
            S.dma("sp", ynv[:, :, g0:g0 + 128], ynT[:], ynT, reads=[ynT])

        load(0)
        for i in range(ntot):
            if i + 1 < ntot:
                load(i + 1)
            heads(i)
            if i > 0:
                out(i - 1)
            tail(i)
        out(ntot - 1)
        S.barrier()


def _ret_gamma():
    h = np.arange(8, dtype=np.float64)
    return np.log1p(-np.exp2(-5.0 - h))


RET_CDEC = [float(np.exp(128.0 * lg)) for lg in _ret_gamma()]


def ret_consts():
    lg = _ret_gamma()
    idx = np.arange(128, dtype=np.float64)
    rel = idx[None, :] - idx[:, None]
    dm = np.where(rel[:, None, :] >= 0, np.exp(rel[:, None, :] * lg[None, :, None]), 0.0)
    qdec = np.exp((idx[None, :] + 1.0) * lg[:, None])
    kdec = np.exp((127.0 - idx)[:, None] * lg[None, :])
    invf = 10000.0 ** (-np.arange(0, 256, 2, dtype=np.float32) / np.float32(256))
    return {"dmatT": dm.astype(np.float32), "qdec": np.tile(qdec[None], (128, 1, 1)).astype(np.float32),
            "kdec": kdec.astype(np.float32), "invfreq": invf.astype(np.float32).reshape(128, 1)}


WEIGHT_SPECS = [
    ("ssm_in_proj", [D, 10304]), ("ssm_convwT", [6144, 4]), ("ssm_convb2", [128, 48]), ("ssm_dt_bias", [64]),
    ("ssm_A_log", [64]), ("ssm_D", [64]), ("ssm_norm_w", [4096]), ("ssm_out_proj", [4096, D]),
    ("ret_in_proj", [D, 12288]), ("ret_gn_w", [4096]), ("ret_gn_b", [4096]), ("ret_out_proj", [4096, D]),
    ("mix_ln_w", [2, D]), ("mix_ln_b", [2, D]), ("peer_w_q", [2, D, D]), ("peer_keysT", [2, 16, 128, 128]),
    ("peer_uT", [2, D, 16384]), ("peer_v", [2, 16384, D]), ("ffn_ln_w", [2, D]), ("ffn_ln_b", [2, D]),
    ("identf", [128, 128]), ("iota", [128, 128]), ("negmask", [128, 512]),
    ("dmatT", [128, 8, 128]), ("qdec", [128, 8, 128]), ("kdec", [128, 8]), ("invfreq", [128, 1]),
]


def build_program(NT, LSEQ):
    nc = bass.Bass("TRN2", target_bir_lowering=False)
    NSEQ = NT // LSEQ
    x = nc.dram_tensor("x", [NT, D], F32, kind="ExternalInput").ap()
    pos = nc.dram_tensor("positions", [NSEQ, LSEQ], I32, kind="ExternalInput").ap()
    w = {}
    for name, shp in WEIGHT_SPECS:
        w[name] = nc.dram_tensor(name, list(shp), F32, kind="ExternalInput").ap()
    out = nc.dram_tensor("out", [NT, D], F32, kind="ExternalOutput").ap()
    S = Sched(nc)
    C = Ctx()
    with contextlib.ExitStack() as st:
        S.init_banks(st)
        setup_consts(S, C, st, w["identf"], w["iota"])
        xT0 = S.hbm("xT0", [D, NT], BF16)
        xbcT = S.hbm("xbcT", [6144, NT], BF16)
        z_d = S.hbm("z", [NT, 4096], BF16)
        dtT = S.hbm("dtT", [64, NT], F32)
        dAT = S.hbm("dAT", [64, NT], F32)
        ynT = S.hbm("ynT", [4096, NT], BF16)
        x1 = S.hbm("x1", [NT, D], F32)
        x1T = S.hbm("x1T", [D, NT], BF16)
        Gh = S.hbm("Gh", [NT // 128, 128, 128, 128], BF16)
        x2 = S.hbm("x2", [NT, D], F32)
        x2T = S.hbm("x2T", [D, NT], BF16)
        qkT = S.hbm("qkT", [4096, NT], BF16)
        vg = S.hbm("vg", [NT, 8192], BF16)
        x3 = S.hbm("x3", [NT, D], F32)
        x3T = S.hbm("x3T", [D, NT], BF16)
        phase_make_xT(S, C, x, xT0, NT)
        phase_ssm_inproj(S, C, xT0, w["ssm_in_proj"], w["ssm_convwT"], w["ssm_convb2"], w["ssm_dt_bias"], w["ssm_A_log"],
                         xbcT, z_d, dtT, dAT, NT, LSEQ)
        phase_ssd(S, C, xbcT, z_d, dtT, dAT, w["ssm_D"], w["ssm_norm_w"], w["negmask"], ynT, NT, LSEQ)
        phase_proj_ln(S, C, ynT, 4096, w["ssm_out_proj"], x, x1, x1T, w["mix_ln_w"][0], w["mix_ln_b"][0], NT)
        phase_peer_route(S, C, x1T, w["peer_w_q"][0], w["peer_keysT"][0], Gh, NT)
        phase_peer_dense(S, C, x1T, w["peer_uT"][0], w["peer_v"][0], Gh, x1, x2, x2T, w["ffn_ln_w"][0], w["ffn_ln_b"][0], NT)
        phase_ret_inproj(S, C, x2T, w["ret_in_proj"], pos, w["invfreq"], qkT, vg, NT, LSEQ)
        phase_ret(S, C, qkT, vg, w["dmatT"], w["qdec"], w["kdec"], RET_CDEC, w["ret_gn_w"], w["ret_gn_b"], ynT, NT, LSEQ)
        phase_proj_ln(S, C, ynT, 4096, w["ret_out_proj"], x2, x3, x3T, w["mix_ln_w"][1], w["mix_ln_b"][1], NT)
        phase_peer_route(S, C, x3T, w["peer_w_q"][1], w["peer_keysT"][1], Gh, NT)
        phase_peer_dense(S, C, x3T, w["peer_uT"][1], w["peer_v"][1], Gh, x3, out, None, w["ffn_ln_w"][1], w["ffn_ln_b"][1], NT)
        S.finish()
    return nc


def host_weights(inp):
    f = lambda a: np.ascontiguousarray(np.asarray(a), dtype=np.float32)
    m = {
        "ssm_in_proj": f(inp["ssm_in_proj"][0]), "ssm_convwT": f(np.asarray(inp["ssm_conv_w"][0]).T),
        "ssm_convb2": f(np.asarray(inp["ssm_conv_b"][0]).reshape(48, 128).T), "ssm_dt_bias": f(inp["ssm_dt_bias"][0]),
        "ssm_A_log": f(inp["ssm_A_log"][0]), "ssm_D": f(inp["ssm_D"][0]), "ssm_norm_w": f(inp["ssm_norm_w"][0]),
        "ssm_out_proj": f(inp["ssm_out_proj"][0]), "ret_in_proj": f(inp["ret_in_proj"][0]), "ret_gn_w": f(inp["ret_gn_w"][0]),
        "ret_gn_b": f(inp["ret_gn_b"][0]), "ret_out_proj": f(inp["ret_out_proj"][0]), "mix_ln_w": f(inp["mix_ln_w"]),
        "mix_ln_b": f(inp["mix_ln_b"]), "peer_w_q": f(inp["peer_w_q"]),
        "peer_keysT": f(np.asarray(inp["peer_sub_keys"]).reshape(2, 16, 128, 128).transpose(0, 1, 3, 2)),
        "peer_uT": f(np.asarray(inp["peer_u"]).transpose(0, 2, 1)), "peer_v": f(inp["peer_v"]),
        "ffn_ln_w": f(inp["ffn_ln_w"]), "ffn_ln_b": f(inp["ffn_ln_b"]),
        "identf": np.eye(128, dtype=np.float32), "iota": np.tile(np.arange(128, dtype=np.float32), (128, 1)),
        "negmask": np.tile(np.where(np.arange(128)[:, None] > np.arange(128)[None, :], NEG, 0.0).astype(np.float32), (1, 4)),
    }
    m.update(ret_consts())
    return m


def kernel(**inputs):
    x = np.asarray(inputs["x"], dtype=np.float32)
    positions = np.asarray(inputs["positions"], dtype=np.int32)
    B, L, _ = x.shape
    ncores = 8 if B % 8 == 0 else 1
    per = B // ncores
    NT = per * L
    nc = build_program(NT, L)
    wts = host_weights(inputs)
    in_maps = []
    for c in range(ncores):
        m = dict(wts)
        m["x"] = np.ascontiguousarray(x[c * per:(c + 1) * per].reshape(NT, D))
        m["positions"] = np.ascontiguousarray(positions[c * per:(c + 1) * per])
        in_maps.append(m)
    res = run_bass_kernel_spmd(nc, in_maps, core_ids=list(range(ncores)))
    outs = [np.asarray(r["out"]).reshape(per, L, D) for r in res.results]
    return np.concatenate(outs, axis=0).astype(np.float32)
```
